# Optimizing a Trainium2 kernel written in Bass

```python
import numpy as np
import jax, jax.numpy as jnp
from jax import lax

D_MODEL = 1024
BATCH = 32
SEQ = 2048
DEPTH = 1

POOL_WINDOWS = (2, 4, 8, 16)
N_POOL_GROUPS = len(POOL_WINDOWS)
POOL_GROUP = D_MODEL // 8
POOL_WIDTH = POOL_GROUP * N_POOL_GROUPS
HEAD_DIM = 64
N_HEADS = D_MODEL // HEAD_DIM
N_KV_GROUPS = 2
HEADS_PER_GROUP = N_HEADS // N_KV_GROUPS
ATTN_WIDTH = N_HEADS * HEAD_DIM
N_NSA_BRANCHES = 3
KV_WIDTH = N_NSA_BRANCHES * 2 * N_KV_GROUPS * HEAD_DIM
CMP_BLOCK = 32
CMP_STRIDE = 16
CMP_HIDDEN = 128
SLC_BLOCK = 64
SLC_TOP_N = 8
WINDOW = 512
Q_BLOCK = 32
ROPE_THETA = 10000.0
SCALE = HEAD_DIM ** -0.5
FORCE_BONUS = 1000.0
NEG_INF = -1e30
N_MERGE = 2
OFF_Q = POOL_WIDTH
OFF_KV = OFF_Q + ATTN_WIDTH
OFF_NSA_G = OFF_KV + KV_WIDTH
OFF_MERGE_G = OFF_NSA_G + N_NSA_BRANCHES * N_HEADS
IN_WIDTH = OFF_MERGE_G + N_MERGE * D_MODEL
D_FF = 2752
CONV_WIDTH = 3
EPS = 1e-6

kernel_name = "hybrid_pool_nsa_convffn"


def rms_norm(x, w):
    xf = x.astype(jnp.float32)
    y = xf * lax.rsqrt(jnp.mean(xf * xf, axis=-1, keepdims=True) + EPS)
    return (y * w.astype(jnp.float32)).astype(x.dtype)


def rope(x, pos):
    half = HEAD_DIM // 2
    freqs = ROPE_THETA ** (-jnp.arange(half, dtype=jnp.float32) / half)
    ang = pos.astype(jnp.float32)[:, None] * freqs[None, :]
    shp = (1, pos.shape[0]) + (1,) * (x.ndim - 3) + (half,)
    cos = jnp.cos(ang).reshape(shp)
    sin = jnp.sin(ang).reshape(shp)
    xf = x.astype(jnp.float32)
    x1, x2 = xf[..., :half], xf[..., half:]
    return jnp.concatenate([x1 * cos - x2 * sin, x2 * cos + x1 * sin], axis=-1).astype(x.dtype)


def masked_softmax(s, mask):
    s = jnp.where(mask, s.astype(jnp.float32), NEG_INF)
    p = jax.nn.softmax(s, axis=-1)
    return jnp.where(mask, p, 0.0)


def pool_mixer(u, pool_w, pool_scale):
    B, S, _ = u.shape
    ug = u.reshape(B, S, N_POOL_GROUPS, POOL_GROUP).astype(jnp.float32)
    c = jnp.pad(jnp.cumsum(ug, axis=1), ((0, 0), (1, 0), (0, 0), (0, 0)))
    t = jnp.arange(S)
    outs = []
    for gi, w in enumerate(POOL_WINDOWS):
        lo = jnp.maximum(t + 1 - w, 0)
        cnt = jnp.minimum(t + 1, w).astype(jnp.float32)
        outs.append((c[:, 1:, gi] - c[:, lo, gi]) / cnt[None, :, None])
    pooled = jnp.stack(outs, axis=2) - ug
    y = jnp.einsum('bsgc,gcd->bsgd', pooled.astype(u.dtype), pool_w)
    return y.reshape(B, S, POOL_WIDTH) * pool_scale


def compress(k, pos_emb, w1, b1, w2):
    B, S = k.shape[0], k.shape[1]
    n_cmp = (S - CMP_BLOCK) // CMP_STRIDE + 1
    idx = np.arange(n_cmp)[:, None] * CMP_STRIDE + np.arange(CMP_BLOCK)[None, :]
    blk = k[:, idx] + pos_emb[None, None, :, None, :]
    flat = jnp.moveaxis(blk, 3, 2).reshape(B, n_cmp, N_KV_GROUPS, CMP_BLOCK * HEAD_DIM)
    hid = jax.nn.gelu(flat @ w1 + b1)
    return hid @ w2


def nsa_attention(q, k, v, nsa_g, cmp_pos, cmp_w1, cmp_b1, cmp_w2):
    B, S = q.shape[0], q.shape[1]
    k_c, k_s, k_w = k[:, :, 0], k[:, :, 1], k[:, :, 2]
    v_c, v_s, v_w = v[:, :, 0], v[:, :, 1], v[:, :, 2]
    k_cmp = compress(k_c, cmp_pos[0], cmp_w1[0], cmp_b1[0], cmp_w2[0])
    v_cmp = compress(v_c, cmp_pos[1], cmp_w1[1], cmp_b1[1], cmp_w2[1])
    n_cmp = k_cmp.shape[1]
    cmp_end = jnp.asarray(np.arange(n_cmp) * CMP_STRIDE + CMP_BLOCK - 1, dtype=jnp.int32)
    n_slc = S // SLC_BLOCK
    n_sel = min(SLC_TOP_N, n_slc)
    ci = np.arange(n_cmp)[:, None]
    sj = np.arange(n_slc)[None, :]
    overlap = jnp.asarray(((ci * CMP_STRIDE < (sj + 1) * SLC_BLOCK) &
                           (ci * CMP_STRIDE + CMP_BLOCK > sj * SLC_BLOCK)).astype(np.float32))
    ks_blk = k_s.reshape(B, n_slc, SLC_BLOCK, N_KV_GROUPS, HEAD_DIM).transpose(0, 3, 1, 2, 4)
    vs_blk = v_s.reshape(B, n_slc, SLC_BLOCK, N_KV_GROUPS, HEAD_DIM).transpose(0, 3, 1, 2, 4)
    pad = ((0, 0), (WINDOW, 0), (0, 0), (0, 0))
    kw_pad = jnp.pad(k_w, pad)
    vw_pad = jnp.pad(v_w, pad)
    b_ix = jnp.arange(B)[:, None, None, None]
    g_ix = jnp.arange(N_KV_GROUPS)[None, :, None, None]
    blk_ids = jnp.arange(n_slc)

    def chunk(args):
        qc, gc, start = args
        tq = start + jnp.arange(Q_BLOCK)
        qg = qc.reshape(B, Q_BLOCK, N_KV_GROUPS, HEADS_PER_GROUP, HEAD_DIM)
        s_c = jnp.einsum('bqghd,bngd->bghqn', qg, k_cmp)
        p_c = masked_softmax(s_c, cmp_end[None, :] <= tq[:, None])
        o_c = jnp.einsum('bghqn,bngd->bqghd', p_c.astype(v_cmp.dtype), v_cmp)
        imp = jnp.einsum('bghqn,nj->bgqj', p_c, overlap)
        cur = tq // SLC_BLOCK
        forced = ((blk_ids[None] == 0) | (blk_ids[None] == cur[:, None]) |
                  (blk_ids[None] == cur[:, None] - 1)).astype(jnp.float32)
        valid_b = blk_ids[None] * SLC_BLOCK <= tq[:, None]
        score = jnp.where(valid_b, imp + FORCE_BONUS * forced, NEG_INF)
        _, idx = lax.top_k(score, n_sel)
        kg = ks_blk[b_ix, g_ix, idx]
        vg = vs_blk[b_ix, g_ix, idx]
        m_tok = n_sel * SLC_BLOCK
        kpos = idx[..., None] * SLC_BLOCK + jnp.arange(SLC_BLOCK)
        mask_s = (kpos <= tq[None, None, :, None, None]).reshape(B, N_KV_GROUPS, 1, Q_BLOCK, m_tok)
        s_s = jnp.einsum('bqghd,bgqnld->bghqnl', qg, kg).reshape(B, N_KV_GROUPS, HEADS_PER_GROUP, Q_BLOCK, m_tok)
        p_s = masked_softmax(s_s, mask_s)
        o_s = jnp.einsum('bghqm,bgqmd->bqghd', p_s.astype(vg.dtype),
                         vg.reshape(B, N_KV_GROUPS, Q_BLOCK, m_tok, HEAD_DIM))
        kw = lax.dynamic_slice_in_dim(kw_pad, start, Q_BLOCK + WINDOW, axis=1)
        vw = lax.dynamic_slice_in_dim(vw_pad, start, Q_BLOCK + WINDOW, axis=1)
        kp = start - WINDOW + jnp.arange(Q_BLOCK + WINDOW)
        dist = tq[:, None] - kp[None, :]
        mask_w = (dist >= 0) & (dist < WINDOW) & (kp[None, :] >= 0)
        s_w = jnp.einsum('bqghd,bkgd->bghqk', qg, kw)
        p_w = masked_softmax(s_w, mask_w)
        o_w = jnp.einsum('bghqk,bkgd->bqghd', p_w.astype(vw.dtype), vw)
        gc = gc.reshape(B, Q_BLOCK, N_KV_GROUPS, HEADS_PER_GROUP, N_NSA_BRANCHES)
        o = gc[..., 0:1] * o_c + gc[..., 1:2] * o_s + gc[..., 2:3] * o_w
        return o.reshape(B, Q_BLOCK, ATTN_WIDTH)

    n_q = S // Q_BLOCK
    q_chunks = q.reshape(B, n_q, Q_BLOCK, N_HEADS, HEAD_DIM).transpose(1, 0, 2, 3, 4)
    g_chunks = nsa_g.reshape(B, n_q, Q_BLOCK, N_HEADS, N_NSA_BRANCHES).transpose(1, 0, 2, 3, 4)
    starts = jnp.arange(n_q, dtype=jnp.int32) * Q_BLOCK
    out = lax.map(chunk, (q_chunks, g_chunks, starts))
    return out.transpose(1, 0, 2, 3).reshape(B, S, ATTN_WIDTH)


def token_mixer(h, w_in, pool_w, pool_scale, q_norm_w, k_norm_w, cmp_pos, cmp_w1, cmp_b1,
                cmp_w2, w_pool_br, w_attn_br, w_o):
    B, S, _ = h.shape
    z = h @ w_in
    u_pool = z[..., :OFF_Q]
    q = z[..., OFF_Q:OFF_KV].reshape(B, S, N_HEADS, HEAD_DIM)
    kv = z[..., OFF_KV:OFF_NSA_G].reshape(B, S, N_NSA_BRANCHES, 2, N_KV_GROUPS, HEAD_DIM)
    nsa_g = jax.nn.sigmoid(z[..., OFF_NSA_G:OFF_MERGE_G].astype(jnp.float32)).astype(h.dtype)
    nsa_g = nsa_g.reshape(B, S, N_HEADS, N_NSA_BRANCHES)
    merge_g = jax.nn.sigmoid(z[..., OFF_MERGE_G:].astype(jnp.float32)).astype(h.dtype)
    merge_g = merge_g.reshape(B, S, N_MERGE, D_MODEL)
    pos = jnp.arange(S)
    q = rope(rms_norm(q, q_norm_w), pos) * SCALE
    k = rope(rms_norm(kv[:, :, :, 0], k_norm_w[:, None, :]), pos)
    v = kv[:, :, :, 1]
    y_pool = pool_mixer(u_pool, pool_w, pool_scale) @ w_pool_br
    y_attn = nsa_attention(q, k, v, nsa_g, cmp_pos, cmp_w1, cmp_b1, cmp_w2) @ w_attn_br
    merged = merge_g[:, :, 0] * y_pool + merge_g[:, :, 1] * y_attn
    return merged @ w_o


def conv_ffn(h, w_up, conv_w, conv_b, w_down):
    S = h.shape[1]
    u = h @ w_up
    up = jnp.pad(u, ((0, 0), (CONV_WIDTH - 1, 0), (0, 0)))
    c = conv_b + conv_w[0] * up[:, 0:S]
    for tap in range(1, CONV_WIDTH):
        c = c + conv_w[tap] * up[:, tap:tap + S]
    gate, val = jnp.split(c, 2, axis=-1)
    return (jax.nn.silu(gate) * val) @ w_down


def setup_inputs(seed: int = 0) -> dict:
    key = jax.random.key(seed)
    ks = jax.random.split(key, 20)
    f32 = jnp.float32
    nrm = lambda k, shape, s: jax.random.normal(k, shape, f32) * s
    L = DEPTH
    return {
        "x": jax.random.normal(ks[0], (BATCH, SEQ, D_MODEL), f32),
        "attn_norm_w": 1.0 + nrm(ks[1], (L, D_MODEL), 0.05),
        "w_in": nrm(ks[2], (L, D_MODEL, IN_WIDTH), D_MODEL ** -0.5),
        "pool_w": nrm(ks[3], (L, N_POOL_GROUPS, POOL_GROUP, POOL_GROUP), POOL_GROUP ** -0.5),
        "pool_scale": 1.0 + nrm(ks[4], (L, POOL_WIDTH), 0.1),
        "q_norm_w": 1.0 + nrm(ks[5], (L, HEAD_DIM), 0.05),
        "k_norm_w": 1.0 + nrm(ks[6], (L, N_NSA_BRANCHES, HEAD_DIM), 0.05),
        "cmp_pos": nrm(ks[7], (L, 2, CMP_BLOCK, HEAD_DIM), 0.5),
        "cmp_w1": nrm(ks[8], (L, 2, CMP_BLOCK * HEAD_DIM, CMP_HIDDEN), (CMP_BLOCK * HEAD_DIM) ** -0.5),
        "cmp_b1": nrm(ks[9], (L, 2, CMP_HIDDEN), 0.02),
        "cmp_w2": nrm(ks[10], (L, 2, CMP_HIDDEN, HEAD_DIM), CMP_HIDDEN ** -0.5),
        "w_pool_br": nrm(ks[11], (L, POOL_WIDTH, D_MODEL), POOL_WIDTH ** -0.5),
        "w_attn_br": nrm(ks[12], (L, ATTN_WIDTH, D_MODEL), ATTN_WIDTH ** -0.5),
        "w_o": nrm(ks[13], (L, D_MODEL, D_MODEL), D_MODEL ** -0.5),
        "ffn_norm_w": 1.0 + nrm(ks[14], (L, D_MODEL), 0.05),
        "w_up": nrm(ks[15], (L, D_MODEL, 2 * D_FF), D_MODEL ** -0.5),
        "conv_w": nrm(ks[16], (L, CONV_WIDTH, 2 * D_FF), CONV_WIDTH ** -0.5),
        "conv_b": nrm(ks[17], (L, 2 * D_FF), 0.02),
        "w_down": nrm(ks[18], (L, D_FF, D_MODEL), D_FF ** -0.5),
    }


def reference(x, attn_norm_w, w_in, pool_w, pool_scale, q_norm_w, k_norm_w, cmp_pos, cmp_w1,
              cmp_b1, cmp_w2, w_pool_br, w_attn_br, w_o, ffn_norm_w, w_up, conv_w, conv_b, w_down):
    for l in range(DEPTH):
        h = rms_norm(x, attn_norm_w[l])
        x = x + token_mixer(h, w_in[l], pool_w[l], pool_scale[l], q_norm_w[l], k_norm_w[l],
                            cmp_pos[l], cmp_w1[l], cmp_b1[l], cmp_w2[l], w_pool_br[l],
                            w_attn_br[l], w_o[l])
        h = rms_norm(x, ffn_norm_w[l])
        x = x + conv_ffn(h, w_up[l], conv_w[l], conv_b[l], w_down[l])
    return x
```

```python
import contextlib
import os
import numpy as np
import ml_dtypes
import concourse.bass as bass
import concourse.mybir as mybir
from concourse.bass_utils import run_bass_kernel_spmd

F32 = mybir.dt.float32
BF16 = mybir.dt.bfloat16
ALU = mybir.AluOpType
AF = mybir.ActivationFunctionType
AX = mybir.AxisListType

S = 2048
D = 1024
NCORES = 8
EPS = 1e-6
NEGB = -30000.0
D_FF = 2752
ENGS = ("pe", "act", "dve", "pool", "sp")


class Op:
    __slots__ = ("eng", "fn", "dma", "deps", "signal", "val", "w", "waits")


class Sched:
    def __init__(self):
        self.ops = []
        self.lastw = {}
        self.readers = {}
        self.dmacum = {}
        self.barriers = []

    def add(self, eng, fn, r=(), w=(), dma=None):
        op = Op()
        op.eng = eng
        op.fn = fn
        op.dma = dma
        op.signal = False
        op.val = None
        op.deps = []
        seen = set()
        for n in tuple(r) + tuple(w):
            if n not in self.lastw:
                for pf, bop in reversed(self.barriers):
                    if n.startswith(pf):
                        self.lastw[n] = bop
                        self.readers.setdefault(n, [])
                        break
        for n in r:
            d = self.lastw.get(n)
            if d is not None and id(d) not in seen:
                seen.add(id(d))
                op.deps.append((d, "raw"))
        for n in w:
            d = self.lastw.get(n)
            if d is not None and id(d) not in seen:
                seen.add(id(d))
                op.deps.append((d, "waw"))
            for d in self.readers.get(n, ()):
                if id(d) not in seen:
                    seen.add(id(d))
                    op.deps.append((d, "war"))
        for n in r:
            self.readers.setdefault(n, []).append(op)
        for n in w:
            self.lastw[n] = op
            self.readers[n] = []
        if dma is not None:
            self.dmacum[dma[0]] = self.dmacum.get(dma[0], 0) + 16 * dma[1]
            op.val = self.dmacum[dma[0]]
        self.ops.append(op)
        return op

    def barrier(self, prefixes):
        names = [n for n in set(self.lastw) | set(self.readers) if n.startswith(prefixes)]
        bop = self.add("pool", lambda e: e.nop(), r=(), w=names)
        self.barriers.append((prefixes, bop))

    def finalize(self):
        for op in self.ops:
            op.w = []
            for d, kind in op.deps:
                if d.dma is not None:
                    op.w.append(d)
                elif d.eng == op.eng:
                    if op.dma is not None:
                        op.w.append(d)
                        d.signal = True
                    elif op.eng == "pe":
                        continue
                    elif kind == "raw":
                        op.w.append(d)
                        d.signal = True
                else:
                    op.w.append(d)
                    d.signal = True
        cnt = {}
        for op in self.ops:
            if op.dma is None and op.signal:
                cnt[op.eng] = cnt.get(op.eng, 0) + 1
                op.val = cnt[op.eng]
        waited = {e: {} for e in ENGS}
        for op in self.ops:
            ws = {}
            for d in op.w:
                key = ("dma", d.dma[0]) if d.dma is not None else ("eng", d.eng)
                if waited[op.eng].get(key, 0) >= d.val:
                    continue
                ws[key] = max(ws.get(key, 0), d.val)
            for k, v in ws.items():
                waited[op.eng][k] = v
            op.waits = list(ws.items())

    def sem_keys(self):
        return [("eng", e) for e in ENGS] + [("dma", s) for s in self.dmacum]

    def emit(self, eng_name, e, sems):
        for op in self.ops:
            if op.eng != eng_name:
                continue
            for key, v in op.waits:
                e.wait_ge(sems[key], v)
            insts = op.fn(e)
            if not isinstance(insts, (list, tuple)):
                insts = [insts]
            if op.dma is not None:
                assert len(insts) == op.dma[1], (len(insts), op.dma)
                for i in insts:
                    i.then_inc(sems[("dma", op.dma[0])], 16)
            elif op.signal:
                insts[-1].then_inc(sems[("eng", op.eng)], 1)
        if eng_name == "sp":
            for s, v in self.dmacum.items():
                e.wait_ge(sems[("dma", s)], v)


def host_consts():
    bf = ml_dtypes.bfloat16
    c = {}
    c["ident"] = np.eye(128, dtype=np.float32).astype(bf)
    half = 32
    freqs = (10000.0 ** (-np.arange(half, dtype=np.float32) / half)).astype(np.float32)
    pos = np.arange(S, dtype=np.float32)
    ang = (pos[:, None] * freqs[None, :]).astype(np.float32)
    cs = np.stack([np.cos(ang), np.sin(ang)], 0).astype(np.float32)
    c["cossin"] = np.ascontiguousarray(cs.reshape(2, 16, 128, 32).transpose(2, 0, 1, 3))
    n = np.arange(128)[:, None]
    q = np.arange(S)[None, :]
    c["cmpmask"] = np.where((n < 127) & (16 * n + 31 <= q), 0.0, NEGB).astype(bf)
    kk = np.arange(128)[:, None]
    qq = np.arange(128)[None, :]
    tri = np.concatenate([np.where(qq >= kk, 0.0, NEGB), np.where(qq < kk, 0.0, NEGB)], 1)
    c["tri"] = tri.astype(bf)
    j = np.arange(32)[:, None]
    k = np.arange(S)[None, :]
    c["esel"] = (k // 64 == j).astype(np.float32).astype(bf)
    tq = np.arange(S)[:, None]
    jb = np.arange(32)[None, :]
    cur = tq // 64
    forced = (jb == 0) | (jb == cur) | (jb == cur - 1)
    valid = jb * 64 <= tq
    fb = np.where(valid, 1000.0 * forced, -1e30).astype(np.float32)
    c["fb"] = np.ascontiguousarray(fb.reshape(16, 128, 32).transpose(1, 0, 2))
    ci = np.arange(128)[:, None]
    sj = np.arange(32)[None, :]
    ov = ((ci * 16 < (sj + 1) * 64) & (ci * 16 + 32 > sj * 64) & (ci < 127)).astype(np.float32)
    c["ov"] = ov.astype(bf)
    bs = np.zeros((48, 48, 64), np.float32)
    for i in range(48):
        bs[i, i, :] = 1.0
    c["bsel"] = bs.reshape(48, 48 * 64).astype(bf)
    pc = np.ones((128, 4, 16), np.float32)
    for g, w in enumerate((2, 4, 8, 16)):
        t = np.arange(16)
        pc[:, g, :] = (w / np.minimum(t + 1, w))[None, :]
    c["poolcorr"] = pc
    return c


def host_weights(inp):
    f = np.float32
    w = {}
    A = lambda a: np.ascontiguousarray(np.asarray(a, dtype=f))
    w["w_in"] = A(inp["w_in"][0].reshape(8, 128, 4400).transpose(1, 0, 2))
    w["w_pool_br"] = A(inp["w_pool_br"][0].reshape(4, 128, 1024).transpose(1, 0, 2))
    w["w_attn_br"] = A(inp["w_attn_br"][0].reshape(8, 128, 1024).transpose(1, 0, 2))
    w["w_o"] = A(inp["w_o"][0].reshape(8, 128, 1024).transpose(1, 0, 2))
    w["w_up"] = A(inp["w_up"][0].reshape(8, 128, 5504).transpose(1, 0, 2))
    wd = np.zeros((22 * 128, 1024), f)
    wd[:D_FF] = inp["w_down"][0]
    w["w_down"] = A(wd.reshape(22, 128, 1024).transpose(1, 0, 2))
    w["pool_w"] = A(inp["pool_w"][0].transpose(1, 0, 2))
    w1 = np.asarray(inp["cmp_w1"][0]).reshape(2, 32, 64, 128).transpose(2, 0, 1, 3)
    w["cmp_w1"] = A(np.concatenate([w1, w1], 0))
    w["cmp_w2"] = A(np.asarray(inp["cmp_w2"][0]).transpose(1, 0, 2))
    pt = np.asarray(inp["cmp_pos"][0]).transpose(2, 0, 1)
    w["cmp_posT"] = A(np.concatenate([pt, pt], 0))
    w["cmp_b1"] = A(np.asarray(inp["cmp_b1"][0]).T)
    nrm = np.stack([np.asarray(inp["attn_norm_w"][0]).reshape(8, 128).T,
                    np.asarray(inp["ffn_norm_w"][0]).reshape(8, 128).T], 1)
    w["nrm"] = A(nrm)
    w["pool_scale"] = A(np.asarray(inp["pool_scale"][0]).reshape(4, 128).T)
    qk = np.concatenate([np.asarray(inp["q_norm_w"][0])[None], np.asarray(inp["k_norm_w"][0])], 0)
    w["qkw"] = A(np.broadcast_to(qk[None], (128, 4, 64)))
    cw = np.asarray(inp["conv_w"][0])
    cb = np.asarray(inp["conv_b"][0])
    cv = np.zeros((128, 2, 22, 4), f)
    for gv in range(2):
        for fc in range(22):
            rows = 128 if fc < 21 else 64
            sl = slice(gv * D_FF + fc * 128, gv * D_FF + fc * 128 + rows)
            cv[:rows, gv, fc, 0:3] = cw[:, sl].T
            cv[:rows, gv, fc, 3] = cb[sl]
    w["conv"] = cv
    return w


DRAM_SPECS = {
    "w_in": ([128, 8, 4400], F32), "w_pool_br": ([128, 4, 1024], F32), "w_attn_br": ([128, 8, 1024], F32),
    "w_o": ([128, 8, 1024], F32), "w_up": ([128, 8, 5504], F32), "w_down": ([128, 22, 1024], F32),
    "pool_w": ([128, 4, 128], F32), "cmp_w1": ([128, 2, 32, 128], F32), "cmp_w2": ([128, 2, 64], F32),
    "cmp_posT": ([128, 2, 32], F32), "cmp_b1": ([128, 2], F32), "nrm": ([128, 2, 8], F32),
    "pool_scale": ([128, 4], F32), "qkw": ([128, 4, 64], F32), "conv": ([128, 2, 22, 4], F32),
    "ident": ([128, 128], BF16), "cossin": ([128, 2, 16, 32], F32), "cmpmask": ([128, 2048], BF16),
    "tri": ([128, 256], BF16), "esel": ([32, 2048], BF16), "fb": ([128, 16, 32], F32),
    "ov": ([128, 32], BF16), "bsel": ([48, 3072], BF16), "poolcorr": ([128, 4, 16], F32),
}


def build(nseq=4, stop_after=99, dbg=False):
    nc = bass.Bass("TRN2", target_bir_lowering=False)
    sch = Sched()
    dr = {}
    for name, (shape, dt) in DRAM_SPECS.items():
        dr[name] = nc.dram_tensor(name, shape, dt, kind="ExternalInput").ap()
    x_d = nc.dram_tensor("x", [nseq, S, D], F32, kind="ExternalInput").ap()
    out_d = nc.dram_tensor("out", [nseq, S, D], F32, kind="ExternalOutput").ap()
    x1_d = nc.dram_tensor("x1scratch", [nseq + 1, S, D], F32, kind="Internal").ap()[1:nseq + 1]
    dbg_d = {}

    es = contextlib.ExitStack()
    ARENA_B = 206 * 1024
    arena = es.enter_context(nc.sbuf_tensor("arena", [128, ARENA_B // 2], BF16))
    cur = [0]

    def alloc(shape, dt=BF16):
        n = 1
        for s_ in shape[1:]:
            n *= s_
        nb = n * (4 if dt == F32 else 2)
        nb = (nb + 63) // 64 * 64
        off = cur[0]
        cur[0] += nb
        assert cur[0] <= ARENA_B, ("arena overflow", cur[0])
        ap = arena[:, off // 2:(off + nb) // 2]
        if dt == F32:
            ap = ap.bitcast(F32)
        ap = ap[:, 0:n]
        if len(shape) == 3:
            ap = ap.rearrange("p (a b) -> p a b", a=shape[1], b=shape[2])
        elif len(shape) == 4:
            ap = ap.rearrange("p (a b c) -> p a b c", a=shape[1], b=shape[2], c=shape[3])
        elif len(shape) == 5:
            ap = ap.rearrange("p (a b c d) -> p a b c d", a=shape[1], b=shape[2], c=shape[3], d=shape[4])
        return ap

    PB = [es.enter_context(nc.psum_tensor("pb%d" % i, [128, 512], F32)) for i in range(7)]
    BT = es.enter_context(nc.psum_tensor("pbt", [128, 1024], BF16))
    pbrot = [0]

    def nextbank(pool):
        i = pool[pbrot[0] % len(pool)]
        pbrot[0] += 1
        return i

    ident = alloc([128, 128]); cossin = alloc([128, 2, 16, 32], F32); cmpmask = alloc([128, 2048])
    tri = alloc([128, 256]); fb = alloc([128, 16, 32], F32); ov = alloc([128, 32])
    bsel = alloc([128, 3072]); poolcorr = alloc([128, 4, 16], F32); nrm = alloc([128, 2, 8], F32)
    pool_scale = alloc([128, 4], F32); qkw = alloc([128, 4, 64], F32); conv = alloc([128, 2, 22, 4], F32)
    b1 = alloc([128, 2], F32); chid = alloc([128, 2], F32); w2 = alloc([128, 2, 64]); pool_w = alloc([128, 4, 128])
    posT = alloc([128, 2, 32]); ones = alloc([128, 128]); mhalf = alloc([128, 16], F32)
    KCMP = alloc([128, 2, 128]); VCMP = alloc([128, 2, 128])
    ssq = alloc([128, 16], F32); rs = alloc([128, 16], F32); sr = alloc([128, 16], F32); rstd = alloc([128, 16], F32)
    mark0 = cur[0]
    hT = alloc([128, 8, S]); OT = alloc([128, 8, S])
    markX = cur[0]
    KAs = alloc([128, 2, S]); KAw = alloc([128, 2, S])
    VA = alloc([128, 2, 2, 16, 128]); G = alloc([128, S])
    Wsm = alloc([128, 8, 1024])
    w1 = Wsm.rearrange("p a b -> p (a b)").rearrange("p (k l h) -> p k l h", k=2, l=32, h=128)
    ZQ = alloc([128, 1024], F32); SQ = alloc([128, 1024], F32); QW = alloc([128, 1024], F32)
    TA = alloc([128, 512], F32); TB = alloc([128, 512], F32); TC = alloc([128, 512], F32); TD = alloc([128, 512], F32)
    QN = alloc([128, 1024]); KZb = alloc([128, 384], F32)
    markU = cur[0]
    xs = [alloc([128, 1024], F32) for _ in range(2)]
    hn = [alloc([128, 1024]) for _ in range(2)]
    endU1 = cur[0]
    cur[0] = markU
    KC = alloc([128, S]); VC = alloc([128, S])
    HID = alloc([128, 128]); XG = alloc([128, 128], F32); X2 = alloc([128, 128], F32); X3 = alloc([128, 128], F32)
    endU2 = cur[0]
    cur[0] = markU
    QA = alloc([128, 16, 512])
    PT = [alloc([128, 512]) for _ in range(3)]
    RR = alloc([128, 512], F32); R2 = alloc([128, 512], F32)
    U1 = alloc([128, 512], F32); U2 = alloc([128, 512], F32); UT = alloc([128, 512], F32)
    ECN = [alloc([128, 512]) for _ in range(2)]
    PSC = alloc([128, 512], F32); PSCb = alloc([128, 512])
    SC = alloc([128, 4, 32], F32); M8 = alloc([128, 4, 8], F32); SBf = alloc([128, 4, 32], F32); SBb = alloc([128, 4, 32])
    GSB = alloc([128, 512], F32); RCPb = alloc([128, 512], F32)
    endX1 = max(cur[0], endU1, endU2)
    cur[0] = markX
    YP = alloc([128, 4, S])
    markY = cur[0]
    UF = alloc([128, 16 + S], F32); SA = alloc([128, 16 + S], F32); SBp = alloc([128, 16 + S], F32); PLb = alloc([128, S])
    Wp = alloc([128, 8, 512])
    endX2 = cur[0]
    cur[0] = markY
    Wmg = alloc([128, 8, 2048]); Wpb = alloc([128, 4, 1024]); Wab = alloc([128, 8, 1024]); Wo = alloc([128, 8, 1024])
    MT = alloc([128, 8, 512]); SG = [alloc([128, 512], F32) for _ in range(2)]
    T0 = alloc([128, 512], F32); T1 = alloc([128, 512], F32)
    xs6 = [alloc([128, 1024], F32) for _ in range(2)]
    endX3 = cur[0]
    cur[0] = mark0
    Wup = alloc([128, 8, 5504]); Wdn = alloc([128, 22, 1024])
    H2 = [alloc([128, 8, 256]) for _ in range(2)]; X1S = [alloc([128, 1024], F32) for _ in range(4)]
    hn7 = [alloc([128, 1024]) for _ in range(2)]
    UE = [alloc([128, 2, 258], F32) for _ in range(3)]
    CG = [alloc([128, 256], F32) for _ in range(3)]; CV = [alloc([128, 256], F32) for _ in range(3)]
    SGF = [alloc([128, 256], F32) for _ in range(3)]; AF_ = [alloc([128, 256]) for _ in range(3)]
    HALO = alloc([128, 2, 22, 2], F32)
    endF = cur[0]
    print("arena bytes: X1 %d X2 %d X3 %d FFN %d" % (endX1, endX2, endX3, endF))

    def dump(name, ap, shape, dt=F32, reads=()):
        if not dbg:
            return
        d = nc.dram_tensor("dbg_" + name, shape, dt, kind="ExternalOutput").ap()
        dbg_d[name] = d
        sch.add("sp", lambda e, d=d, ap=ap: e.dma_start(out=d, in_=ap), r=reads, w=(), dma=("dbg_" + name, 1))

    def ld(dst, src, slot, wn, eng="sp"):
        sch.add(eng, lambda e: e.dma_start(out=dst, in_=src), r=(), w=(wn,), dma=("L" + wn, 1))

    ld(ident, dr["ident"], "c0", "c:ident"); ld(cossin, dr["cossin"], "c0", "c:cossin")
    ld(cmpmask, dr["cmpmask"], "c0", "c:cmpmask"); ld(tri, dr["tri"], "c0", "c:tri"); ld(fb, dr["fb"], "c0", "c:fb")
    ld(ov, dr["ov"], "c0", "c:ov"); ld(bsel[0:48], dr["bsel"], "c0", "c:bsel"); ld(poolcorr, dr["poolcorr"], "c0", "c:poolcorr")
    ld(nrm, dr["nrm"], "c0", "c:nrm"); ld(pool_scale, dr["pool_scale"], "c0", "c:pool_scale"); ld(qkw, dr["qkw"], "c0", "c:qkw")
    ld(conv, dr["conv"], "c0", "c:conv"); ld(b1, dr["cmp_b1"], "c0", "c:b1")
    ld(w2, dr["cmp_w2"], "c1", "c:w2", "pool"); ld(pool_w, dr["pool_w"], "c1", "c:pool_w", "pool")
    ld(posT, dr["cmp_posT"], "c1", "c:posT", "pool")
    sch.add("dve", lambda e: e.memset(ones, 1.0), w=("c:ones",))
    sch.add("dve", lambda e: e.memset(mhalf, -0.5), w=("c:mhalf",))
    sch.add("dve", lambda e: e.memset(KCMP, 0.0), w=("m:KCMP",))
    sch.add("dve", lambda e: e.memset(VCMP, 0.0), w=("m:VCMP",))
    sch.add("dve", lambda e: e.memset(VCMP[0:127, :, 64:128], 1.0), w=("m:VCMP",))

    scr = {}

    def pieces(a_, b_):
        out = []
        for k in range(a_.shape[1]):
            if len(a_.shape) == 3 and a_.shape[2] > 2816:
                w_ = a_.shape[2] // 4
                for i in range(4):
                    out.append((a_[:, k, i * w_:(i + 1) * w_], b_[:, k, i * w_:(i + 1) * w_]))
            else:
                out.append((a_[:, k], b_[:, k]))
        return out

    def wload(dst, srcap, wn, slot, nk, s=0, key=None, cast_fn=None):
        key = key or slot
        if key not in scr:
            scr[key] = nc.dram_tensor("wscr_" + key, list(dst.shape), BF16, kind="Internal").ap()
        sc = scr[key]
        if s == 0:
            if cast_fn is None:
                sch.add("pool", lambda e: [e.dma_start(out=dst[:, k, :], in_=srcap[:, k, :]) for k in range(nk)], w=(wn,), dma=(slot, nk))
            else:
                cast_fn()
            sch.add("sp", lambda e: [e.dma_start(out=a_, in_=b_) for a_, b_ in pieces(sc, dst)], r=(wn,), w=("scr:" + key,), dma=("st_" + key, len(pieces(sc, dst))))
        else:
            sch.add("sp", lambda e: [e.dma_start(out=a_, in_=b_) for a_, b_ in pieces(dst, sc)], r=("scr:" + key,), w=(wn,), dma=("ld_" + key, len(pieces(dst, sc))))

    def mm(e, out, lhsT, rhs, start, stop):
        return e.matmul(out, lhsT=lhsT, rhs=rhs, start=start, stop=stop)

    def norm_tile(src_tile, src_name, hn_t, hn_name, nrm_idx, dst, dst_name, col0, stat_col):
        sc = slice(stat_col, stat_col + 1)
        sch.add("act", lambda e: e.activation(out=hn_t, in_=src_tile, func=AF.Square, accum_out=ssq[:, sc]),
                r=(src_name,), w=(hn_name, "t:ssq"))
        sch.add("dve", lambda e: e.tensor_scalar(out=rs[:, sc], in0=ssq[:, sc], scalar1=1.0 / D, scalar2=EPS,
                                                 op0=ALU.mult, op1=ALU.add), r=("t:ssq",), w=("t:rs",))
        sch.add("pool", lambda e: e.tensor_tensor(out=rstd[:, sc], in0=rs[:, sc], in1=mhalf[:, sc], op=ALU.pow), r=("t:rs", "c:mhalf"), w=("t:rstd",))
        sch.add("act", lambda e: e.activation(out=hn_t, in_=src_tile, func=AF.Copy, scale=rstd[:, sc]),
                r=(src_name, "t:rstd"), w=(hn_name,))
        btv = BT[:, 0:1024].rearrange("p (a b) -> p a b", a=8, b=128)

        def tr(e):
            return [e.transpose(btv[:, kc, :], hn_t[:, kc * 128:(kc + 1) * 128], ident) for kc in range(8)]
        sch.add("pe", tr, r=(hn_name, "c:ident"), w=("BT",))
        sch.add("dve", lambda e: e.tensor_tensor(out=dst[:, :, col0:col0 + 128], in0=btv,
                                                 in1=nrm[:, nrm_idx, :].unsqueeze(2).to_broadcast([128, 8, 128]),
                                                 op=ALU.mult), r=("BT", "c:nrm"), w=(dst_name,))

    def rope_A(Z, zname, nh, widx_ap, eps_eff):
        n = nh * 64
        sqv = SQ[:, 0:n].rearrange("p (h d) -> p h d", h=nh, d=64)
        qwv = QW[:, 0:n].rearrange("p (h d) -> p h d", h=nh, d=64)
        sch.add("dve", lambda e: e.tensor_tensor(out=sqv, in0=Z, in1=Z, op=ALU.mult), r=(zname,), w=("t:SQ",))
        sch.add("dve", lambda e: e.tensor_reduce(out=ssq[:, 0:nh], in_=sqv, axis=AX.X, op=ALU.add), r=("t:SQ",), w=("t:ssq",))
        sch.add("dve", lambda e: e.tensor_scalar(out=rs[:, 0:nh], in0=ssq[:, 0:nh], scalar1=eps_eff[0], scalar2=eps_eff[1],
                                                 op0=ALU.mult, op1=ALU.add), r=("t:ssq",), w=("t:rs",))
        sch.add("pool", lambda e: e.tensor_tensor(out=rstd[:, 0:nh], in0=rs[:, 0:nh], in1=mhalf[:, 0:nh], op=ALU.pow), r=("t:rs", "c:mhalf"), w=("t:rstd",))
        sch.add("dve", lambda e: e.tensor_tensor(out=qwv, in0=Z, in1=widx_ap, op=ALU.mult), r=(zname, "c:qkw"), w=("t:QW",))

    def rope_B(nh, tt, out_bf, out_name):
        n = nh * 64
        qwv = QW[:, 0:n].rearrange("p (h d) -> p h d", h=nh, d=64)
        h = nh * 32
        cosb = cossin[:, 0, tt, :].unsqueeze(1).to_broadcast([128, nh, 32])
        sinb = cossin[:, 1, tt, :].unsqueeze(1).to_broadcast([128, nh, 32])
        v3 = lambda T: T[:, 0:h].rearrange("p (h d) -> p h d", h=nh, d=32)
        q1 = qwv[:, :, 0:32]
        q2 = qwv[:, :, 32:64]
        sch.add("pool", lambda e: e.tensor_tensor(out=v3(TA), in0=q1, in1=cosb, op=ALU.mult), r=("t:QW", "c:cossin"), w=("t:TA",))
        sch.add("pool", lambda e: e.tensor_tensor(out=v3(TB), in0=q2, in1=sinb, op=ALU.mult), r=("t:QW", "c:cossin"), w=("t:TB",))
        sch.add("pool", lambda e: e.tensor_tensor(out=v3(TC), in0=q2, in1=cosb, op=ALU.mult), r=("t:QW", "c:cossin"), w=("t:TC",))
        sch.add("pool", lambda e: e.tensor_tensor(out=v3(TD), in0=q1, in1=sinb, op=ALU.mult), r=("t:QW", "c:cossin"), w=("t:TD",))
        sch.add("dve", lambda e: e.tensor_tensor(out=v3(TA), in0=v3(TA), in1=v3(TB), op=ALU.subtract), r=("t:TA", "t:TB"), w=("t:TA",))
        sch.add("dve", lambda e: e.tensor_tensor(out=v3(TC), in0=v3(TC), in1=v3(TD), op=ALU.add), r=("t:TC", "t:TD"), w=("t:TC",))
        rb = rstd[:, 0:nh].unsqueeze(2).to_broadcast([128, nh, 32])
        sch.add("dve", lambda e: e.tensor_tensor(out=out_bf[:, :, 0:32], in0=v3(TA), in1=rb, op=ALU.mult), r=("t:TA", "t:rstd"), w=(out_name,))
        sch.add("dve", lambda e: e.tensor_tensor(out=out_bf[:, :, 32:64], in0=v3(TC), in1=rb, op=ALU.mult), r=("t:TC", "t:rstd"), w=(out_name,))


    UPFX = ("u1:", "u2:", "t:", "x1:QA")

    def do_seq(s):
        for tt in range(16):
            xt = xs[tt % 2]
            xn = "u1:xs%d" % (tt % 2)
            sch.add("sp", lambda e, xt=xt, tt=tt: e.dma_start(out=xt, in_=x_d[s, tt * 128:(tt + 1) * 128, :]),
                    w=(xn,), dma=("xs%d" % (tt % 2), 1))
            norm_tile(xt, xn, hn[tt % 2], "u1:hn%d" % (tt % 2), 0, hT, "m:hT%d" % (tt // 4), tt * 128, 0)
        if s == 0:
            dump("hT", hT, [128, 8, S], BF16, reads=["m:hT%d" % i for i in range(4)])
        if stop_after <= 1:
            return
        sch.barrier(UPFX)
        sch.add("sp", lambda e: [e.dma_start(out=KAs[64:96, g, :], in_=dr["esel"]) for g in range(2)], w=("x1:KAs_e",), dma=("esel", 2))
        sch.add("pool", lambda e: e.memset(VA[:, :, :, :, 64:128], 1.0), w=("x1:VA1",))
        WA = Wsm[:, :, 0:816]
        wload(WA, dr["w_in"][:, :, 1536:2352], "x1:Wsm", "wsm", 8, s, "wa")
        ZK = ZQ[:, 0:768]
        zk5 = ZK.rearrange("p (b k g d) -> p b k g d", b=3, k=2, g=2, d=64)
        def p2_tile(tt):
            c = tt // 4
            KZ = KZb.rearrange("p (b g d) -> p b g d", b=3, g=2, d=64)

            def s_mm():
                def kvmm(e, tt=tt):
                    ins = []
                    for kc in range(8):
                        ins.append(mm(e, PB[0][:, 0:512], hT[:, kc, tt * 128:(tt + 1) * 128], WA[:, kc, 0:512], kc == 0, kc == 7))
                    for kc in range(8):
                        ins.append(mm(e, PB[1][:, 0:256], hT[:, kc, tt * 128:(tt + 1) * 128], WA[:, kc, 512:768], kc == 0, kc == 7))
                    return ins
                sch.add("pe", kvmm, r=("m:hT%d" % c, "x1:Wsm"), w=("B0", "B1"))

            def s_copy():
                sch.add("act", lambda e: e.activation(out=ZK[:, 0:512], in_=PB[0][:, 0:512], func=AF.Copy), r=("B0",), w=("t:ZK",))
                sch.add("act", lambda e: e.activation(out=ZK[:, 512:768], in_=PB[1][:, 0:256], func=AF.Copy), r=("B1",), w=("t:ZK",))

            def s_A():
                sch.add("pool", lambda e, tt=tt: e.tensor_copy(out=VA[:, :, :, tt, 0:64], in_=zk5[:, 1:3, 1, :, :]), r=("t:ZK",), w=("x1:VA",))
                sch.add("pool", lambda e: e.tensor_copy(out=KZ, in_=zk5[:, :, 0, :, :]), r=("t:ZK",), w=("t:KZ",))

            def s_B():
                KZ3 = KZb.rearrange("p (h d) -> p h d", h=6, d=64)
                kw_ap = qkw[:, 1:4, :].unsqueeze(2).to_broadcast([128, 3, 2, 64])
                KN = QN[:, 0:384].rearrange("p (h d) -> p h d", h=6, d=64)
                n = 384
                sqv = SQ[:, 0:n].rearrange("p (h d) -> p h d", h=6, d=64)
                qwv = QW[:, 0:n].rearrange("p (h d) -> p h d", h=6, d=64)
                qwv4 = QW[:, 0:n].rearrange("p (b g d) -> p b g d", b=3, g=2, d=64)
                sch.add("dve", lambda e: e.tensor_tensor(out=sqv, in0=KZ3, in1=KZ3, op=ALU.mult), r=("t:KZ",), w=("t:SQ",))
                sch.add("dve", lambda e: e.tensor_reduce(out=ssq[:, 0:6], in_=sqv, axis=AX.X, op=ALU.add), r=("t:SQ",), w=("t:ssq",))
                sch.add("dve", lambda e: e.tensor_scalar(out=rs[:, 0:6], in0=ssq[:, 0:6], scalar1=1.0 / 64, scalar2=EPS,
                                                         op0=ALU.mult, op1=ALU.add), r=("t:ssq",), w=("t:rs",))
                sch.add("pool", lambda e: e.tensor_tensor(out=rstd[:, 0:6], in0=rs[:, 0:6], in1=mhalf[:, 0:6], op=ALU.pow), r=("t:rs", "c:mhalf"), w=("t:rstd",))
                sch.add("dve", lambda e: e.tensor_tensor(out=qwv4, in0=KZ, in1=kw_ap, op=ALU.mult), r=("t:KZ", "c:qkw"), w=("t:QW",))
                nh = 6
                h_ = nh * 32
                cosb = cossin[:, 0, tt, :].unsqueeze(1).to_broadcast([128, nh, 32])
                sinb = cossin[:, 1, tt, :].unsqueeze(1).to_broadcast([128, nh, 32])
                v3 = lambda T: T[:, 0:h_].rearrange("p (h d) -> p h d", h=nh, d=32)
                q1 = qwv[:, :, 0:32]
                q2 = qwv[:, :, 32:64]
                sch.add("pool", lambda e, cosb=cosb, q1=q1: e.tensor_tensor(out=v3(TA), in0=q1, in1=cosb, op=ALU.mult), r=("t:QW", "c:cossin"), w=("t:TA",))
                sch.add("pool", lambda e, sinb=sinb, q2=q2: e.tensor_tensor(out=v3(TB), in0=q2, in1=sinb, op=ALU.mult), r=("t:QW", "c:cossin"), w=("t:TB",))
                sch.add("pool", lambda e, cosb=cosb, q2=q2: e.tensor_tensor(out=v3(TC), in0=q2, in1=cosb, op=ALU.mult), r=("t:QW", "c:cossin"), w=("t:TC",))
                sch.add("pool", lambda e, sinb=sinb, q1=q1: e.tensor_tensor(out=v3(TD), in0=q1, in1=sinb, op=ALU.mult), r=("t:QW", "c:cossin"), w=("t:TD",))
                sch.add("dve", lambda e: e.tensor_tensor(out=v3(TA), in0=v3(TA), in1=v3(TB), op=ALU.subtract), r=("t:TA", "t:TB"), w=("t:TA",))
                sch.add("dve", lambda e: e.tensor_tensor(out=v3(TC), in0=v3(TC), in1=v3(TD), op=ALU.add), r=("t:TC", "t:TD"), w=("t:TC",))
                rb = rstd[:, 0:nh].unsqueeze(2).to_broadcast([128, nh, 32])
                sch.add("dve", lambda e, rb=rb: e.tensor_tensor(out=KN[:, :, 0:32], in0=v3(TA), in1=rb, op=ALU.mult), r=("t:TA", "t:rstd"), w=("t:KN",))
                sch.add("dve", lambda e, rb=rb: e.tensor_tensor(out=KN[:, :, 32:64], in0=v3(TC), in1=rb, op=ALU.mult), r=("t:TC", "t:rstd"), w=("t:KN",))
                btv = BT[:, 0:384].rearrange("p (a b) -> p a b", a=3, b=128)
                sch.add("pe", lambda e: [e.transpose(btv[:, b, :], QN[:, b * 128:(b + 1) * 128], ident) for b in range(3)],
                        r=("t:KN", "c:ident"), w=("BT",))
                cs = slice(tt * 128, (tt + 1) * 128)
                sch.add("dve", lambda e, cs=cs: e.tensor_copy(out=KC[:, cs], in_=btv[:, 0, :]), r=("BT",), w=("u2:KC",))
                sch.add("dve", lambda e, cs=cs: e.tensor_copy(out=KAs[0:64, 0, cs], in_=btv[0:64, 1, :]), r=("BT",), w=("x1:KAs",))
                sch.add("dve", lambda e, cs=cs: e.tensor_copy(out=KAs[0:64, 1, cs], in_=btv[64:128, 1, :]), r=("BT",), w=("x1:KAs",))
                sch.add("dve", lambda e, cs=cs: e.tensor_copy(out=KAw[0:64, 0, cs], in_=btv[0:64, 2, :]), r=("BT",), w=("x1:KAw",))
                sch.add("dve", lambda e, cs=cs: e.tensor_copy(out=KAw[0:64, 1, cs], in_=btv[64:128, 2, :]), r=("BT",), w=("x1:KAw",))
            return s_mm, s_copy, s_A, s_B
        pt_ = [p2_tile(tt) for tt in range(16)]
        pt_[0][0](); pt_[0][1]()
        for tt in range(16):
            if tt + 1 < 16:
                pt_[tt + 1][0]()
            pt_[tt][2]()
            if tt + 1 < 16:
                pt_[tt + 1][1]()
            pt_[tt][3]()
        for c in range(4):
            cc = slice(c * 512, (c + 1) * 512)
            sch.add("pe", lambda e, cc=cc: [mm(e, PB[2][:, :], WA[:, kc, 128:256], hT[:, kc, cc], kc == 0, kc == 7) for kc in range(8)],
                    r=("m:hT%d" % c, "x1:Wsm"), w=("B2",))
            sch.add("act", lambda e, cc=cc: e.activation(out=VC[:, cc], in_=PB[2][:, :], func=AF.Copy), r=("B2",), w=("u2:VC",))
            sch.add("pe", lambda e, cc=cc: [mm(e, PB[3][0:48, :], WA[:, kc, 768:816], hT[:, kc, cc], kc == 0, kc == 7) for kc in range(8)],
                    r=("m:hT%d" % c, "x1:Wsm"), w=("B3",))
            sch.add("act", lambda e, cc=cc: e.activation(out=G[0:48, cc], in_=PB[3][0:48, :], func=AF.Sigmoid), r=("B3",), w=("x1:G",))
        if s == 0:
            dump("KAs", KAs, [128, 2, S], BF16, reads=("x1:KAs", "x1:KAs_e"))
            dump("KAw", KAw, [128, 2, S], BF16, reads=("x1:KAw",))
            dump("KC", KC, [128, S], BF16, reads=("u2:KC",))
            dump("VC", VC, [128, S], BF16, reads=("u2:VC",))
            dump("VA", VA.rearrange("p a b c d -> p (a b c d)"), [128, 2 * 2 * 16 * 128], BF16, reads=("x1:VA", "x1:VA1"))
            dump("G", G[0:48], [48, S], BF16, reads=("x1:G",))
        if stop_after <= 2:
            return
        wload(w1, None, "x1:Wsm", "wsm", 8, s, "w1", cast_fn=lambda: sch.add("pool", lambda e: [e.dma_start(out=w1[:, kv, 8 * i:8 * (i + 1), :], in_=dr["cmp_w1"][:, kv, 8 * i:8 * (i + 1), :]) for kv in range(2) for i in range(4)], w=("x1:Wsm",), dma=("wsm", 8)))
        if s == 0:
            def chm(e):
                ins = []
                for kv in range(2):
                    for l in range(32):
                        ins.append(mm(e, PB[6][:, kv:kv + 1], w1[0:64, kv, l, :], posT[0:64, kv, l:l + 1], l == 0, l == 31))
                return ins
            sch.add("pe", chm, r=("x1:Wsm", "c:posT"), w=("B6",))
            sch.add("dve", lambda e: e.tensor_tensor(out=chid, in0=PB[6][:, 0:2], in1=b1, op=ALU.add), r=("B6", "c:b1"), w=("c:chid",))
        for kv in range(2):
            for g in range(2):
                src = KC if kv == 0 else VC
                srcn = "u2:KC" if kv == 0 else "u2:VC"
                pr = slice(g * 64, (g + 1) * 64)

                def cm(e, src=src, pr=pr, kv=kv):
                    return [mm(e, PB[4][:, 0:127], w1[pr, kv, l, :], src[pr, l:l + 16 * 126 + 1:16], l == 0, l == 31) for l in range(32)]
                sch.add("pe", cm, r=(srcn, "x1:Wsm"), w=("B4",))
                xg = XG[:, 0:127]; x2 = X2[:, 0:127]; x3 = X3[:, 0:127]
                sch.add("act", lambda e, kv=kv: e.activation(out=xg, in_=PB[4][:, 0:127], func=AF.Identity, bias=chid[:, kv:kv + 1]),
                        r=("B4", "c:chid"), w=("u2:XG",))
                sch.add("dve", lambda e: e.tensor_tensor(out=x2, in0=xg, in1=xg, op=ALU.mult), r=("u2:XG",), w=("u2:X2",))
                sch.add("dve", lambda e: e.tensor_tensor(out=x3, in0=x2, in1=xg, op=ALU.mult), r=("u2:X2", "u2:XG"), w=("u2:X3",))
                sch.add("dve", lambda e: e.scalar_tensor_tensor(out=x2, in0=x3, scalar=0.044715, in1=xg, op0=ALU.mult, op1=ALU.add),
                        r=("u2:X3", "u2:XG"), w=("u2:X2",))
                sch.add("act", lambda e: e.activation(out=x3, in_=x2, func=AF.Sigmoid, scale=1.5957691216057308), r=("u2:X2",), w=("u2:X3",))
                sch.add("dve", lambda e: e.tensor_tensor(out=HID[:, 0:127], in0=xg, in1=x3, op=ALU.mult), r=("u2:XG", "u2:X3"), w=("u2:HID",))
                if kv == 0:
                    sch.add("pe", lambda e: mm(e, PB[5][0:64, 0:127], w2[:, 0, :], HID[:, 0:127], True, True), r=("u2:HID", "c:w2"), w=("B5",))
                    sch.add("act", lambda e, g=g: e.activation(out=KCMP[0:64, g, 0:127], in_=PB[5][0:64, 0:127], func=AF.Copy), r=("B5",), w=("m:KCMP",))
                else:
                    sch.add("pe", lambda e: mm(e, PB[5][0:127, 0:64], HID[:, 0:127], w2[:, 1, :], True, True), r=("u2:HID", "c:w2"), w=("B5",))
                    sch.add("act", lambda e, g=g: e.activation(out=VCMP[0:127, g, 0:64], in_=PB[5][0:127, 0:64], func=AF.Copy), r=("B5",), w=("m:VCMP",))
        if s == 0:
            dump("KCMP", KCMP, [128, 2, 128], BF16, reads=("m:KCMP",))
            dump("VCMP", VCMP, [128, 2, 128], BF16, reads=("m:VCMP",))
        if stop_after <= 3:
            return
        sch.barrier(UPFX)
        WQ = Wsm
        wload(WQ, dr["w_in"][:, :, 512:1536], "x1:Wsm", "wsm", 8, s, "wq")
        def attn_chunk(c):
            cc = slice(c * 512, (c + 1) * 512)
            def q_tile(tl):
                tt = 4 * c + tl
                ts_ = slice(tt * 128, (tt + 1) * 128)
                Z3 = ZQ.rearrange("p (h d) -> p h d", h=16, d=64)
                QN3 = QN.rearrange("p (h d) -> p h d", h=16, d=64)
                wq_ap = qkw[:, 0, :].unsqueeze(1).to_broadcast([128, 16, 64])
                btv = BT[:, 0:1024].rearrange("p (a b) -> p a b", a=8, b=128)
                QAe = QA.rearrange("p (i two) q -> p two i q", two=2)
                ls = slice(tl * 128, (tl + 1) * 128)

                def s_mm():
                    def qmm(e):
                        ins = []
                        for hf in range(2):
                            for kc in range(8):
                                ins.append(mm(e, PB[5 + hf][:, :], hT[:, kc, ts_], WQ[:, kc, hf * 512:(hf + 1) * 512], kc == 0, kc == 7))
                        return ins
                    sch.add("pe", qmm, r=("m:hT%d" % c, "x1:Wsm"), w=("B5", "B6"))

                def s_copy():
                    sch.add("act", lambda e: e.activation(out=ZQ[:, 0:512], in_=PB[5][:, :], func=AF.Copy), r=("B5",), w=("t:ZQ",))
                    sch.add("act", lambda e: e.activation(out=ZQ[:, 512:1024], in_=PB[6][:, :], func=AF.Copy), r=("B6",), w=("t:ZQ",))

                def s_A():
                    rope_A(Z3, "t:ZQ", 16, wq_ap, (1.0, 64 * EPS))

                def s_B():
                    rope_B(16, tt, QN3, "t:QN")
                    sch.add("pe", lambda e: [e.transpose(btv[:, i, :], QN[:, i * 128:(i + 1) * 128], ident) for i in range(8)],
                            r=("t:QN", "c:ident"), w=("BT",))
                    sch.add("dve", lambda e: e.tensor_copy(out=QAe[0:64, 0, :, ls], in_=btv[0:64, :, :]), r=("BT",), w=("x1:QA",))
                    sch.add("dve", lambda e: e.tensor_copy(out=QAe[0:64, 1, :, ls], in_=btv[64:128, :, :]), r=("BT",), w=("x1:QA",))
                return s_mm, s_copy, s_A, s_B
            qt = [q_tile(tl) for tl in range(4)]
            qt[0][0](); qt[0][1]()
            for tl in range(4):
                if tl + 1 < 4:
                    qt[tl + 1][0]()
                qt[tl][2]()
                if tl + 1 < 4:
                    qt[tl + 1][1]()
                qt[tl][3]()
            if s == 0 and c == 1:
                dump("QA", QA.rearrange("p a b -> p (a b)"), [128, 16 * 512], BF16, reads=("x1:QA",))
            if stop_after <= 4:
                return
            SB_ = [0, 1, 2]
            OB_ = [3, 4]
            for g in range(2):
                def stA(hl, g=g):
                    h = g * 8 + hl
                    sb = PB[SB_[h % 3]]; sbn = "B%d" % SB_[h % 3]
                    pt = PT[h % 3]; ptn = "t:PT%d" % (h % 3)
                    sch.add("pe", lambda e: [mm(e, sb[:, :], KCMP[0:64, g, :], QA[0:64, h, :], True, False),
                                             mm(e, sb[:, :], ident, cmpmask[:, cc], False, True)],
                            r=("m:KCMP", "x1:QA", "c:ident", "c:cmpmask"), w=(sbn,))
                    sch.add("act", lambda e: e.activation(out=pt, in_=sb[:, :], func=AF.Exp), r=(sbn,), w=(ptn,))

                def stB(hl, g=g):
                    h = g * 8 + hl
                    pt = PT[h % 3]; ptn = "t:PT%d" % (h % 3)
                    ecn = ECN[h % 2]; ecnn = "t:ECN%d" % (h % 2)
                    sch.add("pe", lambda e: mm(e, PB[5][:, :], ones, pt, True, True), r=(ptn, "c:ones"), w=("B5",))
                    sch.add("act", lambda e: e.activation(out=RR, in_=PB[5][:, :], func=AF.Ln, bias=1e-18), r=("B5",), w=("t:RR",))
                    sch.add("act", lambda e: e.activation(out=R2, in_=RR, func=AF.Exp, scale=-1.0), r=("t:RR",), w=("t:R2",))
                    sch.add("dve", lambda e: e.tensor_tensor(out=ecn, in0=pt, in1=R2, op=ALU.mult), r=(ptn, "t:R2"), w=(ecnn,))
                    if hl == 0:
                        sch.add("pool", lambda e: e.tensor_copy(out=PSC, in_=ecn), r=(ecnn,), w=("t:PSC",))
                    else:
                        sch.add("pool", lambda e: e.tensor_tensor(out=PSC, in0=PSC, in1=ecn, op=ALU.add), r=(ecnn, "t:PSC"), w=("t:PSC",))

                def stC(hl, g=g):
                    h = g * 8 + hl
                    ob = PB[OB_[h % 2]]; obn = "B%d" % OB_[h % 2]
                    ecn = ECN[h % 2]; ecnn = "t:ECN%d" % (h % 2)
                    sch.add("pe", lambda e: mm(e, ob[0:64, :], VCMP[:, g, 0:64], ecn, True, True), r=(ecnn, "m:VCMP"), w=(obn,))
                    ci = 3 * h + 0
                    sch.add("pe", lambda e: mm(e, PB[6][0:64, :], bsel[0:48, ci * 64:(ci + 1) * 64], G[0:48, cc], True, True),
                            r=("x1:G", "c:bsel"), w=("B6",))
                    sch.add("act", lambda e: e.activation(out=GSB[0:64, :], in_=PB[6][0:64, :], func=AF.Copy), r=("B6",), w=("t:GSB",))
                    pr = slice((h % 2) * 64, (h % 2) * 64 + 64)
                    sch.add("dve", lambda e: e.tensor_tensor(out=OT[pr, h // 2, cc], in0=ob[0:64, :], in1=GSB[0:64, :], op=ALU.mult),
                            r=(obn, "t:GSB"), w=("m:OT%d_%d" % (c, h),))
                for k in range(8 + 2):
                    if k < 8:
                        stA(k)
                    if 0 <= k - 1 < 8:
                        stB(k - 1)
                    if 0 <= k - 2 < 8:
                        stC(k - 2)
                sch.add("pool", lambda e: e.tensor_copy(out=PSCb, in_=PSC), r=("t:PSC",), w=("t:PSCb",))
                impv = PB[5][:, 0:128].rearrange("p (a b) -> p a b", a=4, b=32)
                sch.add("pe", lambda e: [mm(e, impv[:, qb, :], PSCb[:, qb * 128:(qb + 1) * 128], ov, True, True) for qb in range(4)],
                        r=("t:PSCb", "c:ov"), w=("B5",))
                sch.add("dve", lambda e, c=c: e.tensor_tensor(out=SC, in0=impv, in1=fb[:, 4 * c:4 * c + 4, :], op=ALU.add), r=("B5", "c:fb"), w=("t:SC",))
                for qb in range(4):
                    sch.add("dve", lambda e, qb=qb: e.max(out=M8[:, qb, :], in_=SC[:, qb, :]), r=("t:SC",), w=("t:M8",))
                    sch.add("dve", lambda e, qb=qb: e.tensor_scalar(out=SBf[:, qb, :], in0=SC[:, qb, :], scalar1=M8[:, qb, 7:8], scalar2=1.0,
                                                                    op0=ALU.is_ge, op1=ALU.subtract), r=("t:SC", "t:M8"), w=("t:SBf",))
                sch.add("dve", lambda e: e.tensor_scalar(out=SBb, in0=SBf, scalar1=-NEGB, scalar2=None, op0=ALU.mult), r=("t:SBf",), w=("t:SBb",))
                sch.add("pe", lambda e: [e.transpose(BT[0:32, qb * 128:(qb + 1) * 128], SBb[:, qb, :], ident) for qb in range(4)],
                        r=("t:SBb", "c:ident"), w=("BT",))
                sch.add("dve", lambda e, g=g: e.tensor_copy(out=QA[64:96, g * 8:(g + 1) * 8, :],
                                                            in_=BT[0:32, 0:512].unsqueeze(1).to_broadcast([32, 8, 512])),
                        r=("BT",), w=("x1:QAs",))
                if s == 0 and c == 1 and g == 0:
                    dump("SC", SC.rearrange("p a b -> p (a b)"), [128, 128], F32, reads=("t:SC",))
                    dump("SBf", SBf.rearrange("p a b -> p (a b)"), [128, 128], F32, reads=("t:SBf",))
            if stop_after <= 5:
                return
            jobs = []
            for h in range(16):
                g = h // 8
                tl_ = []
                for kt in range(0, 4 * c + 4):
                    j = kt - 4 * c
                    lo = 0 if j < 0 else 128 * j
                    tl_.append(dict(kt=kt, lo=lo, hi=512, mask=(None if j < 0 else (0, lo))))
                jobs.append(dict(h=h, g=g, br=1, tiles=tl_))
                tl_ = []
                order = [4] + ([0, 1, 2, 3] if c >= 1 else []) + [5, 6, 7]
                for i in order:
                    kt = 4 * c - 4 + i
                    if i <= 3:
                        tl_.append(dict(kt=kt, lo=0, hi=128 * (i + 1), mask=(1, 128 * i)))
                    else:
                        tl_.append(dict(kt=kt, lo=128 * (i - 4), hi=512, mask=(0, 128 * (i - 4))))
                jobs.append(dict(h=h, g=g, br=2, tiles=tl_))
            flat = []
            for jb_i, jb in enumerate(jobs):
                for ti, t in enumerate(jb["tiles"]):
                    flat.append((jb_i, ti))
            srot = [0]

            def rec_S(k):
                jb_i, ti = flat[k]
                jb = jobs[jb_i]; t = jb["tiles"][ti]
                si = k % 3
                sb = PB[SB_[si]]; t["si"] = si
                lo, hi, kt, g, h = t["lo"], t["hi"], t["kt"], jb["g"], jb["h"]
                ks = slice(kt * 128, (kt + 1) * 128)
                if jb["br"] == 1:
                    lhs = KAs[0:96, g, ks]; rhs = QA[0:96, h, lo:hi]; rn = ("x1:KAs", "x1:KAs_e", "x1:QA", "x1:QAs")
                else:
                    lhs = KAw[0:64, g, ks]; rhs = QA[0:64, h, lo:hi]; rn = ("x1:KAw", "x1:QA")
                mk = t["mask"]

                def f(e):
                    ins = [mm(e, sb[:, lo:hi], lhs, rhs, True, mk is None)]
                    if mk is not None:
                        ins.append(mm(e, sb[:, mk[1]:mk[1] + 128], ident, tri[:, mk[0] * 128:(mk[0] + 1) * 128], False, True))
                    return ins
                sch.add("pe", f, r=rn + ("c:ident", "c:tri"), w=("B%d" % SB_[si],))
                pt = PT[si]
                sch.add("act", lambda e: e.activation(out=pt[:, lo:hi], in_=sb[:, lo:hi], func=AF.Exp), r=("B%d" % SB_[si],), w=("t:PT%d" % si,))

            def rec_PV(k):
                jb_i, ti = flat[k]
                jb = jobs[jb_i]; t = jb["tiles"][ti]
                si = t["si"]
                oi = jb_i % 2
                ob = PB[OB_[oi]]; obn = "B%d" % OB_[oi]
                lo, hi, kt, g, h, br = t["lo"], t["hi"], t["kt"], jb["g"], jb["h"], jb["br"]
                pt = PT[si]
                first = ti == 0
                last = ti == len(jb["tiles"]) - 1
                sch.add("pe", lambda e: mm(e, ob[:, lo:hi], VA[:, br - 1, g, kt, :], pt[:, lo:hi], first, last),
                        r=("t:PT%d" % si, "x1:VA", "x1:VA1"), w=(obn,))
                if last:
                    ci = 3 * h + br
                    sch.add("pe", lambda e: mm(e, PB[6][0:64, :], bsel[0:48, ci * 64:(ci + 1) * 64], G[0:48, cc], True, True),
                            r=("x1:G", "c:bsel"), w=("B6",))
                    sch.add("dve", lambda e: e.reciprocal(out=RCPb[0:64, :], in_=ob[64:128, :]), r=(obn,), w=("t:RCP",))
                    sch.add("dve", lambda e: e.tensor_tensor(out=R2[0:64, :], in0=RCPb[0:64, :], in1=PB[6][0:64, :], op=ALU.mult), r=("t:RCP", "B6"), w=("t:R2",))
                    pr = slice((h % 2) * 64, (h % 2) * 64 + 64)
                    if br == 1:
                        sch.add("dve", lambda e: e.tensor_tensor(out=U1[pr, :], in0=ob[0:64, :], in1=R2[0:64, :], op=ALU.mult), r=(obn, "t:R2"), w=("t:U1",))
                    else:
                        sch.add("dve", lambda e: e.tensor_tensor(out=U2[pr, :], in0=ob[0:64, :], in1=R2[0:64, :], op=ALU.mult), r=(obn, "t:R2"), w=("t:U2",))
                        otn = "m:OT%d_%d" % (c, h)
                        sch.add("pool", lambda e: e.tensor_tensor(out=UT[pr, :], in0=U1[pr, :], in1=U2[pr, :], op=ALU.add), r=("t:U1", "t:U2"), w=("t:UT",))
                        sch.add("pool", lambda e: e.tensor_tensor(out=OT[pr, h // 2, cc], in0=OT[pr, h // 2, cc], in1=UT[pr, :], op=ALU.add),
                                r=("t:UT", otn), w=(otn,))
            DEPTH = 2
            for k in range(len(flat) + DEPTH):
                if k < len(flat):
                    rec_S(k)
                if k - DEPTH >= 0:
                    rec_PV(k - DEPTH)
        for c_i in range(4):
            attn_chunk(c_i)
        if s == 0:
            dump("OT", OT, [128, 8, S], BF16, reads=["m:OT%d_%d" % (c_, h_) for c_ in range(4) for h_ in range(16)])
        if stop_after <= 6:
            return
        sch.barrier(("x1:", "x2:", "t:", "u1:", "u2:"))
        wload(Wp, dr["w_in"][:, :, 0:512], "x2:Wp", "wsm", 8, s, "wp")
        sch.add("dve", lambda e: e.memset(UF[:, 0:16], 0.0), w=("x2:UF",))
        sch.add("dve", lambda e: e.memset(SA[:, 0:16], 0.0), w=("x2:SA",))
        sch.add("dve", lambda e: e.memset(SBp[:, 0:16], 0.0), w=("x2:SB",))
        for g in range(4):
            for c in range(4):
                cc = slice(c * 512, (c + 1) * 512)
                bi = nextbank([0, 1, 2, 3])
                sch.add("pe", lambda e, bi=bi, g=g, cc=cc: [mm(e, PB[bi][:, :], Wp[:, kc, g * 128:(g + 1) * 128], hT[:, kc, cc], kc == 0, kc == 7) for kc in range(8)],
                        r=("m:hT%d" % c, "x2:Wp"), w=("B%d" % bi,))
                sch.add("act", lambda e, bi=bi, c=c: e.activation(out=UF[:, 16 + c * 512:16 + (c + 1) * 512], in_=PB[bi][:, :], func=AF.Copy),
                        r=("B%d" % bi,), w=("x2:UF",))
            bufs = [(UF, "x2:UF"), (SA, "x2:SA"), (SBp, "x2:SB")]
            srcb = bufs[0]
            for lvl in range(g + 1):
                sh = 1 << lvl
                dstb = bufs[1 + (lvl % 2)]
                eng = "dve" if lvl % 2 == 0 else "pool"
                sch.add(eng, lambda e, srcb=srcb, dstb=dstb, sh=sh: e.tensor_tensor(out=dstb[0][:, 16:16 + S], in0=srcb[0][:, 16:16 + S],
                                                                                  in1=srcb[0][:, 16 - sh:16 + S - sh], op=ALU.add),
                        r=(srcb[1],), w=(dstb[1],))
                srcb = dstb
            wdw = 2 << g
            sch.add("dve", lambda e, srcb=srcb, g=g: e.tensor_tensor(out=srcb[0][:, 16:32], in0=srcb[0][:, 16:32], in1=poolcorr[:, g, :], op=ALU.mult),
                    r=(srcb[1], "c:poolcorr"), w=(srcb[1],))
            sch.add("dve", lambda e, srcb=srcb, wdw=wdw: e.scalar_tensor_tensor(out=PLb, in0=srcb[0][:, 16:16 + S], scalar=1.0 / wdw, in1=UF[:, 16:16 + S],
                                                                             op0=ALU.mult, op1=ALU.subtract), r=(srcb[1], "x2:UF"), w=("x2:PLb",))
            for c in range(4):
                cc = slice(c * 512, (c + 1) * 512)
                bi = nextbank([4, 5, 6])
                sch.add("pe", lambda e, bi=bi, g=g, cc=cc: mm(e, PB[bi][:, :], pool_w[:, g, :], PLb[:, cc], True, True), r=("x2:PLb", "c:pool_w"), w=("B%d" % bi,))
                sch.add("act", lambda e, bi=bi, g=g, cc=cc: e.activation(out=YP[:, g, cc], in_=PB[bi][:, :], func=AF.Copy, scale=pool_scale[:, g:g + 1]),
                        r=("B%d" % bi, "c:pool_scale"), w=("m:YP",))
        if s == 0:
            dump("YP", YP, [128, 4, S], BF16, reads=("m:YP",))
        if stop_after <= 7:
            return
        sch.barrier(("x2:", "x3:", "t:"))
        wload(Wmg, dr["w_in"][:, :, 2352:4400], "x3:Wmg", "wmg", 8, s)
        wload(Wpb, dr["w_pool_br"], "x3:Wpb", "wpb", 4, s)
        wload(Wab, dr["w_attn_br"], "x3:Wab", "wab", 8, s)
        wload(Wo, dr["w_o"], "x3:Wo", "wo", 8, s)
        for c in range(4):
            cc = slice(c * 512, (c + 1) * 512)
            otn = ["m:OT%d_%d" % (c, h_) for h_ in range(16)]
            for dc in range(8):
                dsl = slice(dc * 128, (dc + 1) * 128)
                b_yp, b_ya, b_m0, b_m1 = [nextbank([0, 1, 2, 3, 4, 5, 6]) for _ in range(4)]
                sch.add("pe", lambda e, b=b_m0, dsl=dsl, cc=cc: [mm(e, PB[b][:, :], Wmg[:, kc, dsl], hT[:, kc, cc], kc == 0, kc == 7) for kc in range(8)],
                        r=("m:hT%d" % c, "x3:Wmg"), w=("B%d" % b_m0,))
                sch.add("pe", lambda e, b=b_m1, dc=dc, cc=cc: [mm(e, PB[b][:, :], Wmg[:, kc, 1024 + dc * 128:1024 + (dc + 1) * 128], hT[:, kc, cc], kc == 0, kc == 7) for kc in range(8)],
                        r=("m:hT%d" % c, "x3:Wmg"), w=("B%d" % b_m1,))
                sch.add("pe", lambda e, b=b_yp, dsl=dsl, cc=cc: [mm(e, PB[b][:, :], Wpb[:, g, dsl], YP[:, g, cc], g == 0, g == 3) for g in range(4)],
                        r=("m:YP", "x3:Wpb"), w=("B%d" % b_yp,))
                sch.add("pe", lambda e, b=b_ya, dsl=dsl, cc=cc: [mm(e, PB[b][:, :], Wab[:, i, dsl], OT[:, i, cc], i == 0, i == 7) for i in range(8)],
                        r=tuple(otn) + ("x3:Wab",), w=("B%d" % b_ya,))
                sch.add("act", lambda e, b=b_m0: e.activation(out=SG[0], in_=PB[b][:, :], func=AF.Sigmoid), r=("B%d" % b_m0,), w=("x3:SG0",))
                sch.add("act", lambda e, b=b_m1: e.activation(out=SG[1], in_=PB[b][:, :], func=AF.Sigmoid), r=("B%d" % b_m1,), w=("x3:SG1",))
                sch.add("dve", lambda e, b=b_yp: e.tensor_tensor(out=T0, in0=SG[0], in1=PB[b][:, :], op=ALU.mult), r=("x3:SG0", "B%d" % b_yp), w=("x3:T0",))
                sch.add("dve", lambda e, b=b_ya: e.tensor_tensor(out=T1, in0=SG[1], in1=PB[b][:, :], op=ALU.mult), r=("x3:SG1", "B%d" % b_ya), w=("x3:T1",))
                sch.add("pool", lambda e, dc=dc: e.tensor_tensor(out=MT[:, dc, :], in0=T0, in1=T1, op=ALU.add), r=("x3:T0", "x3:T1"), w=("x3:MT",))
            for tl in range(4):
                tt = 4 * c + tl
                ls = slice(tl * 128, (tl + 1) * 128)
                xt = xs6[tt % 2]; xn = "x3:xs%d" % (tt % 2)
                sch.add("sp", lambda e, xt=xt, tt=tt: e.dma_start(out=xt, in_=x_d[s, tt * 128:(tt + 1) * 128, :]), w=(xn,), dma=("xs6%d" % (tt % 2), 1))
                b0, b1_ = nextbank([0, 1, 2, 3, 4, 5, 6]), nextbank([0, 1, 2, 3, 4, 5, 6])
                for hf, b in ((0, b0), (1, b1_)):
                    sch.add("pe", lambda e, b=b, hf=hf, ls=ls: [mm(e, PB[b][:, :], MT[:, dc, ls], Wo[:, dc, hf * 512:(hf + 1) * 512], dc == 0, dc == 7) for dc in range(8)],
                            r=("x3:MT", "x3:Wo"), w=("B%d" % b,))
                    sch.add("dve", lambda e, b=b, hf=hf, xt=xt: e.tensor_tensor(out=xt[:, hf * 512:(hf + 1) * 512], in0=xt[:, hf * 512:(hf + 1) * 512], in1=PB[b][:, :], op=ALU.add),
                            r=(xn, "B%d" % b), w=(xn,))
                sch.add("sp", lambda e, xt=xt, tt=tt: e.dma_start(out=(x1_d if stop_after > 8 else out_d)[s, tt * 128:(tt + 1) * 128, :], in_=xt), r=(xn,), w=("o:%d_%d" % (s, tt),), dma=("xo6%d" % (tt % 2), 1))
        if stop_after <= 8:
            return
        sch.barrier(("m:", "x1:", "x2:", "x3:", "f:", "t:", "u1:", "u2:"))
        if "wup" not in scr:
            scr["wup"] = nc.dram_tensor("wscr_wup", [128, 8, 5504], BF16, kind="Internal").ap()
            scr["wdn"] = nc.dram_tensor("wscr_wdn", [128, 22, 1024], BF16, kind="Internal").ap()

        def wblock(nm, tag, dst_of, src_of, scr_of, ks):
            n = len(ks)
            if s == 0:
                sch.add("pool", lambda e: [e.dma_start(out=dst_of(k), in_=src_of(k)) for k in ks], w=(nm,), dma=("c_" + tag, n))
                sch.add("sp", lambda e: [e.dma_start(out=scr_of(k), in_=dst_of(k)) for k in ks], r=(nm,), w=("scr:" + tag,), dma=("st_" + tag, n))
            else:
                sch.add("sp", lambda e: [e.dma_start(out=dst_of(k), in_=scr_of(k)) for k in ks], r=("scr:" + tag,), w=(nm,), dma=("ld_" + tag, n))

        def wup_block(i):
            cs = slice(1376 * i, 1376 * (i + 1))
            wblock("f:Wup%d" % i, "wup%d" % i, lambda k: Wup[:, k, cs], lambda k: dr["w_up"][:, k, cs], lambda k: scr["wup"][:, k, cs], list(range(8)))

        def wdn_block(j):
            ks = list(range(6 * j, min(22, 6 * (j + 1))))
            wblock("f:Wdn%d" % j, "wdn%d" % j, lambda k: Wdn[:, k, :], lambda k: dr["w_down"][:, k, :], lambda k: scr["wdn"][:, k, :], ks)
        wup_block(0); wup_block(2); wdn_block(0); wup_block(1); wup_block(3); wdn_block(1); wdn_block(2); wdn_block(3)
        sch.add("dve", lambda e: e.memset(HALO, 0.0), w=tuple("f:HALO%d" % i for i in range(22)))
        DB = [0, 1, 2, 3]
        UB = [4, 5, 6]
        for c8 in range(8):
            H2c = H2[c8 % 2]; h2n = "f:H2_%d" % (c8 % 2)
            for tl in range(2):
                tt = 2 * c8 + tl
                xi = (c8 % 2) * 2 + tl
                xt = X1S[xi]; xn = "f:x1s%d" % xi
                sch.add("sp", lambda e, xt=xt, tt=tt: e.dma_start(out=xt, in_=x1_d[s, tt * 128:(tt + 1) * 128, :]), r=("o:%d_%d" % (s, tt),), w=(xn,), dma=("x1s%d" % xi, 1))
                norm_tile(xt, xn, hn7[tl], "f:hn%d" % tl, 1, H2c, h2n, tl * 128, 1 + tl)

            def rec_up(fc, H2c=H2c, h2n=h2n):
                rows = 128 if fc < 21 else 64
                par = fc % 3
                b = UB[par]; bn = "B%d" % b
                ue = UE[par]; uen = "f:UE%d" % par

                def upmm(e):
                    ins = []
                    for gv in range(2):
                        col0 = gv * D_FF + fc * 128
                        for kc in range(8):
                            ins.append(mm(e, PB[b][0:rows, gv * 256:(gv + 1) * 256], Wup[:, kc, col0:col0 + rows], H2c[:, kc, :], kc == 0, kc == 7))
                    return ins
                ublk = sorted(set(c_ // 1376 for gv_ in range(2) for c_ in (gv_ * D_FF + fc * 128, gv_ * D_FF + fc * 128 + rows - 1)))
                sch.add("pe", upmm, r=(h2n,) + tuple("f:Wup%d" % i_ for i_ in ublk), w=(bn,))
                hn_ = "f:HALO%d" % fc
                sch.add("pool", lambda e: e.tensor_copy(out=ue[0:rows, :, 0:2], in_=HALO[0:rows, :, fc, :]), r=(hn_,), w=(uen,))
                sch.add("act", lambda e: e.activation(out=ue[0:rows, :, 2:258], in_=PB[b][0:rows, :].rearrange("p (g t) -> p g t", g=2, t=256), func=AF.Copy),
                        r=(bn,), w=(uen,))
                sch.add("pool", lambda e: e.tensor_copy(out=HALO[0:rows, :, fc, :], in_=ue[0:rows, :, 256:258]), r=(uen,), w=(hn_,))

            def rec_conv(fc):
                rows = 128 if fc < 21 else 64
                par = fc % 3
                ue = UE[par]; uen = "f:UE%d" % par
                cxs = ((0, CG[par], "f:CG%d" % par), (1, CV[par], "f:CV%d" % par))
                cw = lambda tap, gv: conv[0:rows, gv, fc, tap:tap + 1]
                for gv, cx, cxn in cxs:
                    sch.add("dve", lambda e, cx=cx, gv=gv: e.tensor_scalar(out=cx[0:rows, :], in0=ue[0:rows, gv, 2:258], scalar1=cw(2, gv), scalar2=cw(3, gv), op0=ALU.mult, op1=ALU.add),
                            r=(uen, "c:conv"), w=(cxn,))
                for gv, cx, cxn in cxs:
                    sch.add("dve", lambda e, cx=cx, gv=gv: e.scalar_tensor_tensor(out=cx[0:rows, :], in0=ue[0:rows, gv, 1:257], scalar=cw(1, gv), in1=cx[0:rows, :], op0=ALU.mult, op1=ALU.add),
                            r=(uen, cxn, "c:conv"), w=(cxn,))
                for gv, cx, cxn in cxs:
                    sch.add("dve", lambda e, cx=cx, gv=gv: e.scalar_tensor_tensor(out=cx[0:rows, :], in0=ue[0:rows, gv, 0:256], scalar=cw(0, gv), in1=cx[0:rows, :], op0=ALU.mult, op1=ALU.add),
                            r=(uen, cxn, "c:conv"), w=(cxn,))

            def rec_act(fc):
                rows = 128 if fc < 21 else 64
                par = fc % 3
                sg = SGF[par]; af = AF_[par]
                sch.add("act", lambda e: e.activation(out=sg[0:rows, :], in_=CG[par][0:rows, :], func=AF.Silu), r=("f:CG%d" % par,), w=("f:SG%d" % par,))
                sch.add("pool", lambda e: e.tensor_tensor(out=af[0:rows, :], in0=sg[0:rows, :], in1=CV[par][0:rows, :], op=ALU.mult),
                        r=("f:SG%d" % par, "f:CV%d" % par), w=("f:A%d" % par,))

            def rec_down(fc):
                rows = 128 if fc < 21 else 64
                par = fc % 3
                af = AF_[par]

                def dmm(e):
                    ins = []
                    for tl in range(2):
                        for hf in range(2):
                            ins.append(mm(e, PB[DB[tl * 2 + hf]][:, :], af[0:rows, tl * 128:(tl + 1) * 128], Wdn[0:rows, fc, hf * 512:(hf + 1) * 512], fc == 0, fc == 21))
                    return ins
                sch.add("pe", dmm, r=("f:A%d" % par, "f:Wdn%d" % (fc // 6)), w=("B0", "B1", "B2", "B3"))
            for k in range(22 + 3):
                if k < 22:
                    rec_up(k)
                    rec_conv(k)
                if 0 <= k - 1 < 22:
                    rec_act(k - 1)
                if 0 <= k - 3 < 22:
                    rec_down(k - 3)
            for tl in range(2):
                tt = 2 * c8 + tl
                xi = (c8 % 2) * 2 + tl
                xt = X1S[xi]; xn = "f:x1s%d" % xi
                for hf in range(2):
                    b = DB[tl * 2 + hf]
                    sch.add("dve", lambda e, xt=xt, b=b, hf=hf: e.tensor_tensor(out=xt[:, hf * 512:(hf + 1) * 512], in0=xt[:, hf * 512:(hf + 1) * 512], in1=PB[b][:, :], op=ALU.add),
                            r=(xn, "B%d" % b), w=(xn,))
                sch.add("sp", lambda e, xt=xt, tt=tt: e.dma_start(out=out_d[s, tt * 128:(tt + 1) * 128, :], in_=xt), r=(xn,), w=("of:%d_%d" % (s, tt),), dma=("x1o%d" % xi, 1))
        sch.barrier(("m:", "x1:", "x2:", "x3:", "f:", "t:", "u1:", "u2:"))

    for s_i in range(nseq):
        do_seq(s_i)

    sch.finalize()
    print("ops:", len(sch.ops))
    sems = {}
    for k in sch.sem_keys():
        sems[k] = es.enter_context(nc.semaphore("s_%s_%s" % k))
    with nc.Block() as block:
        @block.sync
        def _(e):
            sch.emit("sp", e, sems)

        @block.tensor
        def _(e):
            sch.emit("pe", e, sems)

        @block.scalar
        def _(e):
            sch.emit("act", e, sems)

        @block.vector
        def _(e):
            sch.emit("dve", e, sems)

        @block.gpsimd
        def _(e):
            sch.emit("pool", e, sems)
    es.close()
    return nc, dbg_d


_CACHE = {}


def kernel(**inputs):
    x = np.asarray(inputs["x"], dtype=np.float32)
    B = x.shape[0]
    nseq = B // NCORES
    consts = host_consts()
    wts = host_weights(inputs)
    if "nc" not in _CACHE:
        _CACHE["nc"] = build(nseq=nseq)[0]
    nc = _CACHE["nc"]
    in_maps = []
    for c in range(NCORES):
        m = {"x": np.ascontiguousarray(x[c * nseq:(c + 1) * nseq])}
        m.update(consts)
        m.update(wts)
        in_maps.append(m)
    res = run_bass_kernel_spmd(nc, in_maps, core_ids=list(range(NCORES)))
    out = np.concatenate([np.asarray(r["out"], dtype=np.float32) for r in res.results], axis=0)
    return out
```

```python
import contextlib
import os
import numpy as np
import ml_dtypes
import concourse.bass as bass
import concourse.mybir as mybir
from concourse.bass_utils import run_bass_kernel_spmd

F32 = mybir.dt.float32
BF16 = mybir.dt.bfloat16
ALU = mybir.AluOpType
AF = mybir.ActivationFunctionType
AX = mybir.AxisListType

S = 2048
D = 1024
NCORES = 8
EPS = 1e-6
NEGB = -30000.0
D_FF = 2752
ENGS = ("pe", "act", "dve", "pool", "sp")


class Op:
    __slots__ = ("eng", "fn", "dma", "deps", "signal", "val", "w", "waits")


class Sched:
    def __init__(self):
        self.ops = []
        self.lastw = {}
        self.readers = {}
        self.dmacum = {}
        self.barriers = []

    def add(self, eng, fn, r=(), w=(), dma=None):
        op = Op()
        op.eng = eng
        op.fn = fn
        op.dma = dma
        op.signal = False
        op.val = None
        op.deps = []
        seen = set()
        for n in tuple(r) + tuple(w):
            if n not in self.lastw:
                for pf, bop in reversed(self.barriers):
                    if n.startswith(pf):
                        self.lastw[n] = bop
                        self.readers.setdefault(n, [])
                        break
        for n in r:
            d = self.lastw.get(n)
            if d is not None and id(d) not in seen:
                seen.add(id(d))
                op.deps.append((d, "raw"))
        for n in w:
            d = self.lastw.get(n)
            if d is not None and id(d) not in seen:
                seen.add(id(d))
                op.deps.append((d, "waw"))
            for d in self.readers.get(n, ()):
                if id(d) not in seen:
                    seen.add(id(d))
                    op.deps.append((d, "war"))
        for n in r:
            self.readers.setdefault(n, []).append(op)
        for n in w:
            self.lastw[n] = op
            self.readers[n] = []
        if dma is not None:
            self.dmacum[dma[0]] = self.dmacum.get(dma[0], 0) + 16 * dma[1]
            op.val = self.dmacum[dma[0]]
        self.ops.append(op)
        return op

    def barrier(self, prefixes):
        names = [n for n in set(self.lastw) | set(self.readers) if n.startswith(prefixes)]
        bop = self.add("pool", lambda e: e.nop(), r=(), w=names)
        self.barriers.append((prefixes, bop))

    def finalize(self):
        for op in self.ops:
            op.w = []
            for d, kind in op.deps:
                if d.dma is not None:
                    op.w.append(d)
                elif d.eng == op.eng:
                    if op.dma is not None:
                        op.w.append(d)
                        d.signal = True
                    elif op.eng == "pe":
                        continue
                    elif kind == "raw":
                        op.w.append(d)
                        d.signal = True
                else:
                    op.w.append(d)
                    d.signal = True
        cnt = {}
        for op in self.ops:
            if op.dma is None and op.signal:
                cnt[op.eng] = cnt.get(op.eng, 0) + 1
                op.val = cnt[op.eng]
        waited = {e: {} for e in ENGS}
        for op in self.ops:
            ws = {}
            for d in op.w:
                key = ("dma", d.dma[0]) if d.dma is not None else ("eng", d.eng)
                if waited[op.eng].get(key, 0) >= d.val:
                    continue
                ws[key] = max(ws.get(key, 0), d.val)
            for k, v in ws.items():
                waited[op.eng][k] = v
            op.waits = list(ws.items())

    def sem_keys(self):
        return [("eng", e) for e in ENGS] + [("dma", s) for s in self.dmacum]

    def emit(self, eng_name, e, sems):
        for op in self.ops:
            if op.eng != eng_name:
                continue
            for key, v in op.waits:
                e.wait_ge(sems[key], v)
            insts = op.fn(e)
            if not isinstance(insts, (list, tuple)):
                insts = [insts]
            if op.dma is not None:
                assert len(insts) == op.dma[1], (len(insts), op.dma)
                for i in insts:
                    i.then_inc(sems[("dma", op.dma[0])], 16)
            elif op.signal:
                insts[-1].then_inc(sems[("eng", op.eng)], 1)
        if eng_name == "sp":
            for s, v in self.dmacum.items():
                e.wait_ge(sems[("dma", s)], v)


def host_consts():
    bf = ml_dtypes.bfloat16
    c = {}
    c["ident"] = np.eye(128, dtype=np.float32).astype(bf)
    half = 32
    freqs = (10000.0 ** (-np.arange(half, dtype=np.float32) / half)).astype(np.float32)
    pos = np.arange(S, dtype=np.float32)
    ang = (pos[:, None] * freqs[None, :]).astype(np.float32)
    cs = np.stack([np.cos(ang), np.sin(ang)], 0).astype(np.float32)
    c["cossin"] = np.ascontiguousarray(cs.reshape(2, 16, 128, 32).transpose(2, 0, 1, 3))
    n = np.arange(128)[:, None]
    q = np.arange(S)[None, :]
    c["cmpmask"] = np.where((n < 127) & (16 * n + 31 <= q), 0.0, NEGB).astype(bf)
    kk = np.arange(128)[:, None]
    qq = np.arange(128)[None, :]
    tri = np.concatenate([np.where(qq >= kk, 0.0, NEGB), np.where(qq < kk, 0.0, NEGB)], 1)
    c["tri"] = tri.astype(bf)
    j = np.arange(32)[:, None]
    k = np.arange(S)[None, :]
    c["esel"] = (k // 64 == j).astype(np.float32).astype(bf)
    tq = np.arange(S)[:, None]
    jb = np.arange(32)[None, :]
    cur = tq // 64
    forced = (jb == 0) | (jb == cur) | (jb == cur - 1)
    valid = jb * 64 <= tq
    fb = np.where(valid, 1000.0 * forced, -1e30).astype(np.float32)
    c["fb"] = np.ascontiguousarray(fb.reshape(16, 128, 32).transpose(1, 0, 2))
    ci = np.arange(128)[:, None]
    sj = np.arange(32)[None, :]
    ov = ((ci * 16 < (sj + 1) * 64) & (ci * 16 + 32 > sj * 64) & (ci < 127)).astype(np.float32)
    c["ov"] = ov.astype(bf)
    bs = np.zeros((48, 48, 64), np.float32)
    for i in range(48):
        bs[i, i, :] = 1.0
    c["bsel"] = bs.reshape(48, 48 * 64).astype(bf)
    pc = np.ones((128, 4, 16), np.float32)
    for g, w in enumerate((2, 4, 8, 16)):
        t = np.arange(16)
        pc[:, g, :] = (w / np.minimum(t + 1, w))[None, :]
    c["poolcorr"] = pc
    return c


def host_weights(inp):
    f = np.float32
    w = {}
    A = lambda a: np.ascontiguousarray(np.asarray(a, dtype=f))
    w["w_in"] = A(inp["w_in"][0].reshape(8, 128, 4400).transpose(1, 0, 2))
    w["w_pool_br"] = A(inp["w_pool_br"][0].reshape(4, 128, 1024).transpose(1, 0, 2))
    w["w_attn_br"] = A(inp["w_attn_br"][0].reshape(8, 128, 1024).transpose(1, 0, 2))
    w["w_o"] = A(inp["w_o"][0].reshape(8, 128, 1024).transpose(1, 0, 2))
    w["w_up"] = A(inp["w_up"][0].reshape(8, 128, 5504).transpose(1, 0, 2))
    wd = np.zeros((22 * 128, 1024), f)
    wd[:D_FF] = inp["w_down"][0]
    w["w_down"] = A(wd.reshape(22, 128, 1024).transpose(1, 0, 2))
    w["pool_w"] = A(inp["pool_w"][0].transpose(1, 0, 2))
    w1 = np.asarray(inp["cmp_w1"][0]).reshape(2, 32, 64, 128).transpose(2, 0, 1, 3)
    w["cmp_w1"] = A(np.concatenate([w1, w1], 0))
    w["cmp_w2"] = A(np.asarray(inp["cmp_w2"][0]).transpose(1, 0, 2))
    pt = np.asarray(inp["cmp_pos"][0]).transpose(2, 0, 1)
    w["cmp_posT"] = A(np.concatenate([pt, pt], 0))
    w["cmp_b1"] = A(np.asarray(inp["cmp_b1"][0]).T)
    nrm = np.stack([np.asarray(inp["attn_norm_w"][0]).reshape(8, 128).T,
                    np.asarray(inp["ffn_norm_w"][0]).reshape(8, 128).T], 1)
    w["nrm"] = A(nrm)
    w["pool_scale"] = A(np.asarray(inp["pool_scale"][0]).reshape(4, 128).T)
    qk = np.concatenate([np.asarray(inp["q_norm_w"][0])[None], np.asarray(inp["k_norm_w"][0])], 0)
    w["qkw"] = A(np.broadcast_to(qk[None], (128, 4, 64)))
    cw = np.asarray(inp["conv_w"][0])
    cb = np.asarray(inp["conv_b"][0])
    cv = np.zeros((128, 2, 22, 4), f)
    for gv in range(2):
        for fc in range(22):
            rows = 128 if fc < 21 else 64
            sl = slice(gv * D_FF + fc * 128, gv * D_FF + fc * 128 + rows)
            cv[:rows, gv, fc, 0:3] = cw[:, sl].T
            cv[:rows, gv, fc, 3] = cb[sl]
    w["conv"] = cv
    return w


DRAM_SPECS = {
    "w_in": ([128, 8, 4400], F32), "w_pool_br": ([128, 4, 1024], F32), "w_attn_br": ([128, 8, 1024], F32),
    "w_o": ([128, 8, 1024], F32), "w_up": ([128, 8, 5504], F32), "w_down": ([128, 22, 1024], F32),
    "pool_w": ([128, 4, 128], F32), "cmp_w1": ([128, 2, 32, 128], F32), "cmp_w2": ([128, 2, 64], F32),
    "cmp_posT": ([128, 2, 32], F32), "cmp_b1": ([128, 2], F32), "nrm": ([128, 2, 8], F32),
    "pool_scale": ([128, 4], F32), "qkw": ([128, 4, 64], F32), "conv": ([128, 2, 22, 4], F32),
    "ident": ([128, 128], BF16), "cossin": ([128, 2, 16, 32], F32), "cmpmask": ([128, 2048], BF16),
    "tri": ([128, 256], BF16), "esel": ([32, 2048], BF16), "fb": ([128, 16, 32], F32),
    "ov": ([128, 32], BF16), "bsel": ([48, 3072], BF16), "poolcorr": ([128, 4, 16], F32),
}


def build(nseq=4, stop_after=99, dbg=False):
    nc = bass.Bass("TRN2", target_bir_lowering=False)
    sch = Sched()
    dr = {}
    for name, (shape, dt) in DRAM_SPECS.items():
        dr[name] = nc.dram_tensor(name, shape, dt, kind="ExternalInput").ap()
    x_d = nc.dram_tensor("x", [nseq, S, D], F32, kind="ExternalInput").ap()
    out_d = nc.dram_tensor("out", [nseq, S, D], F32, kind="ExternalOutput").ap()
    x1_d = nc.dram_tensor("x1scratch", [nseq + 1, S, D], F32, kind="Internal").ap()[1:nseq + 1]
    dbg_d = {}

    es = contextlib.ExitStack()
    ARENA_B = 206 * 1024
    arena = es.enter_context(nc.sbuf_tensor("arena", [128, ARENA_B // 2], BF16))
    cur = [0]

    def alloc(shape, dt=BF16):
        n = 1
        for s_ in shape[1:]:
            n *= s_
        nb = n * (4 if dt == F32 else 2)
        nb = (nb + 63) // 64 * 64
        off = cur[0]
        cur[0] += nb
        assert cur[0] <= ARENA_B, ("arena overflow", cur[0])
        ap = arena[:, off // 2:(off + nb) // 2]
        if dt == F32:
            ap = ap.bitcast(F32)
        ap = ap[:, 0:n]
        if len(shape) == 3:
            ap = ap.rearrange("p (a b) -> p a b", a=shape[1], b=shape[2])
        elif len(shape) == 4:
            ap = ap.rearrange("p (a b c) -> p a b c", a=shape[1], b=shape[2], c=shape[3])
        elif len(shape) == 5:
            ap = ap.rearrange("p (a b c d) -> p a b c d", a=shape[1], b=shape[2], c=shape[3], d=shape[4])
        return ap

    PB = [es.enter_context(nc.psum_tensor("pb%d" % i, [128, 512], F32)) for i in range(7)]
    BT = es.enter_context(nc.psum_tensor("pbt", [128, 1024], BF16))
    pbrot = [0]

    def nextbank(pool):
        i = pool[pbrot[0] % len(pool)]
        pbrot[0] += 1
        return i

    ident = alloc([128, 128]); cossin = alloc([128, 2, 16, 32], F32); cmpmask = alloc([128, 2048])
    tri = alloc([128, 256]); fb = alloc([128, 16, 32], F32); ov = alloc([128, 32])
    bsel = alloc([128, 3072]); poolcorr = alloc([128, 4, 16], F32); nrm = alloc([128, 2, 8], F32)
    pool_scale = alloc([128, 4], F32); qkw = alloc([128, 4, 64], F32); conv = alloc([128, 2, 22, 4], F32)
    b1 = alloc([128, 2], F32); chid = alloc([128, 2], F32); w2 = alloc([128, 2, 64]); pool_w = alloc([128, 4, 128])
    posT = alloc([128, 2, 32]); ones = alloc([128, 128]); mhalf = alloc([128, 16], F32)
    KCMP = alloc([128, 2, 128]); VCMP = alloc([128, 2, 128])
    ssq = alloc([128, 16], F32); rs = alloc([128, 16], F32); sr = alloc([128, 16], F32); rstd = alloc([128, 16], F32)
    mark0 = cur[0]
    hT = alloc([128, 8, S]); OT = alloc([128, 8, S])
    markX = cur[0]
    KAs = alloc([128, 2, S]); KAw = alloc([128, 2, S])
    VA = alloc([128, 2, 2, 16, 128]); G = alloc([128, S])
    Wsm = alloc([128, 8, 1024])
    w1 = Wsm.rearrange("p a b -> p (a b)").rearrange("p (k l h) -> p k l h", k=2, l=32, h=128)
    ZQ = alloc([128, 1024], F32); SQ = alloc([128, 1024], F32); QW = alloc([128, 1024], F32)
    TA = alloc([128, 512], F32); TB = alloc([128, 512], F32); TC = alloc([128, 512], F32); TD = alloc([128, 512], F32)
    QN = alloc([128, 1024]); KZb = alloc([128, 384], F32)
    markU = cur[0]
    xs = [alloc([128, 1024], F32) for _ in range(2)]
    hn = [alloc([128, 1024]) for _ in range(2)]
    endU1 = cur[0]
    cur[0] = markU
    KC = alloc([128, S]); VC = alloc([128, S])
    HID = alloc([128, 128]); XG = alloc([128, 128], F32); X2 = alloc([128, 128], F32); X3 = alloc([128, 128], F32)
    endU2 = cur[0]
    cur[0] = markU
    QA = alloc([128, 16, 512])
    PT = [alloc([128, 512]) for _ in range(3)]
    RR = alloc([128, 512], F32); R2 = alloc([128, 512], F32)
    U1 = alloc([128, 512], F32); U2 = alloc([128, 512], F32); UT = alloc([128, 512], F32)
    ECN = [alloc([128, 512]) for _ in range(2)]
    PSC = alloc([128, 512], F32); PSCb = alloc([128, 512])
    SC = alloc([128, 4, 32], F32); M8 = alloc([128, 4, 8], F32); SBf = alloc([128, 4, 32], F32); SBb = alloc([128, 4, 32])
    GSB = alloc([128, 512], F32); RCPb = alloc([128, 512], F32)
    endX1 = max(cur[0], endU1, endU2)
    cur[0] = markX
    YP = alloc([128, 4, S])
    markY = cur[0]
    UF = alloc([128, 16 + S], F32); SA = alloc([128, 16 + S], F32); SBp = alloc([128, 16 + S], F32); PLb = alloc([128, S])
    Wp = alloc([128, 8, 512])
    endX2 = cur[0]
    cur[0] = markY
    Wmg = alloc([128, 8, 2048]); Wpb = alloc([128, 4, 1024]); Wab = alloc([128, 8, 1024]); Wo = alloc([128, 8, 1024])
    MT = alloc([128, 8, 512]); SG = [alloc([128, 512], F32) for _ in range(2)]
    T0 = alloc([128, 512], F32); T1 = alloc([128, 512], F32)
    xs6 = [alloc([128, 1024], F32) for _ in range(2)]
    endX3 = cur[0]
    cur[0] = mark0
    Wup = alloc([128, 8, 5504]); Wdn = alloc([128, 22, 1024])
    H2 = [alloc([128, 8, 256]) for _ in range(2)]; X1S = [alloc([128, 1024], F32) for _ in range(4)]
    hn7 = [alloc([128, 1024]) for _ in range(2)]
    UE = [alloc([128, 2, 258], F32) for _ in range(3)]
    CG = [alloc([128, 256], F32) for _ in range(3)]; CV = [alloc([128, 256], F32) for _ in range(3)]
    SGF = [alloc([128, 256], F32) for _ in range(3)]; AF_ = [alloc([128, 256]) for _ in range(3)]
    HALO = alloc([128, 2, 22, 2], F32)
    endF = cur[0]
    print("arena bytes: X1 %d X2 %d X3 %d FFN %d" % (endX1, endX2, endX3, endF))

    def dump(name, ap, shape, dt=F32, reads=()):
        if not dbg:
            return
        d = nc.dram_tensor("dbg_" + name, shape, dt, kind="ExternalOutput").ap()
        dbg_d[name] = d
        sch.add("sp", lambda e, d=d, ap=ap: e.dma_start(out=d, in_=ap), r=reads, w=(), dma=("dbg_" + name, 1))

    def ld(dst, src, slot, wn, eng="sp"):
        sch.add(eng, lambda e: e.dma_start(out=dst, in_=src), r=(), w=(wn,), dma=("L" + wn, 1))

    ld(ident, dr["ident"], "c0", "c:ident"); ld(cossin, dr["cossin"], "c0", "c:cossin")
    ld(cmpmask, dr["cmpmask"], "c0", "c:cmpmask"); ld(tri, dr["tri"], "c0", "c:tri"); ld(fb, dr["fb"], "c0", "c:fb")
    ld(ov, dr["ov"], "c0", "c:ov"); ld(bsel[0:48], dr["bsel"], "c0", "c:bsel"); ld(poolcorr, dr["poolcorr"], "c0", "c:poolcorr")
    ld(nrm, dr["nrm"], "c0", "c:nrm"); ld(pool_scale, dr["pool_scale"], "c0", "c:pool_scale"); ld(qkw, dr["qkw"], "c0", "c:qkw")
    ld(conv, dr["conv"], "c0", "c:conv"); ld(b1, dr["cmp_b1"], "c0", "c:b1")
    ld(w2, dr["cmp_w2"], "c1", "c:w2", "pool"); ld(pool_w, dr["pool_w"], "c1", "c:pool_w", "pool")
    ld(posT, dr["cmp_posT"], "c1", "c:posT", "pool")
    sch.add("dve", lambda e: e.memset(ones, 1.0), w=("c:ones",))
    sch.add("dve", lambda e: e.memset(mhalf, -0.5), w=("c:mhalf",))
    sch.add("dve", lambda e: e.memset(KCMP, 0.0), w=("m:KCMP",))
    sch.add("dve", lambda e: e.memset(VCMP, 0.0), w=("m:VCMP",))
    sch.add("dve", lambda e: e.memset(VCMP[0:127, :, 64:128], 1.0), w=("m:VCMP",))

    scr = {}

    def pieces(a_, b_):
        out = []
        for k in range(a_.shape[1]):
            if len(a_.shape) == 3 and a_.shape[2] > 2816:
                w_ = a_.shape[2] // 4
                for i in range(4):
                    out.append((a_[:, k, i * w_:(i + 1) * w_], b_[:, k, i * w_:(i + 1) * w_]))
            else:
                out.append((a_[:, k], b_[:, k]))
        return out

    def wload(dst, srcap, wn, slot, nk, s=0, key=None, cast_fn=None):
        key = key or slot
        if key not in scr:
            scr[key] = nc.dram_tensor("wscr_" + key, list(dst.shape), BF16, kind="Internal").ap()
        sc = scr[key]
        if s == 0:
            if cast_fn is None:
                sch.add("pool", lambda e: [e.dma_start(out=dst[:, k, :], in_=srcap[:, k, :]) for k in range(nk)], w=(wn,), dma=(slot, nk))
            else:
                cast_fn()
            sch.add("sp", lambda e: [e.dma_start(out=a_, in_=b_) for a_, b_ in pieces(sc, dst)], r=(wn,), w=("scr:" + key,), dma=("st_" + key, len(pieces(sc, dst))))
        else:
            sch.add("sp", lambda e: [e.dma_start(out=a_, in_=b_) for a_, b_ in pieces(dst, sc)], r=("scr:" + key,), w=(wn,), dma=("ld_" + key, len(pieces(dst, sc))))

    def mm(e, out, lhsT, rhs, start, stop):
        return e.matmul(out, lhsT=lhsT, rhs=rhs, start=start, stop=stop)

    def norm_A(src_tile, src_name, hn_t, hn_name, stat_col):
        sc = slice(stat_col, stat_col + 1)
        k_ = str(stat_col)
        sch.add("act", lambda e: e.activation(out=hn_t, in_=src_tile, func=AF.Square, accum_out=ssq[:, sc]),
                r=(src_name,), w=(hn_name, "t:ssq" + k_))
        sch.add("dve", lambda e: e.tensor_scalar(out=rs[:, sc], in0=ssq[:, sc], scalar1=1.0 / D, scalar2=EPS,
                                                 op0=ALU.mult, op1=ALU.add), r=("t:ssq" + k_,), w=("t:rs" + k_,))
        sch.add("pool", lambda e: e.tensor_tensor(out=rstd[:, sc], in0=rs[:, sc], in1=mhalf[:, sc], op=ALU.pow), r=("t:rs" + k_, "c:mhalf"), w=("t:rstd" + k_,))

    def norm_B1(src_tile, src_name, hn_t, hn_name, stat_col):
        sc = slice(stat_col, stat_col + 1)
        sch.add("act", lambda e: e.activation(out=hn_t, in_=src_tile, func=AF.Copy, scale=rstd[:, sc]),
                r=(src_name, "t:rstd" + str(stat_col)), w=(hn_name,))

    def norm_B2(hn_t, hn_name, nrm_idx, dst, dst_name, col0):
        btv = BT[:, 0:1024].rearrange("p (a b) -> p a b", a=8, b=128)
        sch.add("pe", lambda e: [e.transpose(btv[:, kc, :], hn_t[:, kc * 128:(kc + 1) * 128], ident) for kc in range(8)],
                r=(hn_name, "c:ident"), w=("BT",))
        sch.add("dve", lambda e: e.tensor_tensor(out=dst[:, :, col0:col0 + 128], in0=btv,
                                                 in1=nrm[:, nrm_idx, :].unsqueeze(2).to_broadcast([128, 8, 128]),
                                                 op=ALU.mult), r=("BT", "c:nrm"), w=(dst_name,))

    def norm_tile(src_tile, src_name, hn_t, hn_name, nrm_idx, dst, dst_name, col0, stat_col):
        norm_A(src_tile, src_name, hn_t, hn_name, stat_col)
        norm_B1(src_tile, src_name, hn_t, hn_name, stat_col)
        norm_B2(hn_t, hn_name, nrm_idx, dst, dst_name, col0)

    def rope_A(Z, zname, nh, widx_ap, eps_eff):
        n = nh * 64
        sqv = SQ[:, 0:n].rearrange("p (h d) -> p h d", h=nh, d=64)
        qwv = QW[:, 0:n].rearrange("p (h d) -> p h d", h=nh, d=64)
        sch.add("dve", lambda e: e.tensor_tensor(out=sqv, in0=Z, in1=Z, op=ALU.mult), r=(zname,), w=("t:SQ",))
        sch.add("dve", lambda e: e.tensor_reduce(out=ssq[:, 0:nh], in_=sqv, axis=AX.X, op=ALU.add), r=("t:SQ",), w=("t:ssq",))
        sch.add("dve", lambda e: e.tensor_scalar(out=rs[:, 0:nh], in0=ssq[:, 0:nh], scalar1=eps_eff[0], scalar2=eps_eff[1],
                                                 op0=ALU.mult, op1=ALU.add), r=("t:ssq",), w=("t:rs",))
        sch.add("pool", lambda e: e.tensor_tensor(out=rstd[:, 0:nh], in0=rs[:, 0:nh], in1=mhalf[:, 0:nh], op=ALU.pow), r=("t:rs", "c:mhalf"), w=("t:rstd",))
        sch.add("dve", lambda e: e.tensor_tensor(out=qwv, in0=Z, in1=widx_ap, op=ALU.mult), r=(zname, "c:qkw"), w=("t:QW",))

    def rope_B(nh, tt, out_bf, out_name):
        n = nh * 64
        qwv = QW[:, 0:n].rearrange("p (h d) -> p h d", h=nh, d=64)
        h = nh * 32
        cosb = cossin[:, 0, tt, :].unsqueeze(1).to_broadcast([128, nh, 32])
        sinb = cossin[:, 1, tt, :].unsqueeze(1).to_broadcast([128, nh, 32])
        v3 = lambda T: T[:, 0:h].rearrange("p (h d) -> p h d", h=nh, d=32)
        q1 = qwv[:, :, 0:32]
        q2 = qwv[:, :, 32:64]
        sch.add("pool", lambda e: e.tensor_tensor(out=v3(TA), in0=q1, in1=cosb, op=ALU.mult), r=("t:QW", "c:cossin"), w=("t:TA",))
        sch.add("pool", lambda e: e.tensor_tensor(out=v3(TB), in0=q2, in1=sinb, op=ALU.mult), r=("t:QW", "c:cossin"), w=("t:TB",))
        sch.add("pool", lambda e: e.tensor_tensor(out=v3(TC), in0=q2, in1=cosb, op=ALU.mult), r=("t:QW", "c:cossin"), w=("t:TC",))
        sch.add("pool", lambda e: e.tensor_tensor(out=v3(TD), in0=q1, in1=sinb, op=ALU.mult), r=("t:QW", "c:cossin"), w=("t:TD",))
        sch.add("dve", lambda e: e.tensor_tensor(out=v3(TA), in0=v3(TA), in1=v3(TB), op=ALU.subtract), r=("t:TA", "t:TB"), w=("t:TA",))
        sch.add("dve", lambda e: e.tensor_tensor(out=v3(TC), in0=v3(TC), in1=v3(TD), op=ALU.add), r=("t:TC", "t:TD"), w=("t:TC",))
        rb = rstd[:, 0:nh].unsqueeze(2).to_broadcast([128, nh, 32])
        sch.add("dve", lambda e: e.tensor_tensor(out=out_bf[:, :, 0:32], in0=v3(TA), in1=rb, op=ALU.mult), r=("t:TA", "t:rstd"), w=(out_name,))
        sch.add("dve", lambda e: e.tensor_tensor(out=out_bf[:, :, 32:64], in0=v3(TC), in1=rb, op=ALU.mult), r=("t:TC", "t:rstd"), w=(out_name,))


    UPFX = ("u1:", "u2:", "t:", "x1:QA")

    def do_seq(s):
        for tt in range(16):
            xt = xs[tt % 2]
            xn = "u1:xs%d" % (tt % 2)
            sch.add("sp", lambda e, xt=xt, tt=tt: e.dma_start(out=xt, in_=x_d[s, tt * 128:(tt + 1) * 128, :]),
                    w=(xn,), dma=("xs%d" % (tt % 2), 1))
            norm_tile(xt, xn, hn[tt % 2], "u1:hn%d" % (tt % 2), 0, hT, "m:hT%d" % (tt // 4), tt * 128, 0)
        if s == 0:
            dump("hT", hT, [128, 8, S], BF16, reads=["m:hT%d" % i for i in range(4)])
        if stop_after <= 1:
            return
        sch.barrier(UPFX)
        sch.add("sp", lambda e: [e.dma_start(out=KAs[64:96, g, :], in_=dr["esel"]) for g in range(2)], w=("x1:KAs_e",), dma=("esel", 2))
        sch.add("pool", lambda e: e.memset(VA[:, :, :, :, 64:128], 1.0), w=("x1:VA1",))
        WA = Wsm[:, :, 0:816]
        wload(WA, dr["w_in"][:, :, 1536:2352], "x1:Wsm", "wsm", 8, s, "wa")
        ZK = ZQ[:, 0:768]
        zk5 = ZK.rearrange("p (b k g d) -> p b k g d", b=3, k=2, g=2, d=64)
        def p2_tile(tt):
            c = tt // 4
            KZ = KZb.rearrange("p (b g d) -> p b g d", b=3, g=2, d=64)

            def s_mm():
                def kvmm(e, tt=tt):
                    ins = []
                    for kc in range(8):
                        ins.append(mm(e, PB[0][:, 0:512], hT[:, kc, tt * 128:(tt + 1) * 128], WA[:, kc, 0:512], kc == 0, kc == 7))
                    for kc in range(8):
                        ins.append(mm(e, PB[1][:, 0:256], hT[:, kc, tt * 128:(tt + 1) * 128], WA[:, kc, 512:768], kc == 0, kc == 7))
                    return ins
                sch.add("pe", kvmm, r=("m:hT%d" % c, "x1:Wsm"), w=("B0", "B1"))

            def s_copy():
                sch.add("act", lambda e: e.activation(out=ZK[:, 0:512], in_=PB[0][:, 0:512], func=AF.Copy), r=("B0",), w=("t:ZK",))
                sch.add("act", lambda e: e.activation(out=ZK[:, 512:768], in_=PB[1][:, 0:256], func=AF.Copy), r=("B1",), w=("t:ZK",))

            def s_A():
                sch.add("pool", lambda e, tt=tt: e.tensor_copy(out=VA[:, :, :, tt, 0:64], in_=zk5[:, 1:3, 1, :, :]), r=("t:ZK",), w=("x1:VA",))
                sch.add("pool", lambda e: e.tensor_copy(out=KZ, in_=zk5[:, :, 0, :, :]), r=("t:ZK",), w=("t:KZ",))

            def s_B():
                KZ3 = KZb.rearrange("p (h d) -> p h d", h=6, d=64)
                kw_ap = qkw[:, 1:4, :].unsqueeze(2).to_broadcast([128, 3, 2, 64])
                KN = QN[:, 0:384].rearrange("p (h d) -> p h d", h=6, d=64)
                n = 384
                sqv = SQ[:, 0:n].rearrange("p (h d) -> p h d", h=6, d=64)
                qwv = QW[:, 0:n].rearrange("p (h d) -> p h d", h=6, d=64)
                qwv4 = QW[:, 0:n].rearrange("p (b g d) -> p b g d", b=3, g=2, d=64)
                sch.add("dve", lambda e: e.tensor_tensor(out=sqv, in0=KZ3, in1=KZ3, op=ALU.mult), r=("t:KZ",), w=("t:SQ",))
                sch.add("dve", lambda e: e.tensor_reduce(out=ssq[:, 0:6], in_=sqv, axis=AX.X, op=ALU.add), r=("t:SQ",), w=("t:ssq",))
                sch.add("dve", lambda e: e.tensor_scalar(out=rs[:, 0:6], in0=ssq[:, 0:6], scalar1=1.0 / 64, scalar2=EPS,
                                                         op0=ALU.mult, op1=ALU.add), r=("t:ssq",), w=("t:rs",))
                sch.add("pool", lambda e: e.tensor_tensor(out=rstd[:, 0:6], in0=rs[:, 0:6], in1=mhalf[:, 0:6], op=ALU.pow), r=("t:rs", "c:mhalf"), w=("t:rstd",))
                sch.add("dve", lambda e: e.tensor_tensor(out=qwv4, in0=KZ, in1=kw_ap, op=ALU.mult), r=("t:KZ", "c:qkw"), w=("t:QW",))
                nh = 6
                h_ = nh * 32
                cosb = cossin[:, 0, tt, :].unsqueeze(1).to_broadcast([128, nh, 32])
                sinb = cossin[:, 1, tt, :].unsqueeze(1).to_broadcast([128, nh, 32])
                v3 = lambda T: T[:, 0:h_].rearrange("p (h d) -> p h d", h=nh, d=32)
                q1 = qwv[:, :, 0:32]
                q2 = qwv[:, :, 32:64]
                sch.add("pool", lambda e, cosb=cosb, q1=q1: e.tensor_tensor(out=v3(TA), in0=q1, in1=cosb, op=ALU.mult), r=("t:QW", "c:cossin"), w=("t:TA",))
                sch.add("pool", lambda e, sinb=sinb, q2=q2: e.tensor_tensor(out=v3(TB), in0=q2, in1=sinb, op=ALU.mult), r=("t:QW", "c:cossin"), w=("t:TB",))
                sch.add("pool", lambda e, cosb=cosb, q2=q2: e.tensor_tensor(out=v3(TC), in0=q2, in1=cosb, op=ALU.mult), r=("t:QW", "c:cossin"), w=("t:TC",))
                sch.add("pool", lambda e, sinb=sinb, q1=q1: e.tensor_tensor(out=v3(TD), in0=q1, in1=sinb, op=ALU.mult), r=("t:QW", "c:cossin"), w=("t:TD",))
                sch.add("dve", lambda e: e.tensor_tensor(out=v3(TA), in0=v3(TA), in1=v3(TB), op=ALU.subtract), r=("t:TA", "t:TB"), w=("t:TA",))
                sch.add("dve", lambda e: e.tensor_tensor(out=v3(TC), in0=v3(TC), in1=v3(TD), op=ALU.add), r=("t:TC", "t:TD"), w=("t:TC",))
                rb = rstd[:, 0:nh].unsqueeze(2).to_broadcast([128, nh, 32])
                sch.add("dve", lambda e, rb=rb: e.tensor_tensor(out=KN[:, :, 0:32], in0=v3(TA), in1=rb, op=ALU.mult), r=("t:TA", "t:rstd"), w=("t:KN",))
                sch.add("dve", lambda e, rb=rb: e.tensor_tensor(out=KN[:, :, 32:64], in0=v3(TC), in1=rb, op=ALU.mult), r=("t:TC", "t:rstd"), w=("t:KN",))
                btv = BT[:, 0:384].rearrange("p (a b) -> p a b", a=3, b=128)
                sch.add("pe", lambda e: [e.transpose(btv[:, b, :], QN[:, b * 128:(b + 1) * 128], ident) for b in range(3)],
                        r=("t:KN", "c:ident"), w=("BT",))
                cs = slice(tt * 128, (tt + 1) * 128)
                sch.add("dve", lambda e, cs=cs: e.tensor_copy(out=KC[:, cs], in_=btv[:, 0, :]), r=("BT",), w=("u2:KC",))
                sch.add("dve", lambda e, cs=cs: e.tensor_copy(out=KAs[0:64, 0, cs], in_=btv[0:64, 1, :]), r=("BT",), w=("x1:KAs",))
                sch.add("dve", lambda e, cs=cs: e.tensor_copy(out=KAs[0:64, 1, cs], in_=btv[64:128, 1, :]), r=("BT",), w=("x1:KAs",))
                sch.add("dve", lambda e, cs=cs: e.tensor_copy(out=KAw[0:64, 0, cs], in_=btv[0:64, 2, :]), r=("BT",), w=("x1:KAw",))
                sch.add("dve", lambda e, cs=cs: e.tensor_copy(out=KAw[0:64, 1, cs], in_=btv[64:128, 2, :]), r=("BT",), w=("x1:KAw",))
            return s_mm, s_copy, s_A, s_B
        pt_ = [p2_tile(tt) for tt in range(16)]
        pt_[0][0](); pt_[0][1]()
        for tt in range(16):
            if tt + 1 < 16:
                pt_[tt + 1][0]()
            pt_[tt][2]()
            if tt + 1 < 16:
                pt_[tt + 1][1]()
            pt_[tt][3]()
        for c in range(4):
            cc = slice(c * 512, (c + 1) * 512)
            sch.add("pe", lambda e, cc=cc: [mm(e, PB[2][:, :], WA[:, kc, 128:256], hT[:, kc, cc], kc == 0, kc == 7) for kc in range(8)],
                    r=("m:hT%d" % c, "x1:Wsm"), w=("B2",))
            sch.add("act", lambda e, cc=cc: e.activation(out=VC[:, cc], in_=PB[2][:, :], func=AF.Copy), r=("B2",), w=("u2:VC",))
            sch.add("pe", lambda e, cc=cc: [mm(e, PB[3][0:48, :], WA[:, kc, 768:816], hT[:, kc, cc], kc == 0, kc == 7) for kc in range(8)],
                    r=("m:hT%d" % c, "x1:Wsm"), w=("B3",))
            sch.add("act", lambda e, cc=cc: e.activation(out=G[0:48, cc], in_=PB[3][0:48, :], func=AF.Sigmoid), r=("B3",), w=("x1:G",))
        if s == 0:
            dump("KAs", KAs, [128, 2, S], BF16, reads=("x1:KAs", "x1:KAs_e"))
            dump("KAw", KAw, [128, 2, S], BF16, reads=("x1:KAw",))
            dump("KC", KC, [128, S], BF16, reads=("u2:KC",))
            dump("VC", VC, [128, S], BF16, reads=("u2:VC",))
            dump("VA", VA.rearrange("p a b c d -> p (a b c d)"), [128, 2 * 2 * 16 * 128], BF16, reads=("x1:VA", "x1:VA1"))
            dump("G", G[0:48], [48, S], BF16, reads=("x1:G",))
        if stop_after <= 2:
            return
        wload(w1, None, "x1:Wsm", "wsm", 8, s, "w1", cast_fn=lambda: sch.add("pool", lambda e: [e.dma_start(out=w1[:, kv, 8 * i:8 * (i + 1), :], in_=dr["cmp_w1"][:, kv, 8 * i:8 * (i + 1), :]) for kv in range(2) for i in range(4)], w=("x1:Wsm",), dma=("wsm", 8)))
        if s == 0:
            def chm(e):
                ins = []
                for kv in range(2):
                    for l in range(32):
                        ins.append(mm(e, PB[6][:, kv:kv + 1], w1[0:64, kv, l, :], posT[0:64, kv, l:l + 1], l == 0, l == 31))
                return ins
            sch.add("pe", chm, r=("x1:Wsm", "c:posT"), w=("B6",))
            sch.add("dve", lambda e: e.tensor_tensor(out=chid, in0=PB[6][:, 0:2], in1=b1, op=ALU.add), r=("B6", "c:b1"), w=("c:chid",))
        for kv in range(2):
            for g in range(2):
                src = KC if kv == 0 else VC
                srcn = "u2:KC" if kv == 0 else "u2:VC"
                pr = slice(g * 64, (g + 1) * 64)

                def cm(e, src=src, pr=pr, kv=kv):
                    return [mm(e, PB[4][:, 0:127], w1[pr, kv, l, :], src[pr, l:l + 16 * 126 + 1:16], l == 0, l == 31) for l in range(32)]
                sch.add("pe", cm, r=(srcn, "x1:Wsm"), w=("B4",))
                xg = XG[:, 0:127]; x2 = X2[:, 0:127]; x3 = X3[:, 0:127]
                sch.add("act", lambda e, kv=kv: e.activation(out=xg, in_=PB[4][:, 0:127], func=AF.Identity, bias=chid[:, kv:kv + 1]),
                        r=("B4", "c:chid"), w=("u2:XG",))
                sch.add("dve", lambda e: e.tensor_tensor(out=x2, in0=xg, in1=xg, op=ALU.mult), r=("u2:XG",), w=("u2:X2",))
                sch.add("dve", lambda e: e.tensor_tensor(out=x3, in0=x2, in1=xg, op=ALU.mult), r=("u2:X2", "u2:XG"), w=("u2:X3",))
                sch.add("dve", lambda e: e.scalar_tensor_tensor(out=x2, in0=x3, scalar=0.044715, in1=xg, op0=ALU.mult, op1=ALU.add),
                        r=("u2:X3", "u2:XG"), w=("u2:X2",))
                sch.add("act", lambda e: e.activation(out=x3, in_=x2, func=AF.Sigmoid, scale=1.5957691216057308), r=("u2:X2",), w=("u2:X3",))
                sch.add("dve", lambda e: e.tensor_tensor(out=HID[:, 0:127], in0=xg, in1=x3, op=ALU.mult), r=("u2:XG", "u2:X3"), w=("u2:HID",))
                if kv == 0:
                    sch.add("pe", lambda e: mm(e, PB[5][0:64, 0:127], w2[:, 0, :], HID[:, 0:127], True, True), r=("u2:HID", "c:w2"), w=("B5",))
                    sch.add("act", lambda e, g=g: e.activation(out=KCMP[0:64, g, 0:127], in_=PB[5][0:64, 0:127], func=AF.Copy), r=("B5",), w=("m:KCMP",))
                else:
                    sch.add("pe", lambda e: mm(e, PB[5][0:127, 0:64], HID[:, 0:127], w2[:, 1, :], True, True), r=("u2:HID", "c:w2"), w=("B5",))
                    sch.add("act", lambda e, g=g: e.activation(out=VCMP[0:127, g, 0:64], in_=PB[5][0:127, 0:64], func=AF.Copy), r=("B5",), w=("m:VCMP",))
        if s == 0:
            dump("KCMP", KCMP, [128, 2, 128], BF16, reads=("m:KCMP",))
            dump("VCMP", VCMP, [128, 2, 128], BF16, reads=("m:VCMP",))
        if stop_after <= 3:
            return
        sch.barrier(UPFX)
        WQ = Wsm
        wload(WQ, dr["w_in"][:, :, 512:1536], "x1:Wsm", "wsm", 8, s, "wq")
        def attn_chunk(c):
            cc = slice(c * 512, (c + 1) * 512)
            def q_tile(tl):
                tt = 4 * c + tl
                ts_ = slice(tt * 128, (tt + 1) * 128)
                Z3 = ZQ.rearrange("p (h d) -> p h d", h=16, d=64)
                QN3 = QN.rearrange("p (h d) -> p h d", h=16, d=64)
                wq_ap = qkw[:, 0, :].unsqueeze(1).to_broadcast([128, 16, 64])
                btv = BT[:, 0:1024].rearrange("p (a b) -> p a b", a=8, b=128)
                QAe = QA.rearrange("p (i two) q -> p two i q", two=2)
                ls = slice(tl * 128, (tl + 1) * 128)

                def s_mm():
                    def qmm(e):
                        ins = []
                        for hf in range(2):
                            for kc in range(8):
                                ins.append(mm(e, PB[5 + hf][:, :], hT[:, kc, ts_], WQ[:, kc, hf * 512:(hf + 1) * 512], kc == 0, kc == 7))
                        return ins
                    sch.add("pe", qmm, r=("m:hT%d" % c, "x1:Wsm"), w=("B5", "B6"))

                def s_copy():
                    sch.add("act", lambda e: e.activation(out=ZQ[:, 0:512], in_=PB[5][:, :], func=AF.Copy), r=("B5",), w=("t:ZQ",))
                    sch.add("act", lambda e: e.activation(out=ZQ[:, 512:1024], in_=PB[6][:, :], func=AF.Copy), r=("B6",), w=("t:ZQ",))

                def s_A():
                    rope_A(Z3, "t:ZQ", 16, wq_ap, (1.0, 64 * EPS))

                def s_B():
                    rope_B(16, tt, QN3, "t:QN")
                    sch.add("pe", lambda e: [e.transpose(btv[:, i, :], QN[:, i * 128:(i + 1) * 128], ident) for i in range(8)],
                            r=("t:QN", "c:ident"), w=("BT",))
                    sch.add("dve", lambda e: e.tensor_copy(out=QAe[0:64, 0, :, ls], in_=btv[0:64, :, :]), r=("BT",), w=("x1:QA",))
                    sch.add("dve", lambda e: e.tensor_copy(out=QAe[0:64, 1, :, ls], in_=btv[64:128, :, :]), r=("BT",), w=("x1:QA",))
                return s_mm, s_copy, s_A, s_B
            qt = [q_tile(tl) for tl in range(4)]
            qt[0][0](); qt[0][1]()
            for tl in range(4):
                if tl + 1 < 4:
                    qt[tl + 1][0]()
                qt[tl][2]()
                if tl + 1 < 4:
                    qt[tl + 1][1]()
                qt[tl][3]()
            if s == 0 and c == 1:
                dump("QA", QA.rearrange("p a b -> p (a b)"), [128, 16 * 512], BF16, reads=("x1:QA",))
            if stop_after <= 4:
                return
            SB_ = [0, 1, 2]
            OB_ = [3, 4]
            for g in range(2):
                def stA(hl, g=g):
                    h = g * 8 + hl
                    sb = PB[SB_[h % 3]]; sbn = "B%d" % SB_[h % 3]
                    pt = PT[h % 3]; ptn = "t:PT%d" % (h % 3)
                    sch.add("pe", lambda e: [mm(e, sb[:, :], KCMP[0:64, g, :], QA[0:64, h, :], True, False),
                                             mm(e, sb[:, :], ident, cmpmask[:, cc], False, True)],
                            r=("m:KCMP", "x1:QA", "c:ident", "c:cmpmask"), w=(sbn,))
                    sch.add("act", lambda e: e.activation(out=pt, in_=sb[:, :], func=AF.Exp), r=(sbn,), w=(ptn,))

                def stB(hl, g=g):
                    h = g * 8 + hl
                    pt = PT[h % 3]; ptn = "t:PT%d" % (h % 3)
                    ecn = ECN[h % 2]; ecnn = "t:ECN%d" % (h % 2)
                    sch.add("pe", lambda e: mm(e, PB[5][:, :], ones, pt, True, True), r=(ptn, "c:ones"), w=("B5",))
                    sch.add("act", lambda e: e.activation(out=RR, in_=PB[5][:, :], func=AF.Ln, bias=1e-18), r=("B5",), w=("t:RR",))
                    sch.add("act", lambda e: e.activation(out=R2, in_=RR, func=AF.Exp, scale=-1.0), r=("t:RR",), w=("t:R2",))
                    sch.add("dve", lambda e: e.tensor_tensor(out=ecn, in0=pt, in1=R2, op=ALU.mult), r=(ptn, "t:R2"), w=(ecnn,))
                    if hl == 0:
                        sch.add("pool", lambda e: e.tensor_copy(out=PSC, in_=ecn), r=(ecnn,), w=("t:PSC",))
                    else:
                        sch.add("pool", lambda e: e.tensor_tensor(out=PSC, in0=PSC, in1=ecn, op=ALU.add), r=(ecnn, "t:PSC"), w=("t:PSC",))

                def stC(hl, g=g):
                    h = g * 8 + hl
                    ob = PB[OB_[h % 2]]; obn = "B%d" % OB_[h % 2]
                    ecn = ECN[h % 2]; ecnn = "t:ECN%d" % (h % 2)
                    sch.add("pe", lambda e: mm(e, ob[0:64, :], VCMP[:, g, 0:64], ecn, True, True), r=(ecnn, "m:VCMP"), w=(obn,))
                    ci = 3 * h + 0
                    sch.add("pe", lambda e: mm(e, PB[6][0:64, :], bsel[0:48, ci * 64:(ci + 1) * 64], G[0:48, cc], True, True),
                            r=("x1:G", "c:bsel"), w=("B6",))
                    sch.add("act", lambda e: e.activation(out=GSB[0:64, :], in_=PB[6][0:64, :], func=AF.Copy), r=("B6",), w=("t:GSB",))
                    pr = slice((h % 2) * 64, (h % 2) * 64 + 64)
                    sch.add("dve", lambda e: e.tensor_tensor(out=OT[pr, h // 2, cc], in0=ob[0:64, :], in1=GSB[0:64, :], op=ALU.mult),
                            r=(obn, "t:GSB"), w=("m:OT%d_%d" % (c, h),))
                for k in range(8 + 2):
                    if k < 8:
                        stA(k)
                    if 0 <= k - 1 < 8:
                        stB(k - 1)
                    if 0 <= k - 2 < 8:
                        stC(k - 2)
                sch.add("pool", lambda e: e.tensor_copy(out=PSCb, in_=PSC), r=("t:PSC",), w=("t:PSCb",))
                impv = PB[5][:, 0:128].rearrange("p (a b) -> p a b", a=4, b=32)
                sch.add("pe", lambda e: [mm(e, impv[:, qb, :], PSCb[:, qb * 128:(qb + 1) * 128], ov, True, True) for qb in range(4)],
                        r=("t:PSCb", "c:ov"), w=("B5",))
                sch.add("dve", lambda e, c=c: e.tensor_tensor(out=SC, in0=impv, in1=fb[:, 4 * c:4 * c + 4, :], op=ALU.add), r=("B5", "c:fb"), w=("t:SC",))
                for qb in range(4):
                    sch.add("dve", lambda e, qb=qb: e.max(out=M8[:, qb, :], in_=SC[:, qb, :]), r=("t:SC",), w=("t:M8",))
                    sch.add("dve", lambda e, qb=qb: e.tensor_scalar(out=SBf[:, qb, :], in0=SC[:, qb, :], scalar1=M8[:, qb, 7:8], scalar2=1.0,
                                                                    op0=ALU.is_ge, op1=ALU.subtract), r=("t:SC", "t:M8"), w=("t:SBf",))
                sch.add("dve", lambda e: e.tensor_scalar(out=SBb, in0=SBf, scalar1=-NEGB, scalar2=None, op0=ALU.mult), r=("t:SBf",), w=("t:SBb",))
                sch.add("pe", lambda e: [e.transpose(BT[0:32, qb * 128:(qb + 1) * 128], SBb[:, qb, :], ident) for qb in range(4)],
                        r=("t:SBb", "c:ident"), w=("BT",))
                sch.add("dve", lambda e, g=g: e.tensor_copy(out=QA[64:96, g * 8:(g + 1) * 8, :],
                                                            in_=BT[0:32, 0:512].unsqueeze(1).to_broadcast([32, 8, 512])),
                        r=("BT",), w=("x1:QAs",))
                if s == 0 and c == 1 and g == 0:
                    dump("SC", SC.rearrange("p a b -> p (a b)"), [128, 128], F32, reads=("t:SC",))
                    dump("SBf", SBf.rearrange("p a b -> p (a b)"), [128, 128], F32, reads=("t:SBf",))
            if stop_after <= 5:
                return
            jobs = []
            for h in range(16):
                g = h // 8
                tl_ = []
                for kt in range(0, 4 * c + 4):
                    j = kt - 4 * c
                    lo = 0 if j < 0 else 128 * j
                    tl_.append(dict(kt=kt, lo=lo, hi=512, mask=(None if j < 0 else (0, lo))))
                jobs.append(dict(h=h, g=g, br=1, tiles=tl_))
                tl_ = []
                order = [4] + ([0, 1, 2, 3] if c >= 1 else []) + [5, 6, 7]
                for i in order:
                    kt = 4 * c - 4 + i
                    if i <= 3:
                        tl_.append(dict(kt=kt, lo=0, hi=128 * (i + 1), mask=(1, 128 * i)))
                    else:
                        tl_.append(dict(kt=kt, lo=128 * (i - 4), hi=512, mask=(0, 128 * (i - 4))))
                jobs.append(dict(h=h, g=g, br=2, tiles=tl_))
            flat = []
            for jb_i, jb in enumerate(jobs):
                for ti, t in enumerate(jb["tiles"]):
                    flat.append((jb_i, ti))
            srot = [0]

            def rec_S(k):
                jb_i, ti = flat[k]
                jb = jobs[jb_i]; t = jb["tiles"][ti]
                si = k % 3
                sb = PB[SB_[si]]; t["si"] = si
                lo, hi, kt, g, h = t["lo"], t["hi"], t["kt"], jb["g"], jb["h"]
                ks = slice(kt * 128, (kt + 1) * 128)
                if jb["br"] == 1:
                    lhs = KAs[0:96, g, ks]; rhs = QA[0:96, h, lo:hi]; rn = ("x1:KAs", "x1:KAs_e", "x1:QA", "x1:QAs")
                else:
                    lhs = KAw[0:64, g, ks]; rhs = QA[0:64, h, lo:hi]; rn = ("x1:KAw", "x1:QA")
                mk = t["mask"]

                def f(e):
                    ins = [mm(e, sb[:, lo:hi], lhs, rhs, True, mk is None)]
                    if mk is not None:
                        ins.append(mm(e, sb[:, mk[1]:mk[1] + 128], ident, tri[:, mk[0] * 128:(mk[0] + 1) * 128], False, True))
                    return ins
                sch.add("pe", f, r=rn + ("c:ident", "c:tri"), w=("B%d" % SB_[si],))
                pt = PT[si]
                sch.add("act", lambda e: e.activation(out=pt[:, lo:hi], in_=sb[:, lo:hi], func=AF.Exp), r=("B%d" % SB_[si],), w=("t:PT%d" % si,))

            def rec_PV(k):
                jb_i, ti = flat[k]
                jb = jobs[jb_i]; t = jb["tiles"][ti]
                si = t["si"]
                oi = jb_i % 2
                ob = PB[OB_[oi]]; obn = "B%d" % OB_[oi]
                lo, hi, kt, g, h, br = t["lo"], t["hi"], t["kt"], jb["g"], jb["h"], jb["br"]
                pt = PT[si]
                first = ti == 0
                last = ti == len(jb["tiles"]) - 1
                sch.add("pe", lambda e: mm(e, ob[:, lo:hi], VA[:, br - 1, g, kt, :], pt[:, lo:hi], first, last),
                        r=("t:PT%d" % si, "x1:VA", "x1:VA1"), w=(obn,))
                if last:
                    ci = 3 * h + br
                    sch.add("pe", lambda e: mm(e, PB[6][0:64, :], bsel[0:48, ci * 64:(ci + 1) * 64], G[0:48, cc], True, True),
                            r=("x1:G", "c:bsel"), w=("B6",))
                    sch.add("dve", lambda e: e.reciprocal(out=RCPb[0:64, :], in_=ob[64:128, :]), r=(obn,), w=("t:RCP",))
                    sch.add("dve", lambda e: e.tensor_tensor(out=R2[0:64, :], in0=RCPb[0:64, :], in1=PB[6][0:64, :], op=ALU.mult), r=("t:RCP", "B6"), w=("t:R2",))
                    pr = slice((h % 2) * 64, (h % 2) * 64 + 64)
                    if br == 1:
                        sch.add("dve", lambda e: e.tensor_tensor(out=U1[pr, :], in0=ob[0:64, :], in1=R2[0:64, :], op=ALU.mult), r=(obn, "t:R2"), w=("t:U1",))
                    else:
                        sch.add("dve", lambda e: e.tensor_tensor(out=U2[pr, :], in0=ob[0:64, :], in1=R2[0:64, :], op=ALU.mult), r=(obn, "t:R2"), w=("t:U2",))
                        otn = "m:OT%d_%d" % (c, h)
                        sch.add("pool", lambda e: e.tensor_tensor(out=UT[pr, :], in0=U1[pr, :], in1=U2[pr, :], op=ALU.add), r=("t:U1", "t:U2"), w=("t:UT",))
                        sch.add("pool", lambda e: e.tensor_tensor(out=OT[pr, h // 2, cc], in0=OT[pr, h // 2, cc], in1=UT[pr, :], op=ALU.add),
                                r=("t:UT", otn), w=(otn,))
            DEPTH = 2
            for k in range(len(flat) + DEPTH):
                if k < len(flat):
                    rec_S(k)
                if k - DEPTH >= 0:
                    rec_PV(k - DEPTH)
        for c_i in range(4):
            attn_chunk(c_i)
        if s == 0:
            dump("OT", OT, [128, 8, S], BF16, reads=["m:OT%d_%d" % (c_, h_) for c_ in range(4) for h_ in range(16)])
        if stop_after <= 6:
            return
        sch.barrier(("x1:", "x2:", "t:", "u1:", "u2:"))
        wload(Wp, dr["w_in"][:, :, 0:512], "x2:Wp", "wsm", 8, s, "wp")
        sch.add("dve", lambda e: e.memset(UF[:, 0:16], 0.0), w=("x2:UF",))
        sch.add("dve", lambda e: e.memset(SA[:, 0:16], 0.0), w=("x2:SA",))
        sch.add("dve", lambda e: e.memset(SBp[:, 0:16], 0.0), w=("x2:SB",))
        for g in range(4):
            for c in range(4):
                cc = slice(c * 512, (c + 1) * 512)
                bi = nextbank([0, 1, 2, 3])
                sch.add("pe", lambda e, bi=bi, g=g, cc=cc: [mm(e, PB[bi][:, :], Wp[:, kc, g * 128:(g + 1) * 128], hT[:, kc, cc], kc == 0, kc == 7) for kc in range(8)],
                        r=("m:hT%d" % c, "x2:Wp"), w=("B%d" % bi,))
                sch.add("act", lambda e, bi=bi, c=c: e.activation(out=UF[:, 16 + c * 512:16 + (c + 1) * 512], in_=PB[bi][:, :], func=AF.Copy),
                        r=("B%d" % bi,), w=("x2:UF",))
            bufs = [(UF, "x2:UF"), (SA, "x2:SA"), (SBp, "x2:SB")]
            srcb = bufs[0]
            for lvl in range(g + 1):
                sh = 1 << lvl
                dstb = bufs[1 + (lvl % 2)]
                eng = "dve" if lvl % 2 == 0 else "pool"
                sch.add(eng, lambda e, srcb=srcb, dstb=dstb, sh=sh: e.tensor_tensor(out=dstb[0][:, 16:16 + S], in0=srcb[0][:, 16:16 + S],
                                                                                  in1=srcb[0][:, 16 - sh:16 + S - sh], op=ALU.add),
                        r=(srcb[1],), w=(dstb[1],))
                srcb = dstb
            wdw = 2 << g
            sch.add("dve", lambda e, srcb=srcb, g=g: e.tensor_tensor(out=srcb[0][:, 16:32], in0=srcb[0][:, 16:32], in1=poolcorr[:, g, :], op=ALU.mult),
                    r=(srcb[1], "c:poolcorr"), w=(srcb[1],))
            sch.add("dve", lambda e, srcb=srcb, wdw=wdw: e.scalar_tensor_tensor(out=PLb, in0=srcb[0][:, 16:16 + S], scalar=1.0 / wdw, in1=UF[:, 16:16 + S],
                                                                             op0=ALU.mult, op1=ALU.subtract), r=(srcb[1], "x2:UF"), w=("x2:PLb",))
            for c in range(4):
                cc = slice(c * 512, (c + 1) * 512)
                bi = nextbank([4, 5, 6])
                sch.add("pe", lambda e, bi=bi, g=g, cc=cc: mm(e, PB[bi][:, :], pool_w[:, g, :], PLb[:, cc], True, True), r=("x2:PLb", "c:pool_w"), w=("B%d" % bi,))
                sch.add("act", lambda e, bi=bi, g=g, cc=cc: e.activation(out=YP[:, g, cc], in_=PB[bi][:, :], func=AF.Copy, scale=pool_scale[:, g:g + 1]),
                        r=("B%d" % bi, "c:pool_scale"), w=("m:YP",))
        if s == 0:
            dump("YP", YP, [128, 4, S], BF16, reads=("m:YP",))
        if stop_after <= 7:
            return
        sch.barrier(("x2:", "x3:", "t:"))
        wload(Wmg, dr["w_in"][:, :, 2352:4400], "x3:Wmg", "wmg", 8, s)
        wload(Wpb, dr["w_pool_br"], "x3:Wpb", "wpb", 4, s)
        wload(Wab, dr["w_attn_br"], "x3:Wab", "wab", 8, s)
        wload(Wo, dr["w_o"], "x3:Wo", "wo", 8, s)
        for c in range(4):
            cc = slice(c * 512, (c + 1) * 512)
            otn = ["m:OT%d_%d" % (c, h_) for h_ in range(16)]
            for dc in range(8):
                dsl = slice(dc * 128, (dc + 1) * 128)
                b_yp, b_ya, b_m0, b_m1 = [nextbank([0, 1, 2, 3, 4, 5, 6]) for _ in range(4)]
                sch.add("pe", lambda e, b=b_m0, dsl=dsl, cc=cc: [mm(e, PB[b][:, :], Wmg[:, kc, dsl], hT[:, kc, cc], kc == 0, kc == 7) for kc in range(8)],
                        r=("m:hT%d" % c, "x3:Wmg"), w=("B%d" % b_m0,))
                sch.add("pe", lambda e, b=b_m1, dc=dc, cc=cc: [mm(e, PB[b][:, :], Wmg[:, kc, 1024 + dc * 128:1024 + (dc + 1) * 128], hT[:, kc, cc], kc == 0, kc == 7) for kc in range(8)],
                        r=("m:hT%d" % c, "x3:Wmg"), w=("B%d" % b_m1,))
                sch.add("pe", lambda e, b=b_yp, dsl=dsl, cc=cc: [mm(e, PB[b][:, :], Wpb[:, g, dsl], YP[:, g, cc], g == 0, g == 3) for g in range(4)],
                        r=("m:YP", "x3:Wpb"), w=("B%d" % b_yp,))
                sch.add("pe", lambda e, b=b_ya, dsl=dsl, cc=cc: [mm(e, PB[b][:, :], Wab[:, i, dsl], OT[:, i, cc], i == 0, i == 7) for i in range(8)],
                        r=tuple(otn) + ("x3:Wab",), w=("B%d" % b_ya,))
                sch.add("act", lambda e, b=b_m0: e.activation(out=SG[0], in_=PB[b][:, :], func=AF.Sigmoid), r=("B%d" % b_m0,), w=("x3:SG0",))
                sch.add("act", lambda e, b=b_m1: e.activation(out=SG[1], in_=PB[b][:, :], func=AF.Sigmoid), r=("B%d" % b_m1,), w=("x3:SG1",))
                sch.add("dve", lambda e, b=b_yp: e.tensor_tensor(out=T0, in0=SG[0], in1=PB[b][:, :], op=ALU.mult), r=("x3:SG0", "B%d" % b_yp), w=("x3:T0",))
                sch.add("dve", lambda e, b=b_ya: e.tensor_tensor(out=T1, in0=SG[1], in1=PB[b][:, :], op=ALU.mult), r=("x3:SG1", "B%d" % b_ya), w=("x3:T1",))
                sch.add("pool", lambda e, dc=dc: e.tensor_tensor(out=MT[:, dc, :], in0=T0, in1=T1, op=ALU.add), r=("x3:T0", "x3:T1"), w=("x3:MT",))
            for tl in range(4):
                tt = 4 * c + tl
                ls = slice(tl * 128, (tl + 1) * 128)
                xt = xs6[tt % 2]; xn = "x3:xs%d" % (tt % 2)
                sch.add("sp", lambda e, xt=xt, tt=tt: e.dma_start(out=xt, in_=x_d[s, tt * 128:(tt + 1) * 128, :]), w=(xn,), dma=("xs6%d" % (tt % 2), 1))
                b0, b1_ = nextbank([0, 1, 2, 3, 4, 5, 6]), nextbank([0, 1, 2, 3, 4, 5, 6])
                for hf, b in ((0, b0), (1, b1_)):
                    sch.add("pe", lambda e, b=b, hf=hf, ls=ls: [mm(e, PB[b][:, :], MT[:, dc, ls], Wo[:, dc, hf * 512:(hf + 1) * 512], dc == 0, dc == 7) for dc in range(8)],
                            r=("x3:MT", "x3:Wo"), w=("B%d" % b,))
                    sch.add("dve", lambda e, b=b, hf=hf, xt=xt: e.tensor_tensor(out=xt[:, hf * 512:(hf + 1) * 512], in0=xt[:, hf * 512:(hf + 1) * 512], in1=PB[b][:, :], op=ALU.add),
                            r=(xn, "B%d" % b), w=(xn,))
                sch.add("sp", lambda e, xt=xt, tt=tt: e.dma_start(out=(x1_d if stop_after > 8 else out_d)[s, tt * 128:(tt + 1) * 128, :], in_=xt), r=(xn,), w=("o:%d_%d" % (s, tt),), dma=("xo6%d" % (tt % 2), 1))
        if stop_after <= 8:
            return
        sch.barrier(("m:", "x1:", "x2:", "x3:", "f:", "t:", "u1:", "u2:"))
        wload(Wup, None, "f:Wup", "wup", 32, s, "wup", cast_fn=lambda: sch.add("pool", lambda e: [e.dma_start(out=Wup[:, k, 1376 * i:1376 * (i + 1)], in_=dr["w_up"][:, k, 1376 * i:1376 * (i + 1)]) for k in range(8) for i in range(4)], w=("f:Wup",), dma=("wup", 32)))
        wload(Wdn, dr["w_down"], "f:Wdn", "wdn", 22, s)
        sch.add("dve", lambda e: e.memset(HALO, 0.0), w=tuple("f:HALO%d" % i for i in range(22)))
        DB = [0, 1, 2, 3]
        UB = [4, 5, 6]
        def pf(c8n, stage):
            for tl in range(2):
                tt = 2 * c8n + tl
                xi = (c8n % 2) * 2 + tl
                xt = X1S[xi]; xn = "f:x1s%d" % xi
                if stage == 0:
                    sch.add("sp", lambda e, xt=xt, tt=tt: e.dma_start(out=xt, in_=x1_d[s, tt * 128:(tt + 1) * 128, :]), r=("o:%d_%d" % (s, tt),), w=(xn,), dma=("x1s%d" % xi, 1))
                elif stage == 1:
                    norm_A(xt, xn, hn7[tl], "f:hn%d" % tl, 1 + tl)
                elif stage == 2:
                    norm_B1(xt, xn, hn7[tl], "f:hn%d" % tl, 1 + tl)
                else:
                    norm_B2(hn7[tl], "f:hn%d" % tl, 1, H2[c8n % 2], "f:H2_%d" % (c8n % 2), tl * 128)
        for st_ in range(4):
            pf(0, st_)
        for c8 in range(8):
            H2c = H2[c8 % 2]; h2n = "f:H2_%d" % (c8 % 2)

            def rec_up(fc, H2c=H2c, h2n=h2n):
                rows = 128 if fc < 21 else 64
                par = fc % 3
                b = UB[par]; bn = "B%d" % b
                ue = UE[par]; uen = "f:UE%d" % par

                def upmm(e):
                    ins = []
                    for gv in range(2):
                        col0 = gv * D_FF + fc * 128
                        for kc in range(8):
                            ins.append(mm(e, PB[b][0:rows, gv * 256:(gv + 1) * 256], Wup[:, kc, col0:col0 + rows], H2c[:, kc, :], kc == 0, kc == 7))
                    return ins
                sch.add("pe", upmm, r=(h2n, "f:Wup"), w=(bn,))
                hn_ = "f:HALO%d" % fc
                sch.add("pool", lambda e: e.tensor_copy(out=ue[0:rows, :, 0:2], in_=HALO[0:rows, :, fc, :]), r=(hn_,), w=(uen,))
                sch.add("act", lambda e: e.activation(out=ue[0:rows, :, 2:258], in_=PB[b][0:rows, :].rearrange("p (g t) -> p g t", g=2, t=256), func=AF.Copy),
                        r=(bn,), w=(uen,))
                sch.add("pool", lambda e: e.tensor_copy(out=HALO[0:rows, :, fc, :], in_=ue[0:rows, :, 256:258]), r=(uen,), w=(hn_,))

            def rec_conv(fc):
                rows = 128 if fc < 21 else 64
                par = fc % 3
                ue = UE[par]; uen = "f:UE%d" % par
                cxs = ((0, CG[par], "f:CG%d" % par), (1, CV[par], "f:CV%d" % par))
                cw = lambda tap, gv: conv[0:rows, gv, fc, tap:tap + 1]
                for gv, cx, cxn in cxs:
                    sch.add("dve", lambda e, cx=cx, gv=gv: e.tensor_scalar(out=cx[0:rows, :], in0=ue[0:rows, gv, 2:258], scalar1=cw(2, gv), scalar2=cw(3, gv), op0=ALU.mult, op1=ALU.add),
                            r=(uen, "c:conv"), w=(cxn,))
                for gv, cx, cxn in cxs:
                    sch.add("dve", lambda e, cx=cx, gv=gv: e.scalar_tensor_tensor(out=cx[0:rows, :], in0=ue[0:rows, gv, 1:257], scalar=cw(1, gv), in1=cx[0:rows, :], op0=ALU.mult, op1=ALU.add),
                            r=(uen, cxn, "c:conv"), w=(cxn,))
                for gv, cx, cxn in cxs:
                    sch.add("dve", lambda e, cx=cx, gv=gv: e.scalar_tensor_tensor(out=cx[0:rows, :], in0=ue[0:rows, gv, 0:256], scalar=cw(0, gv), in1=cx[0:rows, :], op0=ALU.mult, op1=ALU.add),
                            r=(uen, cxn, "c:conv"), w=(cxn,))

            def rec_act(fc):
                rows = 128 if fc < 21 else 64
                par = fc % 3
                sg = SGF[par]; af = AF_[par]
                sch.add("act", lambda e: e.activation(out=sg[0:rows, :], in_=CG[par][0:rows, :], func=AF.Silu), r=("f:CG%d" % par,), w=("f:SG%d" % par,))
                sch.add("pool", lambda e: e.tensor_tensor(out=af[0:rows, :], in0=sg[0:rows, :], in1=CV[par][0:rows, :], op=ALU.mult),
                        r=("f:SG%d" % par, "f:CV%d" % par), w=("f:A%d" % par,))

            def rec_down(fc):
                rows = 128 if fc < 21 else 64
                par = fc % 3
                af = AF_[par]

                def dmm(e):
                    ins = []
                    for tl in range(2):
                        for hf in range(2):
                            ins.append(mm(e, PB[DB[tl * 2 + hf]][:, :], af[0:rows, tl * 128:(tl + 1) * 128], Wdn[0:rows, fc, hf * 512:(hf + 1) * 512], fc == 0, fc == 21))
                    return ins
                sch.add("pe", dmm, r=("f:A%d" % par, "f:Wdn"), w=("B0", "B1", "B2", "B3"))
            for k in range(22 + 3):
                if k < 22:
                    rec_up(k)
                    rec_conv(k)
                if 0 <= k - 1 < 22:
                    rec_act(k - 1)
                if 0 <= k - 3 < 22:
                    rec_down(k - 3)
                if c8 + 1 < 8 and k in (2, 6, 9, 12):
                    pf(c8 + 1, (2, 6, 9, 12).index(k))
            for tl in range(2):
                tt = 2 * c8 + tl
                xi = (c8 % 2) * 2 + tl
                xt = X1S[xi]; xn = "f:x1s%d" % xi
                for hf in range(2):
                    b = DB[tl * 2 + hf]
                    sch.add("dve", lambda e, xt=xt, b=b, hf=hf: e.tensor_tensor(out=xt[:, hf * 512:(hf + 1) * 512], in0=xt[:, hf * 512:(hf + 1) * 512], in1=PB[b][:, :], op=ALU.add),
                            r=(xn, "B%d" % b), w=(xn,))
                sch.add("sp", lambda e, xt=xt, tt=tt: e.dma_start(out=out_d[s, tt * 128:(tt + 1) * 128, :], in_=xt), r=(xn,), w=("of:%d_%d" % (s, tt),), dma=("x1o%d" % xi, 1))
        sch.barrier(("m:", "x1:", "x2:", "x3:", "f:", "t:", "u1:", "u2:"))

    for s_i in range(nseq):
        do_seq(s_i)

    sch.finalize()
    print("ops:", len(sch.ops))
    sems = {}
    for k in sch.sem_keys():
        sems[k] = es.enter_context(nc.semaphore("s_%s_%s" % k))
    with nc.Block() as block:
        @block.sync
        def _(e):
            sch.emit("sp", e, sems)

        @block.tensor
        def _(e):
            sch.emit("pe", e, sems)

        @block.scalar
        def _(e):
            sch.emit("act", e, sems)

        @block.vector
        def _(e):
            sch.emit("dve", e, sems)

        @block.gpsimd
        def _(e):
            sch.emit("pool", e, sems)
    es.close()
    return nc, dbg_d


_CACHE = {}


def kernel(**inputs):
    x = np.asarray(inputs["x"], dtype=np.float32)
    B = x.shape[0]
    nseq = B // NCORES
    consts = host_consts()
    wts = host_weights(inputs)
    if "nc" not in _CACHE:
        _CACHE["nc"] = build(nseq=nseq)[0]
    nc = _CACHE["nc"]
    in_maps = []
    for c in range(NCORES):
        m = {"x": np.ascontiguousarray(x[c * nseq:(c + 1) * nseq])}
        m.update(consts)
        m.update(wts)
        in_maps.append(m)
    res = run_bass_kernel_spmd(nc, in_maps, core_ids=list(range(NCORES)))
    out = np.concatenate([np.asarray(r["out"], dtype=np.float32) for r in res.results], axis=0)
    return out
```

```python
import contextlib
import os
import numpy as np
import ml_dtypes
import concourse.bass as bass
import concourse.mybir as mybir
from concourse.bass_utils import run_bass_kernel_spmd

F32 = mybir.dt.float32
BF16 = mybir.dt.bfloat16
ALU = mybir.AluOpType
AF = mybir.ActivationFunctionType
AX = mybir.AxisListType

S = 2048
D = 1024
NCORES = 8
EPS = 1e-6
NEGB = -30000.0
D_FF = 2752
ENGS = ("pe", "act", "dve", "pool", "sp")


class Op:
    __slots__ = ("eng", "fn", "dma", "deps", "signal", "val", "w", "waits")


class Sched:
    def __init__(self):
        self.ops = []
        self.lastw = {}
        self.readers = {}
        self.dmacum = {}
        self.barriers = []

    def add(self, eng, fn, r=(), w=(), dma=None):
        op = Op()
        op.eng = eng
        op.fn = fn
        op.dma = dma
        op.signal = False
        op.val = None
        op.deps = []
        seen = set()
        for n in tuple(r) + tuple(w):
            if n not in self.lastw:
                for pf, bop in reversed(self.barriers):
                    if n.startswith(pf):
                        self.lastw[n] = bop
                        self.readers.setdefault(n, [])
                        break
        for n in r:
            d = self.lastw.get(n)
            if d is not None and id(d) not in seen:
                seen.add(id(d))
                op.deps.append((d, "raw"))
        for n in w:
            d = self.lastw.get(n)
            if d is not None and id(d) not in seen:
                seen.add(id(d))
                op.deps.append((d, "waw"))
            for d in self.readers.get(n, ()):
                if id(d) not in seen:
                    seen.add(id(d))
                    op.deps.append((d, "war"))
        for n in r:
            self.readers.setdefault(n, []).append(op)
        for n in w:
            self.lastw[n] = op
            self.readers[n] = []
        if dma is not None:
            self.dmacum[dma[0]] = self.dmacum.get(dma[0], 0) + 16 * dma[1]
            op.val = self.dmacum[dma[0]]
        self.ops.append(op)
        return op

    def barrier(self, prefixes):
        names = [n for n in set(self.lastw) | set(self.readers) if n.startswith(prefixes)]
        bop = self.add("pool", lambda e: e.nop(), r=(), w=names)
        self.barriers.append((prefixes, bop))

    def finalize(self):
        for op in self.ops:
            op.w = []
            for d, kind in op.deps:
                if d.dma is not None:
                    op.w.append(d)
                elif d.eng == op.eng:
                    if op.dma is not None:
                        op.w.append(d)
                        d.signal = True
                    elif op.eng == "pe":
                        continue
                    elif kind == "raw":
                        op.w.append(d)
                        d.signal = True
                else:
                    op.w.append(d)
                    d.signal = True
        cnt = {}
        for op in self.ops:
            if op.dma is None and op.signal:
                cnt[op.eng] = cnt.get(op.eng, 0) + 1
                op.val = cnt[op.eng]
        waited = {e: {} for e in ENGS}
        for op in self.ops:
            ws = {}
            for d in op.w:
                key = ("dma", d.dma[0]) if d.dma is not None else ("eng", d.eng)
                if waited[op.eng].get(key, 0) >= d.val:
                    continue
                ws[key] = max(ws.get(key, 0), d.val)
            for k, v in ws.items():
                waited[op.eng][k] = v
            op.waits = list(ws.items())

    def sem_keys(self):
        return [("eng", e) for e in ENGS] + [("dma", s) for s in self.dmacum]

    def emit(self, eng_name, e, sems):
        for op in self.ops:
            if op.eng != eng_name:
                continue
            for key, v in op.waits:
                e.wait_ge(sems[key], v)
            insts = op.fn(e)
            if not isinstance(insts, (list, tuple)):
                insts = [insts]
            if op.dma is not None:
                assert len(insts) == op.dma[1], (len(insts), op.dma)
                for i in insts:
                    i.then_inc(sems[("dma", op.dma[0])], 16)
            elif op.signal:
                insts[-1].then_inc(sems[("eng", op.eng)], 1)
        if eng_name == "sp":
            for s, v in self.dmacum.items():
                e.wait_ge(sems[("dma", s)], v)


def host_consts():
    bf = ml_dtypes.bfloat16
    c = {}
    c["ident"] = np.eye(128, dtype=np.float32).astype(bf)
    half = 32
    freqs = (10000.0 ** (-np.arange(half, dtype=np.float32) / half)).astype(np.float32)
    pos = np.arange(S, dtype=np.float32)
    ang = (pos[:, None] * freqs[None, :]).astype(np.float32)
    cs = np.stack([np.cos(ang), np.sin(ang)], 0).astype(np.float32)
    c["cossin"] = np.ascontiguousarray(cs.reshape(2, 16, 128, 32).transpose(2, 0, 1, 3))
    n = np.arange(128)[:, None]
    q = np.arange(S)[None, :]
    c["cmpmask"] = np.where((n < 127) & (16 * n + 31 <= q), 0.0, NEGB).astype(bf)
    kk = np.arange(128)[:, None]
    qq = np.arange(128)[None, :]
    tri = np.concatenate([np.where(qq >= kk, 0.0, NEGB), np.where(qq < kk, 0.0, NEGB)], 1)
    c["tri"] = tri.astype(bf)
    j = np.arange(32)[:, None]
    k = np.arange(S)[None, :]
    c["esel"] = (k // 64 == j).astype(np.float32).astype(bf)
    tq = np.arange(S)[:, None]
    jb = np.arange(32)[None, :]
    cur = tq // 64
    forced = (jb == 0) | (jb == cur) | (jb == cur - 1)
    valid = jb * 64 <= tq
    fb = np.where(valid, 1000.0 * forced, -1e30).astype(np.float32)
    c["fb"] = np.ascontiguousarray(fb.reshape(16, 128, 32).transpose(1, 0, 2))
    ci = np.arange(128)[:, None]
    sj = np.arange(32)[None, :]
    ov = ((ci * 16 < (sj + 1) * 64) & (ci * 16 + 32 > sj * 64) & (ci < 127)).astype(np.float32)
    c["ov"] = ov.astype(bf)
    bs = np.zeros((48, 48, 64), np.float32)
    for i in range(48):
        bs[i, i, :] = 1.0
    c["bsel"] = bs.reshape(48, 48 * 64).astype(bf)
    pc = np.ones((128, 4, 16), np.float32)
    for g, w in enumerate((2, 4, 8, 16)):
        t = np.arange(16)
        pc[:, g, :] = (w / np.minimum(t + 1, w))[None, :]
    c["poolcorr"] = pc
    return c


def host_weights(inp):
    f = np.float32
    w = {}
    A = lambda a: np.ascontiguousarray(np.asarray(a, dtype=f))
    w["w_in"] = A(inp["w_in"][0].reshape(8, 128, 4400).transpose(1, 0, 2))
    w["w_pool_br"] = A(inp["w_pool_br"][0].reshape(4, 128, 1024).transpose(1, 0, 2))
    w["w_attn_br"] = A(inp["w_attn_br"][0].reshape(8, 128, 1024).transpose(1, 0, 2))
    w["w_o"] = A(inp["w_o"][0].reshape(8, 128, 1024).transpose(1, 0, 2))
    w["w_up"] = A(inp["w_up"][0].reshape(8, 128, 5504).transpose(1, 0, 2))
    wd = np.zeros((22 * 128, 1024), f)
    wd[:D_FF] = inp["w_down"][0]
    w["w_down"] = A(wd.reshape(22, 128, 1024).transpose(1, 0, 2))
    w["pool_w"] = A(inp["pool_w"][0].transpose(1, 0, 2))
    w1 = np.asarray(inp["cmp_w1"][0]).reshape(2, 32, 64, 128).transpose(2, 0, 1, 3)
    w["cmp_w1"] = A(np.concatenate([w1, w1], 0))
    w["cmp_w2"] = A(np.asarray(inp["cmp_w2"][0]).transpose(1, 0, 2))
    pt = np.asarray(inp["cmp_pos"][0]).transpose(2, 0, 1)
    w["cmp_posT"] = A(np.concatenate([pt, pt], 0))
    w["cmp_b1"] = A(np.asarray(inp["cmp_b1"][0]).T)
    nrm = np.stack([np.asarray(inp["attn_norm_w"][0]).reshape(8, 128).T,
                    np.asarray(inp["ffn_norm_w"][0]).reshape(8, 128).T], 1)
    w["nrm"] = A(nrm)
    w["pool_scale"] = A(np.asarray(inp["pool_scale"][0]).reshape(4, 128).T)
    qk = np.concatenate([np.asarray(inp["q_norm_w"][0])[None], np.asarray(inp["k_norm_w"][0])], 0)
    w["qkw"] = A(np.broadcast_to(qk[None], (128, 4, 64)))
    cw = np.asarray(inp["conv_w"][0])
    cb = np.asarray(inp["conv_b"][0])
    cv = np.zeros((128, 2, 22, 4), f)
    for gv in range(2):
        for fc in range(22):
            rows = 128 if fc < 21 else 64
            sl = slice(gv * D_FF + fc * 128, gv * D_FF + fc * 128 + rows)
            cv[:rows, gv, fc, 0:3] = cw[:, sl].T
            cv[:rows, gv, fc, 3] = cb[sl]
    w["conv"] = cv
    return w


DRAM_SPECS = {
    "w_in": ([128, 8, 4400], F32), "w_pool_br": ([128, 4, 1024], F32), "w_attn_br": ([128, 8, 1024], F32),
    "w_o": ([128, 8, 1024], F32), "w_up": ([128, 8, 5504], F32), "w_down": ([128, 22, 1024], F32),
    "pool_w": ([128, 4, 128], F32), "cmp_w1": ([128, 2, 32, 128], F32), "cmp_w2": ([128, 2, 64], F32),
    "cmp_posT": ([128, 2, 32], F32), "cmp_b1": ([128, 2], F32), "nrm": ([128, 2, 8], F32),
    "pool_scale": ([128, 4], F32), "qkw": ([128, 4, 64], F32), "conv": ([128, 2, 22, 4], F32),
    "ident": ([128, 128], BF16), "cossin": ([128, 2, 16, 32], F32), "cmpmask": ([128, 2048], BF16),
    "tri": ([128, 256], BF16), "esel": ([32, 2048], BF16), "fb": ([128, 16, 32], F32),
    "ov": ([128, 32], BF16), "bsel": ([48, 3072], BF16), "poolcorr": ([128, 4, 16], F32),
}


def build(nseq=4, stop_after=99, dbg=False):
    nc = bass.Bass("TRN2", target_bir_lowering=False)
    sch = Sched()
    dr = {}
    for name, (shape, dt) in DRAM_SPECS.items():
        dr[name] = nc.dram_tensor(name, shape, dt, kind="ExternalInput").ap()
    x_d = nc.dram_tensor("x", [nseq, S, D], F32, kind="ExternalInput").ap()
    out_d = nc.dram_tensor("out", [nseq, S, D], F32, kind="ExternalOutput").ap()
    x1_d = nc.dram_tensor("x1scratch", [nseq + 1, S, D], F32, kind="Internal").ap()[1:nseq + 1]
    dbg_d = {}

    es = contextlib.ExitStack()
    ARENA_B = 206 * 1024
    arena = es.enter_context(nc.sbuf_tensor("arena", [128, ARENA_B // 2], BF16))
    cur = [0]

    def alloc(shape, dt=BF16):
        n = 1
        for s_ in shape[1:]:
            n *= s_
        nb = n * (4 if dt == F32 else 2)
        nb = (nb + 63) // 64 * 64
        off = cur[0]
        cur[0] += nb
        assert cur[0] <= ARENA_B, ("arena overflow", cur[0])
        ap = arena[:, off // 2:(off + nb) // 2]
        if dt == F32:
            ap = ap.bitcast(F32)
        ap = ap[:, 0:n]
        if len(shape) == 3:
            ap = ap.rearrange("p (a b) -> p a b", a=shape[1], b=shape[2])
        elif len(shape) == 4:
            ap = ap.rearrange("p (a b c) -> p a b c", a=shape[1], b=shape[2], c=shape[3])
        elif len(shape) == 5:
            ap = ap.rearrange("p (a b c d) -> p a b c d", a=shape[1], b=shape[2], c=shape[3], d=shape[4])
        return ap

    PB = [es.enter_context(nc.psum_tensor("pb%d" % i, [128, 512], F32)) for i in range(7)]
    BT = es.enter_context(nc.psum_tensor("pbt", [128, 1024], BF16))
    pbrot = [0]

    def nextbank(pool):
        i = pool[pbrot[0] % len(pool)]
        pbrot[0] += 1
        return i

    ident = alloc([128, 128]); cossin = alloc([128, 2, 16, 32], F32); cmpmask = alloc([128, 2048])
    tri = alloc([128, 256]); fb = alloc([128, 16, 32], F32); ov = alloc([128, 32])
    bsel = alloc([128, 3072]); poolcorr = alloc([128, 4, 16], F32); nrm = alloc([128, 2, 8], F32)
    pool_scale = alloc([128, 4], F32); qkw = alloc([128, 4, 64], F32); conv = alloc([128, 2, 22, 4], F32)
    b1 = alloc([128, 2], F32); chid = alloc([128, 2], F32); w2 = alloc([128, 2, 64]); pool_w = alloc([128, 4, 128])
    posT = alloc([128, 2, 32]); ones = alloc([128, 128]); mhalf = alloc([128, 16], F32)
    KCMP = alloc([128, 2, 128]); VCMP = alloc([128, 2, 128])
    ssq = alloc([128, 16], F32); rs = alloc([128, 16], F32); sr = alloc([128, 16], F32); rstd = alloc([128, 16], F32)
    mark0 = cur[0]
    hT = alloc([128, 8, S]); OT = alloc([128, 8, S])
    markX = cur[0]
    KAs = alloc([128, 2, S]); KAw = alloc([128, 2, S])
    VA = alloc([128, 2, 2, 16, 128]); G = alloc([128, S])
    Wsm = alloc([128, 8, 1024])
    w1 = Wsm.rearrange("p a b -> p (a b)").rearrange("p (k l h) -> p k l h", k=2, l=32, h=128)
    ZQ = alloc([128, 1024], F32); SQ = alloc([128, 1024], F32); QW = alloc([128, 1024], F32)
    TA = alloc([128, 512], F32); TB = alloc([128, 512], F32); TC = alloc([128, 512], F32); TD = alloc([128, 512], F32)
    QN = alloc([128, 1024]); KZb = alloc([128, 384], F32)
    markU = cur[0]
    xs = [alloc([128, 1024], F32) for _ in range(2)]
    hn = [alloc([128, 1024]) for _ in range(2)]
    endU1 = cur[0]
    cur[0] = markU
    KC = alloc([128, S]); VC = alloc([128, S])
    HID = alloc([128, 128]); XG = alloc([128, 128], F32); X2 = alloc([128, 128], F32); X3 = alloc([128, 128], F32)
    endU2 = cur[0]
    cur[0] = markU
    QA = alloc([128, 16, 512])
    PT = [alloc([128, 512]) for _ in range(3)]
    RR = alloc([128, 512], F32); R2 = alloc([128, 512], F32)
    U1 = alloc([128, 512], F32); U2 = alloc([128, 512], F32); UT = alloc([128, 512], F32)
    ECN = [alloc([128, 512]) for _ in range(2)]
    PSC = alloc([128, 512], F32); PSCb = alloc([128, 512])
    SC = alloc([128, 4, 32], F32); M8 = alloc([128, 4, 8], F32); SBf = alloc([128, 4, 32], F32); SBb = alloc([128, 4, 32])
    GSB = alloc([128, 512], F32); RCPb = alloc([128, 512], F32)
    endX1 = max(cur[0], endU1, endU2)
    cur[0] = markX
    YP = alloc([128, 4, S])
    markY = cur[0]
    UF = alloc([128, 16 + S], F32); SA = alloc([128, 16 + S], F32); SBp = alloc([128, 16 + S], F32); PLb = alloc([128, S])
    Wp = alloc([128, 8, 512])
    endX2 = cur[0]
    cur[0] = markY
    Wmg = alloc([128, 8, 2048]); Wpb = alloc([128, 4, 1024]); Wab = alloc([128, 8, 1024]); Wo = alloc([128, 8, 1024])
    MT = alloc([128, 8, 512]); SG = [alloc([128, 512], F32) for _ in range(2)]
    T0 = alloc([128, 512], F32); T1 = alloc([128, 512], F32)
    xs6 = [alloc([128, 1024], F32) for _ in range(2)]
    endX3 = cur[0]
    cur[0] = mark0
    Wup = alloc([128, 8, 5504]); Wdn = alloc([128, 22, 1024])
    H2 = [alloc([128, 8, 256]) for _ in range(2)]; X1S = [alloc([128, 1024], F32) for _ in range(4)]
    hn7 = [alloc([128, 1024]) for _ in range(2)]
    UE = [alloc([128, 2, 258], F32) for _ in range(3)]
    CG = [alloc([128, 256], F32) for _ in range(3)]; CV = [alloc([128, 256], F32) for _ in range(3)]
    SGF = [alloc([128, 256], F32) for _ in range(3)]; AF_ = [alloc([128, 256]) for _ in range(3)]
    HALO = alloc([128, 2, 22, 2], F32)
    endF = cur[0]
    print("arena bytes: X1 %d X2 %d X3 %d FFN %d" % (endX1, endX2, endX3, endF))

    def dump(name, ap, shape, dt=F32, reads=()):
        if not dbg:
            return
        d = nc.dram_tensor("dbg_" + name, shape, dt, kind="ExternalOutput").ap()
        dbg_d[name] = d
        sch.add("sp", lambda e, d=d, ap=ap: e.dma_start(out=d, in_=ap), r=reads, w=(), dma=("dbg_" + name, 1))

    def ld(dst, src, slot, wn, eng="sp"):
        sch.add(eng, lambda e: e.dma_start(out=dst, in_=src), r=(), w=(wn,), dma=("L" + wn, 1))

    ld(ident, dr["ident"], "c0", "c:ident"); ld(cossin, dr["cossin"], "c0", "c:cossin")
    ld(cmpmask, dr["cmpmask"], "c0", "c:cmpmask"); ld(tri, dr["tri"], "c0", "c:tri"); ld(fb, dr["fb"], "c0", "c:fb")
    ld(ov, dr["ov"], "c0", "c:ov"); ld(bsel[0:48], dr["bsel"], "c0", "c:bsel"); ld(poolcorr, dr["poolcorr"], "c0", "c:poolcorr")
    ld(nrm, dr["nrm"], "c0", "c:nrm"); ld(pool_scale, dr["pool_scale"], "c0", "c:pool_scale"); ld(qkw, dr["qkw"], "c0", "c:qkw")
    ld(conv, dr["conv"], "c0", "c:conv"); ld(b1, dr["cmp_b1"], "c0", "c:b1")
    ld(w2, dr["cmp_w2"], "c1", "c:w2", "pool"); ld(pool_w, dr["pool_w"], "c1", "c:pool_w", "pool")
    ld(posT, dr["cmp_posT"], "c1", "c:posT", "pool")
    sch.add("dve", lambda e: e.memset(ones, 1.0), w=("c:ones",))
    sch.add("dve", lambda e: e.memset(mhalf, -0.5), w=("c:mhalf",))
    sch.add("dve", lambda e: e.memset(KCMP, 0.0), w=("m:KCMP",))
    sch.add("dve", lambda e: e.memset(VCMP, 0.0), w=("m:VCMP",))
    sch.add("dve", lambda e: e.memset(VCMP[0:127, :, 64:128], 1.0), w=("m:VCMP",))

    scr = {}

    def pieces(a_, b_):
        out = []
        for k in range(a_.shape[1]):
            if len(a_.shape) == 3 and a_.shape[2] > 2816:
                w_ = a_.shape[2] // 4
                for i in range(4):
                    out.append((a_[:, k, i * w_:(i + 1) * w_], b_[:, k, i * w_:(i + 1) * w_]))
            else:
                out.append((a_[:, k], b_[:, k]))
        return out

    def wload(dst, srcap, wn, slot, nk, s=0, key=None, cast_fn=None):
        key = key or slot
        if key not in scr:
            scr[key] = nc.dram_tensor("wscr_" + key, list(dst.shape), BF16, kind="Internal").ap()
        sc = scr[key]
        if s == 0:
            if cast_fn is None:
                sch.add("pool", lambda e: [e.dma_start(out=dst[:, k, :], in_=srcap[:, k, :]) for k in range(nk)], w=(wn,), dma=(slot, nk))
            else:
                cast_fn()
            sch.add("sp", lambda e: [e.dma_start(out=a_, in_=b_) for a_, b_ in pieces(sc, dst)], r=(wn,), w=("scr:" + key,), dma=("st_" + key, len(pieces(sc, dst))))
        else:
            sch.add("sp", lambda e: [e.dma_start(out=a_, in_=b_) for a_, b_ in pieces(dst, sc)], r=("scr:" + key,), w=(wn,), dma=("ld_" + key, len(pieces(dst, sc))))

    def mm(e, out, lhsT, rhs, start, stop):
        return e.matmul(out, lhsT=lhsT, rhs=rhs, start=start, stop=stop)

    def norm_A(src_tile, src_name, hn_t, hn_name, stat_col):
        sc = slice(stat_col, stat_col + 1)
        k_ = str(stat_col)
        sch.add("act", lambda e: e.activation(out=hn_t, in_=src_tile, func=AF.Square, accum_out=ssq[:, sc]),
                r=(src_name,), w=(hn_name, "t:ssq" + k_))
        sch.add("dve", lambda e: e.tensor_scalar(out=rs[:, sc], in0=ssq[:, sc], scalar1=1.0 / D, scalar2=EPS,
                                                 op0=ALU.mult, op1=ALU.add), r=("t:ssq" + k_,), w=("t:rs" + k_,))
        sch.add("pool", lambda e: e.tensor_tensor(out=rstd[:, sc], in0=rs[:, sc], in1=mhalf[:, sc], op=ALU.pow), r=("t:rs" + k_, "c:mhalf"), w=("t:rstd" + k_,))

    def norm_B1(src_tile, src_name, hn_t, hn_name, stat_col):
        sc = slice(stat_col, stat_col + 1)
        sch.add("act", lambda e: e.activation(out=hn_t, in_=src_tile, func=AF.Copy, scale=rstd[:, sc]),
                r=(src_name, "t:rstd" + str(stat_col)), w=(hn_name,))

    def norm_B2(hn_t, hn_name, nrm_idx, dst, dst_name, col0):
        btv = BT[:, 0:1024].rearrange("p (a b) -> p a b", a=8, b=128)
        sch.add("pe", lambda e: [e.transpose(btv[:, kc, :], hn_t[:, kc * 128:(kc + 1) * 128], ident) for kc in range(8)],
                r=(hn_name, "c:ident"), w=("BT",))
        sch.add("dve", lambda e: e.tensor_tensor(out=dst[:, :, col0:col0 + 128], in0=btv,
                                                 in1=nrm[:, nrm_idx, :].unsqueeze(2).to_broadcast([128, 8, 128]),
                                                 op=ALU.mult), r=("BT", "c:nrm"), w=(dst_name,))

    def norm_tile(src_tile, src_name, hn_t, hn_name, nrm_idx, dst, dst_name, col0, stat_col):
        norm_A(src_tile, src_name, hn_t, hn_name, stat_col)
        norm_B1(src_tile, src_name, hn_t, hn_name, stat_col)
        norm_B2(hn_t, hn_name, nrm_idx, dst, dst_name, col0)

    def rope_A(Z, zname, nh, widx_ap, eps_eff):
        n = nh * 64
        sqv = SQ[:, 0:n].rearrange("p (h d) -> p h d", h=nh, d=64)
        qwv = QW[:, 0:n].rearrange("p (h d) -> p h d", h=nh, d=64)
        sch.add("dve", lambda e: e.tensor_tensor(out=sqv, in0=Z, in1=Z, op=ALU.mult), r=(zname,), w=("t:SQ",))
        sch.add("dve", lambda e: e.tensor_reduce(out=ssq[:, 0:nh], in_=sqv, axis=AX.X, op=ALU.add), r=("t:SQ",), w=("t:ssq",))
        sch.add("dve", lambda e: e.tensor_scalar(out=rs[:, 0:nh], in0=ssq[:, 0:nh], scalar1=eps_eff[0], scalar2=eps_eff[1],
                                                 op0=ALU.mult, op1=ALU.add), r=("t:ssq",), w=("t:rs",))
        sch.add("pool", lambda e: e.tensor_tensor(out=rstd[:, 0:nh], in0=rs[:, 0:nh], in1=mhalf[:, 0:nh], op=ALU.pow), r=("t:rs", "c:mhalf"), w=("t:rstd",))
        sch.add("dve", lambda e: e.tensor_tensor(out=qwv, in0=Z, in1=widx_ap, op=ALU.mult), r=(zname, "c:qkw"), w=("t:QW",))

    def rope_B(nh, tt, out_bf, out_name):
        n = nh * 64
        qwv = QW[:, 0:n].rearrange("p (h d) -> p h d", h=nh, d=64)
        h = nh * 32
        cosb = cossin[:, 0, tt, :].unsqueeze(1).to_broadcast([128, nh, 32])
        sinb = cossin[:, 1, tt, :].unsqueeze(1).to_broadcast([128, nh, 32])
        v3 = lambda T: T[:, 0:h].rearrange("p (h d) -> p h d", h=nh, d=32)
        q1 = qwv[:, :, 0:32]
        q2 = qwv[:, :, 32:64]
        sch.add("pool", lambda e: e.tensor_tensor(out=v3(TA), in0=q1, in1=cosb, op=ALU.mult), r=("t:QW", "c:cossin"), w=("t:TA",))
        sch.add("pool", lambda e: e.tensor_tensor(out=v3(TB), in0=q2, in1=sinb, op=ALU.mult), r=("t:QW", "c:cossin"), w=("t:TB",))
        sch.add("pool", lambda e: e.tensor_tensor(out=v3(TC), in0=q2, in1=cosb, op=ALU.mult), r=("t:QW", "c:cossin"), w=("t:TC",))
        sch.add("pool", lambda e: e.tensor_tensor(out=v3(TD), in0=q1, in1=sinb, op=ALU.mult), r=("t:QW", "c:cossin"), w=("t:TD",))
        sch.add("dve", lambda e: e.tensor_tensor(out=v3(TA), in0=v3(TA), in1=v3(TB), op=ALU.subtract), r=("t:TA", "t:TB"), w=("t:TA",))
        sch.add("dve", lambda e: e.tensor_tensor(out=v3(TC), in0=v3(TC), in1=v3(TD), op=ALU.add), r=("t:TC", "t:TD"), w=("t:TC",))
        rb = rstd[:, 0:nh].unsqueeze(2).to_broadcast([128, nh, 32])
        sch.add("dve", lambda e: e.tensor_tensor(out=out_bf[:, :, 0:32], in0=v3(TA), in1=rb, op=ALU.mult), r=("t:TA", "t:rstd"), w=(out_name,))
        sch.add("dve", lambda e: e.tensor_tensor(out=out_bf[:, :, 32:64], in0=v3(TC), in1=rb, op=ALU.mult), r=("t:TC", "t:rstd"), w=(out_name,))


    UPFX = ("u1:", "u2:", "t:", "x1:QA")

    def do_seq(s):
        for tt in range(16):
            xt = xs[tt % 2]
            xn = "u1:xs%d" % (tt % 2)
            sch.add("sp", lambda e, xt=xt, tt=tt: e.dma_start(out=xt, in_=x_d[s, tt * 128:(tt + 1) * 128, :]),
                    w=(xn,), dma=("xs%d" % (tt % 2), 1))
            norm_tile(xt, xn, hn[tt % 2], "u1:hn%d" % (tt % 2), 0, hT, "m:hT%d" % (tt // 4), tt * 128, 0)
        if s == 0:
            dump("hT", hT, [128, 8, S], BF16, reads=["m:hT%d" % i for i in range(4)])
        if stop_after <= 1:
            return
        sch.barrier(UPFX)
        sch.add("sp", lambda e: [e.dma_start(out=KAs[64:96, g, :], in_=dr["esel"]) for g in range(2)], w=("x1:KAs_e",), dma=("esel", 2))
        sch.add("pool", lambda e: e.memset(VA[:, :, :, :, 64:128], 1.0), w=("x1:VA1",))
        WA = Wsm[:, :, 0:816]
        wload(WA, dr["w_in"][:, :, 1536:2352], "x1:Wsm", "wsm", 8, s, "wa")
        ZK = ZQ[:, 0:768]
        zk5 = ZK.rearrange("p (b k g d) -> p b k g d", b=3, k=2, g=2, d=64)
        def p2_tile(tt):
            c = tt // 4
            KZ = KZb.rearrange("p (b g d) -> p b g d", b=3, g=2, d=64)

            def s_mm():
                def kvmm(e, tt=tt):
                    ins = []
                    for kc in range(8):
                        ins.append(mm(e, PB[0][:, 0:512], hT[:, kc, tt * 128:(tt + 1) * 128], WA[:, kc, 0:512], kc == 0, kc == 7))
                    for kc in range(8):
                        ins.append(mm(e, PB[1][:, 0:256], hT[:, kc, tt * 128:(tt + 1) * 128], WA[:, kc, 512:768], kc == 0, kc == 7))
                    return ins
                sch.add("pe", kvmm, r=("m:hT%d" % c, "x1:Wsm"), w=("B0", "B1"))

            def s_copy():
                sch.add("act", lambda e: e.activation(out=ZK[:, 0:512], in_=PB[0][:, 0:512], func=AF.Copy), r=("B0",), w=("t:ZK",))
                sch.add("act", lambda e: e.activation(out=ZK[:, 512:768], in_=PB[1][:, 0:256], func=AF.Copy), r=("B1",), w=("t:ZK",))

            def s_A():
                sch.add("pool", lambda e, tt=tt: e.tensor_copy(out=VA[:, :, :, tt, 0:64], in_=zk5[:, 1:3, 1, :, :]), r=("t:ZK",), w=("x1:VA",))
                sch.add("pool", lambda e: e.tensor_copy(out=KZ, in_=zk5[:, :, 0, :, :]), r=("t:ZK",), w=("t:KZ",))

            def s_B():
                KZ3 = KZb.rearrange("p (h d) -> p h d", h=6, d=64)
                kw_ap = qkw[:, 1:4, :].unsqueeze(2).to_broadcast([128, 3, 2, 64])
                KN = QN[:, 0:384].rearrange("p (h d) -> p h d", h=6, d=64)
                n = 384
                sqv = SQ[:, 0:n].rearrange("p (h d) -> p h d", h=6, d=64)
                qwv = QW[:, 0:n].rearrange("p (h d) -> p h d", h=6, d=64)
                qwv4 = QW[:, 0:n].rearrange("p (b g d) -> p b g d", b=3, g=2, d=64)
                sch.add("dve", lambda e: e.tensor_tensor(out=sqv, in0=KZ3, in1=KZ3, op=ALU.mult), r=("t:KZ",), w=("t:SQ",))
                sch.add("dve", lambda e: e.tensor_reduce(out=ssq[:, 0:6], in_=sqv, axis=AX.X, op=ALU.add), r=("t:SQ",), w=("t:ssq",))
                sch.add("dve", lambda e: e.tensor_scalar(out=rs[:, 0:6], in0=ssq[:, 0:6], scalar1=1.0 / 64, scalar2=EPS,
                                                         op0=ALU.mult, op1=ALU.add), r=("t:ssq",), w=("t:rs",))
                sch.add("pool", lambda e: e.tensor_tensor(out=rstd[:, 0:6], in0=rs[:, 0:6], in1=mhalf[:, 0:6], op=ALU.pow), r=("t:rs", "c:mhalf"), w=("t:rstd",))
                sch.add("dve", lambda e: e.tensor_tensor(out=qwv4, in0=KZ, in1=kw_ap, op=ALU.mult), r=("t:KZ", "c:qkw"), w=("t:QW",))
                nh = 6
                h_ = nh * 32
                cosb = cossin[:, 0, tt, :].unsqueeze(1).to_broadcast([128, nh, 32])
                sinb = cossin[:, 1, tt, :].unsqueeze(1).to_broadcast([128, nh, 32])
                v3 = lambda T: T[:, 0:h_].rearrange("p (h d) -> p h d", h=nh, d=32)
                q1 = qwv[:, :, 0:32]
                q2 = qwv[:, :, 32:64]
                sch.add("pool", lambda e, cosb=cosb, q1=q1: e.tensor_tensor(out=v3(TA), in0=q1, in1=cosb, op=ALU.mult), r=("t:QW", "c:cossin"), w=("t:TA",))
                sch.add("pool", lambda e, sinb=sinb, q2=q2: e.tensor_tensor(out=v3(TB), in0=q2, in1=sinb, op=ALU.mult), r=("t:QW", "c:cossin"), w=("t:TB",))
                sch.add("pool", lambda e, cosb=cosb, q2=q2: e.tensor_tensor(out=v3(TC), in0=q2, in1=cosb, op=ALU.mult), r=("t:QW", "c:cossin"), w=("t:TC",))
                sch.add("pool", lambda e, sinb=sinb, q1=q1: e.tensor_tensor(out=v3(TD), in0=q1, in1=sinb, op=ALU.mult), r=("t:QW", "c:cossin"), w=("t:TD",))
                sch.add("dve", lambda e: e.tensor_tensor(out=v3(TA), in0=v3(TA), in1=v3(TB), op=ALU.subtract), r=("t:TA", "t:TB"), w=("t:TA",))
                sch.add("dve", lambda e: e.tensor_tensor(out=v3(TC), in0=v3(TC), in1=v3(TD), op=ALU.add), r=("t:TC", "t:TD"), w=("t:TC",))
                rb = rstd[:, 0:nh].unsqueeze(2).to_broadcast([128, nh, 32])
                sch.add("dve", lambda e, rb=rb: e.tensor_tensor(out=KN[:, :, 0:32], in0=v3(TA), in1=rb, op=ALU.mult), r=("t:TA", "t:rstd"), w=("t:KN",))
                sch.add("dve", lambda e, rb=rb: e.tensor_tensor(out=KN[:, :, 32:64], in0=v3(TC), in1=rb, op=ALU.mult), r=("t:TC", "t:rstd"), w=("t:KN",))
                btv = BT[:, 0:384].rearrange("p (a b) -> p a b", a=3, b=128)
                sch.add("pe", lambda e: [e.transpose(btv[:, b, :], QN[:, b * 128:(b + 1) * 128], ident) for b in range(3)],
                        r=("t:KN", "c:ident"), w=("BT",))
                cs = slice(tt * 128, (tt + 1) * 128)
                sch.add("dve", lambda e, cs=cs: e.tensor_copy(out=KC[:, cs], in_=btv[:, 0, :]), r=("BT",), w=("u2:KC",))
                sch.add("dve", lambda e, cs=cs: e.tensor_copy(out=KAs[0:64, 0, cs], in_=btv[0:64, 1, :]), r=("BT",), w=("x1:KAs",))
                sch.add("dve", lambda e, cs=cs: e.tensor_copy(out=KAs[0:64, 1, cs], in_=btv[64:128, 1, :]), r=("BT",), w=("x1:KAs",))
                sch.add("dve", lambda e, cs=cs: e.tensor_copy(out=KAw[0:64, 0, cs], in_=btv[0:64, 2, :]), r=("BT",), w=("x1:KAw",))
                sch.add("dve", lambda e, cs=cs: e.tensor_copy(out=KAw[0:64, 1, cs], in_=btv[64:128, 2, :]), r=("BT",), w=("x1:KAw",))
            return s_mm, s_copy, s_A, s_B
        pt_ = [p2_tile(tt) for tt in range(16)]
        pt_[0][0](); pt_[0][1]()
        for tt in range(16):
            if tt + 1 < 16:
                pt_[tt + 1][0]()
            pt_[tt][2]()
            if tt + 1 < 16:
                pt_[tt + 1][1]()
            pt_[tt][3]()
        for c in range(4):
            cc = slice(c * 512, (c + 1) * 512)
            sch.add("pe", lambda e, cc=cc: [mm(e, PB[2][:, :], WA[:, kc, 128:256], hT[:, kc, cc], kc == 0, kc == 7) for kc in range(8)],
                    r=("m:hT%d" % c, "x1:Wsm"), w=("B2",))
            sch.add("act", lambda e, cc=cc: e.activation(out=VC[:, cc], in_=PB[2][:, :], func=AF.Copy), r=("B2",), w=("u2:VC",))
            sch.add("pe", lambda e, cc=cc: [mm(e, PB[3][0:48, :], WA[:, kc, 768:816], hT[:, kc, cc], kc == 0, kc == 7) for kc in range(8)],
                    r=("m:hT%d" % c, "x1:Wsm"), w=("B3",))
            sch.add("act", lambda e, cc=cc: e.activation(out=G[0:48, cc], in_=PB[3][0:48, :], func=AF.Sigmoid), r=("B3",), w=("x1:G",))
        if s == 0:
            dump("KAs", KAs, [128, 2, S], BF16, reads=("x1:KAs", "x1:KAs_e"))
            dump("KAw", KAw, [128, 2, S], BF16, reads=("x1:KAw",))
            dump("KC", KC, [128, S], BF16, reads=("u2:KC",))
            dump("VC", VC, [128, S], BF16, reads=("u2:VC",))
            dump("VA", VA.rearrange("p a b c d -> p (a b c d)"), [128, 2 * 2 * 16 * 128], BF16, reads=("x1:VA", "x1:VA1"))
            dump("G", G[0:48], [48, S], BF16, reads=("x1:G",))
        if stop_after <= 2:
            return
        wload(w1, None, "x1:Wsm", "wsm", 8, s, "w1", cast_fn=lambda: sch.add("pool", lambda e: [e.dma_start(out=w1[:, kv, 8 * i:8 * (i + 1), :], in_=dr["cmp_w1"][:, kv, 8 * i:8 * (i + 1), :]) for kv in range(2) for i in range(4)], w=("x1:Wsm",), dma=("wsm", 8)))
        if s == 0:
            def chm(e):
                ins = []
                for kv in range(2):
                    for l in range(32):
                        ins.append(mm(e, PB[6][:, kv:kv + 1], w1[0:64, kv, l, :], posT[0:64, kv, l:l + 1], l == 0, l == 31))
                return ins
            sch.add("pe", chm, r=("x1:Wsm", "c:posT"), w=("B6",))
            sch.add("dve", lambda e: e.tensor_tensor(out=chid, in0=PB[6][:, 0:2], in1=b1, op=ALU.add), r=("B6", "c:b1"), w=("c:chid",))
        for kv in range(2):
            for g in range(2):
                src = KC if kv == 0 else VC
                srcn = "u2:KC" if kv == 0 else "u2:VC"
                pr = slice(g * 64, (g + 1) * 64)

                def cm(e, src=src, pr=pr, kv=kv):
                    return [mm(e, PB[4][:, 0:127], w1[pr, kv, l, :], src[pr, l:l + 16 * 126 + 1:16], l == 0, l == 31) for l in range(32)]
                sch.add("pe", cm, r=(srcn, "x1:Wsm"), w=("B4",))
                xg = XG[:, 0:127]; x2 = X2[:, 0:127]; x3 = X3[:, 0:127]
                sch.add("act", lambda e, kv=kv: e.activation(out=xg, in_=PB[4][:, 0:127], func=AF.Identity, bias=chid[:, kv:kv + 1]),
                        r=("B4", "c:chid"), w=("u2:XG",))
                sch.add("dve", lambda e: e.tensor_tensor(out=x2, in0=xg, in1=xg, op=ALU.mult), r=("u2:XG",), w=("u2:X2",))
                sch.add("dve", lambda e: e.tensor_tensor(out=x3, in0=x2, in1=xg, op=ALU.mult), r=("u2:X2", "u2:XG"), w=("u2:X3",))
                sch.add("dve", lambda e: e.scalar_tensor_tensor(out=x2, in0=x3, scalar=0.044715, in1=xg, op0=ALU.mult, op1=ALU.add),
                        r=("u2:X3", "u2:XG"), w=("u2:X2",))
                sch.add("act", lambda e: e.activation(out=x3, in_=x2, func=AF.Sigmoid, scale=1.5957691216057308), r=("u2:X2",), w=("u2:X3",))
                sch.add("dve", lambda e: e.tensor_tensor(out=HID[:, 0:127], in0=xg, in1=x3, op=ALU.mult), r=("u2:XG", "u2:X3"), w=("u2:HID",))
                if kv == 0:
                    sch.add("pe", lambda e: mm(e, PB[5][0:64, 0:127], w2[:, 0, :], HID[:, 0:127], True, True), r=("u2:HID", "c:w2"), w=("B5",))
                    sch.add("act", lambda e, g=g: e.activation(out=KCMP[0:64, g, 0:127], in_=PB[5][0:64, 0:127], func=AF.Copy), r=("B5",), w=("m:KCMP",))
                else:
                    sch.add("pe", lambda e: mm(e, PB[5][0:127, 0:64], HID[:, 0:127], w2[:, 1, :], True, True), r=("u2:HID", "c:w2"), w=("B5",))
                    sch.add("act", lambda e, g=g: e.activation(out=VCMP[0:127, g, 0:64], in_=PB[5][0:127, 0:64], func=AF.Copy), r=("B5",), w=("m:VCMP",))
        if s == 0:
            dump("KCMP", KCMP, [128, 2, 128], BF16, reads=("m:KCMP",))
            dump("VCMP", VCMP, [128, 2, 128], BF16, reads=("m:VCMP",))
        if stop_after <= 3:
            return
        sch.barrier(UPFX)
        WQ = Wsm
        wload(WQ, dr["w_in"][:, :, 512:1536], "x1:Wsm", "wsm", 8, s, "wq")
        def attn_chunk(c):
            cc = slice(c * 512, (c + 1) * 512)
            def q_tile(tl):
                tt = 4 * c + tl
                ts_ = slice(tt * 128, (tt + 1) * 128)
                Z3 = ZQ.rearrange("p (h d) -> p h d", h=16, d=64)
                QN3 = QN.rearrange("p (h d) -> p h d", h=16, d=64)
                wq_ap = qkw[:, 0, :].unsqueeze(1).to_broadcast([128, 16, 64])
                btv = BT[:, 0:1024].rearrange("p (a b) -> p a b", a=8, b=128)
                QAe = QA.rearrange("p (i two) q -> p two i q", two=2)
                ls = slice(tl * 128, (tl + 1) * 128)

                def s_mm():
                    def qmm(e):
                        ins = []
                        for hf in range(2):
                            for kc in range(8):
                                ins.append(mm(e, PB[5 + hf][:, :], hT[:, kc, ts_], WQ[:, kc, hf * 512:(hf + 1) * 512], kc == 0, kc == 7))
                        return ins
                    sch.add("pe", qmm, r=("m:hT%d" % c, "x1:Wsm"), w=("B5", "B6"))

                def s_copy():
                    sch.add("act", lambda e: e.activation(out=ZQ[:, 0:512], in_=PB[5][:, :], func=AF.Copy), r=("B5",), w=("t:ZQ",))
                    sch.add("act", lambda e: e.activation(out=ZQ[:, 512:1024], in_=PB[6][:, :], func=AF.Copy), r=("B6",), w=("t:ZQ",))

                def s_A():
                    rope_A(Z3, "t:ZQ", 16, wq_ap, (1.0, 64 * EPS))

                def s_B():
                    rope_B(16, tt, QN3, "t:QN")
                    sch.add("pe", lambda e: [e.transpose(btv[:, i, :], QN[:, i * 128:(i + 1) * 128], ident) for i in range(8)],
                            r=("t:QN", "c:ident"), w=("BT",))
                    sch.add("dve", lambda e: e.tensor_copy(out=QAe[0:64, 0, :, ls], in_=btv[0:64, :, :]), r=("BT",), w=("x1:QA",))
                    sch.add("dve", lambda e: e.tensor_copy(out=QAe[0:64, 1, :, ls], in_=btv[64:128, :, :]), r=("BT",), w=("x1:QA",))
                return s_mm, s_copy, s_A, s_B
            qt = [q_tile(tl) for tl in range(4)]
            qt[0][0](); qt[0][1]()
            for tl in range(4):
                if tl + 1 < 4:
                    qt[tl + 1][0]()
                qt[tl][2]()
                if tl + 1 < 4:
                    qt[tl + 1][1]()
                qt[tl][3]()
            if s == 0 and c == 1:
                dump("QA", QA.rearrange("p a b -> p (a b)"), [128, 16 * 512], BF16, reads=("x1:QA",))
            if stop_after <= 4:
                return
            SB_ = [0, 1, 2]
            OB_ = [3, 4]
            for g in range(2):
                def stA(hl, g=g):
                    h = g * 8 + hl
                    sb = PB[SB_[h % 3]]; sbn = "B%d" % SB_[h % 3]
                    pt = PT[h % 3]; ptn = "t:PT%d" % (h % 3)
                    sch.add("pe", lambda e: [mm(e, sb[:, :], KCMP[0:64, g, :], QA[0:64, h, :], True, False),
                                             mm(e, sb[:, :], ident, cmpmask[:, cc], False, True)],
                            r=("m:KCMP", "x1:QA", "c:ident", "c:cmpmask"), w=(sbn,))
                    sch.add("act", lambda e: e.activation(out=pt, in_=sb[:, :], func=AF.Exp), r=(sbn,), w=(ptn,))

                def stB(hl, g=g):
                    h = g * 8 + hl
                    pt = PT[h % 3]; ptn = "t:PT%d" % (h % 3)
                    ecn = ECN[h % 2]; ecnn = "t:ECN%d" % (h % 2)
                    sch.add("pe", lambda e: mm(e, PB[5][:, :], ones, pt, True, True), r=(ptn, "c:ones"), w=("B5",))
                    sch.add("act", lambda e: e.activation(out=RR, in_=PB[5][:, :], func=AF.Ln, bias=1e-18), r=("B5",), w=("t:RR",))
                    sch.add("act", lambda e: e.activation(out=R2, in_=RR, func=AF.Exp, scale=-1.0), r=("t:RR",), w=("t:R2",))
                    sch.add("dve", lambda e: e.tensor_tensor(out=ecn, in0=pt, in1=R2, op=ALU.mult), r=(ptn, "t:R2"), w=(ecnn,))
                    if hl == 0:
                        sch.add("pool", lambda e: e.tensor_copy(out=PSC, in_=ecn), r=(ecnn,), w=("t:PSC",))
                    else:
                        sch.add("pool", lambda e: e.tensor_tensor(out=PSC, in0=PSC, in1=ecn, op=ALU.add), r=(ecnn, "t:PSC"), w=("t:PSC",))

                def stC(hl, g=g):
                    h = g * 8 + hl
                    ob = PB[OB_[h % 2]]; obn = "B%d" % OB_[h % 2]
                    ecn = ECN[h % 2]; ecnn = "t:ECN%d" % (h % 2)
                    sch.add("pe", lambda e: mm(e, ob[0:64, :], VCMP[:, g, 0:64], ecn, True, True), r=(ecnn, "m:VCMP"), w=(obn,))
                    ci = 3 * h + 0
                    sch.add("pe", lambda e: mm(e, PB[6][0:64, :], bsel[0:48, ci * 64:(ci + 1) * 64], G[0:48, cc], True, True),
                            r=("x1:G", "c:bsel"), w=("B6",))
                    sch.add("act", lambda e: e.activation(out=GSB[0:64, :], in_=PB[6][0:64, :], func=AF.Copy), r=("B6",), w=("t:GSB",))
                    pr = slice((h % 2) * 64, (h % 2) * 64 + 64)
                    sch.add("dve", lambda e: e.tensor_tensor(out=OT[pr, h // 2, cc], in0=ob[0:64, :], in1=GSB[0:64, :], op=ALU.mult),
                            r=(obn, "t:GSB"), w=("m:OT%d_%d" % (c, h),))
                for k in range(8 + 2):
                    if k < 8:
                        stA(k)
                    if 0 <= k - 1 < 8:
                        stB(k - 1)
                    if 0 <= k - 2 < 8:
                        stC(k - 2)
                sch.add("pool", lambda e: e.tensor_copy(out=PSCb, in_=PSC), r=("t:PSC",), w=("t:PSCb",))
                impv = PB[5][:, 0:128].rearrange("p (a b) -> p a b", a=4, b=32)
                sch.add("pe", lambda e: [mm(e, impv[:, qb, :], PSCb[:, qb * 128:(qb + 1) * 128], ov, True, True) for qb in range(4)],
                        r=("t:PSCb", "c:ov"), w=("B5",))
                sch.add("dve", lambda e, c=c: e.tensor_tensor(out=SC, in0=impv, in1=fb[:, 4 * c:4 * c + 4, :], op=ALU.add), r=("B5", "c:fb"), w=("t:SC",))
                for qb in range(4):
                    sch.add("dve", lambda e, qb=qb: e.max(out=M8[:, qb, :], in_=SC[:, qb, :]), r=("t:SC",), w=("t:M8",))
                    sch.add("dve", lambda e, qb=qb: e.tensor_scalar(out=SBf[:, qb, :], in0=SC[:, qb, :], scalar1=M8[:, qb, 7:8], scalar2=1.0,
                                                                    op0=ALU.is_ge, op1=ALU.subtract), r=("t:SC", "t:M8"), w=("t:SBf",))
                sch.add("dve", lambda e: e.tensor_scalar(out=SBb, in0=SBf, scalar1=-NEGB, scalar2=None, op0=ALU.mult), r=("t:SBf",), w=("t:SBb",))
                sch.add("pe", lambda e: [e.transpose(BT[0:32, qb * 128:(qb + 1) * 128], SBb[:, qb, :], ident) for qb in range(4)],
                        r=("t:SBb", "c:ident"), w=("BT",))
                sch.add("dve", lambda e, g=g: e.tensor_copy(out=QA[64:96, g * 8:(g + 1) * 8, :],
                                                            in_=BT[0:32, 0:512].unsqueeze(1).to_broadcast([32, 8, 512])),
                        r=("BT",), w=("x1:QAs",))
                if s == 0 and c == 1 and g == 0:
                    dump("SC", SC.rearrange("p a b -> p (a b)"), [128, 128], F32, reads=("t:SC",))
                    dump("SBf", SBf.rearrange("p a b -> p (a b)"), [128, 128], F32, reads=("t:SBf",))
            if stop_after <= 5:
                return
            jobs = []
            for h in range(16):
                g = h // 8
                tl_ = []
                for kt in range(0, 4 * c + 4):
                    j = kt - 4 * c
                    lo = 0 if j < 0 else 128 * j
                    tl_.append(dict(kt=kt, lo=lo, hi=512, mask=(None if j < 0 else (0, lo))))
                jobs.append(dict(h=h, g=g, br=1, tiles=tl_))
                tl_ = []
                order = [4] + ([0, 1, 2, 3] if c >= 1 else []) + [5, 6, 7]
                for i in order:
                    kt = 4 * c - 4 + i
                    if i <= 3:
                        tl_.append(dict(kt=kt, lo=0, hi=128 * (i + 1), mask=(1, 128 * i)))
                    else:
                        tl_.append(dict(kt=kt, lo=128 * (i - 4), hi=512, mask=(0, 128 * (i - 4))))
                jobs.append(dict(h=h, g=g, br=2, tiles=tl_))
            flat = []
            for jb_i, jb in enumerate(jobs):
                for ti, t in enumerate(jb["tiles"]):
                    flat.append((jb_i, ti))
            srot = [0]

            def rec_S(k):
                jb_i, ti = flat[k]
                jb = jobs[jb_i]; t = jb["tiles"][ti]
                si = k % 3
                sb = PB[SB_[si]]; t["si"] = si
                lo, hi, kt, g, h = t["lo"], t["hi"], t["kt"], jb["g"], jb["h"]
                ks = slice(kt * 128, (kt + 1) * 128)
                if jb["br"] == 1:
                    lhs = KAs[0:96, g, ks]; rhs = QA[0:96, h, lo:hi]; rn = ("x1:KAs", "x1:KAs_e", "x1:QA", "x1:QAs")
                else:
                    lhs = KAw[0:64, g, ks]; rhs = QA[0:64, h, lo:hi]; rn = ("x1:KAw", "x1:QA")
                mk = t["mask"]

                def f(e):
                    ins = [mm(e, sb[:, lo:hi], lhs, rhs, True, mk is None)]
                    if mk is not None:
                        ins.append(mm(e, sb[:, mk[1]:mk[1] + 128], ident, tri[:, mk[0] * 128:(mk[0] + 1) * 128], False, True))
                    return ins
                sch.add("pe", f, r=rn + ("c:ident", "c:tri"), w=("B%d" % SB_[si],))
                pt = PT[si]
                sch.add("act", lambda e: e.activation(out=pt[:, lo:hi], in_=sb[:, lo:hi], func=AF.Exp), r=("B%d" % SB_[si],), w=("t:PT%d" % si,))

            def rec_PV(k):
                jb_i, ti = flat[k]
                jb = jobs[jb_i]; t = jb["tiles"][ti]
                si = t["si"]
                oi = jb_i % 2
                ob = PB[OB_[oi]]; obn = "B%d" % OB_[oi]
                lo, hi, kt, g, h, br = t["lo"], t["hi"], t["kt"], jb["g"], jb["h"], jb["br"]
                pt = PT[si]
                first = ti == 0
                last = ti == len(jb["tiles"]) - 1
                sch.add("pe", lambda e: mm(e, ob[:, lo:hi], VA[:, br - 1, g, kt, :], pt[:, lo:hi], first, last),
                        r=("t:PT%d" % si, "x1:VA", "x1:VA1"), w=(obn,))
                if last:
                    ci = 3 * h + br
                    sch.add("pe", lambda e: mm(e, PB[6][0:64, :], bsel[0:48, ci * 64:(ci + 1) * 64], G[0:48, cc], True, True),
                            r=("x1:G", "c:bsel"), w=("B6",))
                    sch.add("dve", lambda e: e.reciprocal(out=RCPb[0:64, :], in_=ob[64:128, :]), r=(obn,), w=("t:RCP",))
                    sch.add("dve", lambda e: e.tensor_tensor(out=R2[0:64, :], in0=RCPb[0:64, :], in1=PB[6][0:64, :], op=ALU.mult), r=("t:RCP", "B6"), w=("t:R2",))
                    pr = slice((h % 2) * 64, (h % 2) * 64 + 64)
                    if br == 1:
                        sch.add("dve", lambda e: e.tensor_tensor(out=U1[pr, :], in0=ob[0:64, :], in1=R2[0:64, :], op=ALU.mult), r=(obn, "t:R2"), w=("t:U1",))
                    else:
                        sch.add("dve", lambda e: e.tensor_tensor(out=U2[pr, :], in0=ob[0:64, :], in1=R2[0:64, :], op=ALU.mult), r=(obn, "t:R2"), w=("t:U2",))
                        otn = "m:OT%d_%d" % (c, h)
                        sch.add("pool", lambda e: e.tensor_tensor(out=UT[pr, :], in0=U1[pr, :], in1=U2[pr, :], op=ALU.add), r=("t:U1", "t:U2"), w=("t:UT",))
                        sch.add("pool", lambda e: e.tensor_tensor(out=OT[pr, h // 2, cc], in0=OT[pr, h // 2, cc], in1=UT[pr, :], op=ALU.add),
                                r=("t:UT", otn), w=(otn,))
            DEPTH = 2
            for k in range(len(flat) + DEPTH):
                if k < len(flat):
                    rec_S(k)
                if k - DEPTH >= 0:
                    rec_PV(k - DEPTH)
        for c_i in range(4):
            attn_chunk(c_i)
        if s == 0:
            dump("OT", OT, [128, 8, S], BF16, reads=["m:OT%d_%d" % (c_, h_) for c_ in range(4) for h_ in range(16)])
        if stop_after <= 6:
            return
        sch.barrier(("x1:", "x2:", "t:", "u1:", "u2:"))
        wload(Wp, dr["w_in"][:, :, 0:512], "x2:Wp", "wsm", 8, s, "wp")
        sch.add("dve", lambda e: e.memset(UF[:, 0:16], 0.0), w=("x2:UF",))
        sch.add("dve", lambda e: e.memset(SA[:, 0:16], 0.0), w=("x2:SA",))
        sch.add("dve", lambda e: e.memset(SBp[:, 0:16], 0.0), w=("x2:SB",))
        for g in range(4):
            for c in range(4):
                cc = slice(c * 512, (c + 1) * 512)
                bi = nextbank([0, 1, 2, 3])
                sch.add("pe", lambda e, bi=bi, g=g, cc=cc: [mm(e, PB[bi][:, :], Wp[:, kc, g * 128:(g + 1) * 128], hT[:, kc, cc], kc == 0, kc == 7) for kc in range(8)],
                        r=("m:hT%d" % c, "x2:Wp"), w=("B%d" % bi,))
                sch.add("act", lambda e, bi=bi, c=c: e.activation(out=UF[:, 16 + c * 512:16 + (c + 1) * 512], in_=PB[bi][:, :], func=AF.Copy),
                        r=("B%d" % bi,), w=("x2:UF",))
            bufs = [(UF, "x2:UF"), (SA, "x2:SA"), (SBp, "x2:SB")]
            srcb = bufs[0]
            for lvl in range(g + 1):
                sh = 1 << lvl
                dstb = bufs[1 + (lvl % 2)]
                eng = "dve" if lvl % 2 == 0 else "pool"
                sch.add(eng, lambda e, srcb=srcb, dstb=dstb, sh=sh: e.tensor_tensor(out=dstb[0][:, 16:16 + S], in0=srcb[0][:, 16:16 + S],
                                                                                  in1=srcb[0][:, 16 - sh:16 + S - sh], op=ALU.add),
                        r=(srcb[1],), w=(dstb[1],))
                srcb = dstb
            wdw = 2 << g
            sch.add("dve", lambda e, srcb=srcb, g=g: e.tensor_tensor(out=srcb[0][:, 16:32], in0=srcb[0][:, 16:32], in1=poolcorr[:, g, :], op=ALU.mult),
                    r=(srcb[1], "c:poolcorr"), w=(srcb[1],))
            sch.add("dve", lambda e, srcb=srcb, wdw=wdw: e.scalar_tensor_tensor(out=PLb, in0=srcb[0][:, 16:16 + S], scalar=1.0 / wdw, in1=UF[:, 16:16 + S],
                                                                             op0=ALU.mult, op1=ALU.subtract), r=(srcb[1], "x2:UF"), w=("x2:PLb",))
            for c in range(4):
                cc = slice(c * 512, (c + 1) * 512)
                bi = nextbank([4, 5, 6])
                sch.add("pe", lambda e, bi=bi, g=g, cc=cc: mm(e, PB[bi][:, :], pool_w[:, g, :], PLb[:, cc], True, True), r=("x2:PLb", "c:pool_w"), w=("B%d" % bi,))
                sch.add("act", lambda e, bi=bi, g=g, cc=cc: e.activation(out=YP[:, g, cc], in_=PB[bi][:, :], func=AF.Copy, scale=pool_scale[:, g:g + 1]),
                        r=("B%d" % bi, "c:pool_scale"), w=("m:YP",))
        if s == 0:
            dump("YP", YP, [128, 4, S], BF16, reads=("m:YP",))
        if stop_after <= 7:
            return
        sch.barrier(("x2:", "x3:", "t:"))
        wload(Wmg, dr["w_in"][:, :, 2352:4400], "x3:Wmg", "wmg", 8, s)
        wload(Wpb, dr["w_pool_br"], "x3:Wpb", "wpb", 4, s)
        wload(Wab, dr["w_attn_br"], "x3:Wab", "wab", 8, s)
        wload(Wo, dr["w_o"], "x3:Wo", "wo", 8, s)
        for c in range(4):
            cc = slice(c * 512, (c + 1) * 512)
            otn = ["m:OT%d_%d" % (c, h_) for h_ in range(16)]
            for dc in range(8):
                dsl = slice(dc * 128, (dc + 1) * 128)
                b_yp, b_ya, b_m0, b_m1 = [nextbank([0, 1, 2, 3, 4, 5, 6]) for _ in range(4)]
                sch.add("pe", lambda e, b=b_m0, dsl=dsl, cc=cc: [mm(e, PB[b][:, :], Wmg[:, kc, dsl], hT[:, kc, cc], kc == 0, kc == 7) for kc in range(8)],
                        r=("m:hT%d" % c, "x3:Wmg"), w=("B%d" % b_m0,))
                sch.add("pe", lambda e, b=b_m1, dc=dc, cc=cc: [mm(e, PB[b][:, :], Wmg[:, kc, 1024 + dc * 128:1024 + (dc + 1) * 128], hT[:, kc, cc], kc == 0, kc == 7) for kc in range(8)],
                        r=("m:hT%d" % c, "x3:Wmg"), w=("B%d" % b_m1,))
                sch.add("pe", lambda e, b=b_yp, dsl=dsl, cc=cc: [mm(e, PB[b][:, :], Wpb[:, g, dsl], YP[:, g, cc], g == 0, g == 3) for g in range(4)],
                        r=("m:YP", "x3:Wpb"), w=("B%d" % b_yp,))
                sch.add("pe", lambda e, b=b_ya, dsl=dsl, cc=cc: [mm(e, PB[b][:, :], Wab[:, i, dsl], OT[:, i, cc], i == 0, i == 7) for i in range(8)],
                        r=tuple(otn) + ("x3:Wab",), w=("B%d" % b_ya,))
                sch.add("act", lambda e, b=b_m0: e.activation(out=SG[0], in_=PB[b][:, :], func=AF.Sigmoid), r=("B%d" % b_m0,), w=("x3:SG0",))
                sch.add("act", lambda e, b=b_m1: e.activation(out=SG[1], in_=PB[b][:, :], func=AF.Sigmoid), r=("B%d" % b_m1,), w=("x3:SG1",))
                sch.add("dve", lambda e, b=b_yp: e.tensor_tensor(out=T0, in0=SG[0], in1=PB[b][:, :], op=ALU.mult), r=("x3:SG0", "B%d" % b_yp), w=("x3:T0",))
                sch.add("dve", lambda e, b=b_ya: e.tensor_tensor(out=T1, in0=SG[1], in1=PB[b][:, :], op=ALU.mult), r=("x3:SG1", "B%d" % b_ya), w=("x3:T1",))
                sch.add("pool", lambda e, dc=dc: e.tensor_tensor(out=MT[:, dc, :], in0=T0, in1=T1, op=ALU.add), r=("x3:T0", "x3:T1"), w=("x3:MT",))
            def ld6(tt):
                sch.add("sp", lambda e: e.dma_start(out=xs6[tt % 2], in_=x_d[s, tt * 128:(tt + 1) * 128, :]), w=("x3:xs%d" % (tt % 2),), dma=("xs6%d" % (tt % 2), 1))
            for tl in range(4):
                tt = 4 * c + tl
                ls = slice(tl * 128, (tl + 1) * 128)
                xt = xs6[tt % 2]; xn = "x3:xs%d" % (tt % 2)
                if tl == 0:
                    ld6(tt)
                if tl + 1 < 4:
                    ld6(tt + 1)
                b0, b1_ = nextbank([0, 1, 2, 3, 4, 5, 6]), nextbank([0, 1, 2, 3, 4, 5, 6])
                for hf, b in ((0, b0), (1, b1_)):
                    sch.add("pe", lambda e, b=b, hf=hf, ls=ls: [mm(e, PB[b][:, :], MT[:, dc, ls], Wo[:, dc, hf * 512:(hf + 1) * 512], dc == 0, dc == 7) for dc in range(8)],
                            r=("x3:MT", "x3:Wo"), w=("B%d" % b,))
                    sch.add("dve", lambda e, b=b, hf=hf, xt=xt: e.tensor_tensor(out=xt[:, hf * 512:(hf + 1) * 512], in0=xt[:, hf * 512:(hf + 1) * 512], in1=PB[b][:, :], op=ALU.add),
                            r=(xn, "B%d" % b), w=(xn,))
                sch.add("sp", lambda e, xt=xt, tt=tt: e.dma_start(out=(x1_d if stop_after > 8 else out_d)[s, tt * 128:(tt + 1) * 128, :], in_=xt), r=(xn,), w=("o:%d_%d" % (s, tt),), dma=("xo6%d" % (tt % 2), 1))
        if stop_after <= 8:
            return
        sch.barrier(("m:", "x1:", "x2:", "x3:", "f:", "t:", "u1:", "u2:"))
        wload(Wup, None, "f:Wup", "wup", 32, s, "wup", cast_fn=lambda: sch.add("pool", lambda e: [e.dma_start(out=Wup[:, k, 1376 * i:1376 * (i + 1)], in_=dr["w_up"][:, k, 1376 * i:1376 * (i + 1)]) for k in range(8) for i in range(4)], w=("f:Wup",), dma=("wup", 32)))
        wload(Wdn, dr["w_down"], "f:Wdn", "wdn", 22, s)
        sch.add("dve", lambda e: e.memset(HALO, 0.0), w=tuple("f:HALO%d" % i for i in range(22)))
        DB = [0, 1, 2, 3]
        UB = [4, 5, 6]
        def pf(c8n, stage):
            for tl in range(2):
                tt = 2 * c8n + tl
                xi = (c8n % 2) * 2 + tl
                xt = X1S[xi]; xn = "f:x1s%d" % xi
                if stage == 0:
                    sch.add("sp", lambda e, xt=xt, tt=tt: e.dma_start(out=xt, in_=x1_d[s, tt * 128:(tt + 1) * 128, :]), r=("o:%d_%d" % (s, tt),), w=(xn,), dma=("x1s%d" % xi, 1))
                elif stage == 1:
                    norm_A(xt, xn, hn7[tl], "f:hn%d" % tl, 1 + tl)
                elif stage == 2:
                    norm_B1(xt, xn, hn7[tl], "f:hn%d" % tl, 1 + tl)
                else:
                    norm_B2(hn7[tl], "f:hn%d" % tl, 1, H2[c8n % 2], "f:H2_%d" % (c8n % 2), tl * 128)
        for st_ in range(4):
            pf(0, st_)
        for c8 in range(8):
            H2c = H2[c8 % 2]; h2n = "f:H2_%d" % (c8 % 2)

            def rec_up(fc, H2c=H2c, h2n=h2n):
                rows = 128 if fc < 21 else 64
                par = fc % 3
                b = UB[par]; bn = "B%d" % b
                ue = UE[par]; uen = "f:UE%d" % par

                def upmm(e):
                    ins = []
                    for gv in range(2):
                        col0 = gv * D_FF + fc * 128
                        for kc in range(8):
                            ins.append(mm(e, PB[b][0:rows, gv * 256:(gv + 1) * 256], Wup[:, kc, col0:col0 + rows], H2c[:, kc, :], kc == 0, kc == 7))
                    return ins
                sch.add("pe", upmm, r=(h2n, "f:Wup"), w=(bn,))
                hn_ = "f:HALO%d" % fc
                sch.add("pool", lambda e: e.tensor_copy(out=ue[0:rows, :, 0:2], in_=HALO[0:rows, :, fc, :]), r=(hn_,), w=(uen,))
                sch.add("act", lambda e: e.activation(out=ue[0:rows, :, 2:258], in_=PB[b][0:rows, :].rearrange("p (g t) -> p g t", g=2, t=256), func=AF.Copy),
                        r=(bn,), w=(uen,))
                sch.add("pool", lambda e: e.tensor_copy(out=HALO[0:rows, :, fc, :], in_=ue[0:rows, :, 256:258]), r=(uen,), w=(hn_,))

            def rec_conv(fc):
                rows = 128 if fc < 21 else 64
                par = fc % 3
                ue = UE[par]; uen = "f:UE%d" % par
                cxs = ((0, CG[par], "f:CG%d" % par), (1, CV[par], "f:CV%d" % par))
                cw = lambda tap, gv: conv[0:rows, gv, fc, tap:tap + 1]
                for gv, cx, cxn in cxs:
                    sch.add("dve", lambda e, cx=cx, gv=gv: e.tensor_scalar(out=cx[0:rows, :], in0=ue[0:rows, gv, 2:258], scalar1=cw(2, gv), scalar2=cw(3, gv), op0=ALU.mult, op1=ALU.add),
                            r=(uen, "c:conv"), w=(cxn,))
                for gv, cx, cxn in cxs:
                    sch.add("dve", lambda e, cx=cx, gv=gv: e.scalar_tensor_tensor(out=cx[0:rows, :], in0=ue[0:rows, gv, 1:257], scalar=cw(1, gv), in1=cx[0:rows, :], op0=ALU.mult, op1=ALU.add),
                            r=(uen, cxn, "c:conv"), w=(cxn,))
                for gv, cx, cxn in cxs:
                    sch.add("dve", lambda e, cx=cx, gv=gv: e.scalar_tensor_tensor(out=cx[0:rows, :], in0=ue[0:rows, gv, 0:256], scalar=cw(0, gv), in1=cx[0:rows, :], op0=ALU.mult, op1=ALU.add),
                            r=(uen, cxn, "c:conv"), w=(cxn,))

            def rec_act(fc):
                rows = 128 if fc < 21 else 64
                par = fc % 3
                sg = SGF[par]; af = AF_[par]
                sch.add("act", lambda e: e.activation(out=sg[0:rows, :], in_=CG[par][0:rows, :], func=AF.Silu), r=("f:CG%d" % par,), w=("f:SG%d" % par,))
                sch.add("pool", lambda e: e.tensor_tensor(out=af[0:rows, :], in0=sg[0:rows, :], in1=CV[par][0:rows, :], op=ALU.mult),
                        r=("f:SG%d" % par, "f:CV%d" % par), w=("f:A%d" % par,))

            def rec_down(fc):
                rows = 128 if fc < 21 else 64
                par = fc % 3
                af = AF_[par]

                def dmm(e):
                    ins = []
                    for tl in range(2):
                        for hf in range(2):
                            ins.append(mm(e, PB[DB[tl * 2 + hf]][:, :], af[0:rows, tl * 128:(tl + 1) * 128], Wdn[0:rows, fc, hf * 512:(hf + 1) * 512], fc == 0, fc == 21))
                    return ins
                sch.add("pe", dmm, r=("f:A%d" % par, "f:Wdn"), w=("B0", "B1", "B2", "B3"))
            for k in range(22 + 3):
                if k < 22:
                    rec_up(k)
                    rec_conv(k)
                if 0 <= k - 1 < 22:
                    rec_act(k - 1)
                if 0 <= k - 3 < 22:
                    rec_down(k - 3)
                if c8 + 1 < 8 and k in (2, 6, 9, 12):
                    pf(c8 + 1, (2, 6, 9, 12).index(k))
            for tl in range(2):
                tt = 2 * c8 + tl
                xi = (c8 % 2) * 2 + tl
                xt = X1S[xi]; xn = "f:x1s%d" % xi
                for hf in range(2):
                    b = DB[tl * 2 + hf]
                    sch.add("dve", lambda e, xt=xt, b=b, hf=hf: e.tensor_tensor(out=xt[:, hf * 512:(hf + 1) * 512], in0=xt[:, hf * 512:(hf + 1) * 512], in1=PB[b][:, :], op=ALU.add),
                            r=(xn, "B%d" % b), w=(xn,))
                sch.add("sp", lambda e, xt=xt, tt=tt: e.dma_start(out=out_d[s, tt * 128:(tt + 1) * 128, :], in_=xt), r=(xn,), w=("of:%d_%d" % (s, tt),), dma=("x1o%d" % xi, 1))
        sch.barrier(("m:", "x1:", "x2:", "x3:", "f:", "t:", "u1:", "u2:"))

    for s_i in range(nseq):
        do_seq(s_i)

    sch.finalize()
    print("ops:", len(sch.ops))
    sems = {}
    for k in sch.sem_keys():
        sems[k] = es.enter_context(nc.semaphore("s_%s_%s" % k))
    with nc.Block() as block:
        @block.sync
        def _(e):
            sch.emit("sp", e, sems)

        @block.tensor
        def _(e):
            sch.emit("pe", e, sems)

        @block.scalar
        def _(e):
            sch.emit("act", e, sems)

        @block.vector
        def _(e):
            sch.emit("dve", e, sems)

        @block.gpsimd
        def _(e):
            sch.emit("pool", e, sems)
    es.close()
    return nc, dbg_d


_CACHE = {}


def kernel(**inputs):
    x = np.asarray(inputs["x"], dtype=np.float32)
    B = x.shape[0]
    nseq = B // NCORES
    consts = host_consts()
    wts = host_weights(inputs)
    if "nc" not in _CACHE:
        _CACHE["nc"] = build(nseq=nseq)[0]
    nc = _CACHE["nc"]
    in_maps = []
    for c in range(NCORES):
        m = {"x": np.ascontiguousarray(x[c * nseq:(c + 1) * nseq])}
        m.update(consts)
        m.update(wts)
        in_maps.append(m)
    res = run_bass_kernel_spmd(nc, in_maps, core_ids=list(range(NCORES)))
    out = np.concatenate([np.asarray(r["out"], dtype=np.float32) for r in res.results], axis=0)
    return out
```

```python
import contextlib
import os
import numpy as np
import ml_dtypes
import concourse.bass as bass
import concourse.mybir as mybir
from concourse.bass_utils import run_bass_kernel_spmd

F32 = mybir.dt.float32
BF16 = mybir.dt.bfloat16
ALU = mybir.AluOpType
AF = mybir.ActivationFunctionType
AX = mybir.AxisListType

S = 2048
D = 1024
NCORES = 8
EPS = 1e-6
NEGB = -30000.0
D_FF = 2752
ENGS = ("pe", "act", "dve", "pool", "sp")


class Op:
    __slots__ = ("eng", "fn", "dma", "deps", "signal", "val", "w", "waits")


class Sched:
    def __init__(self):
        self.ops = []
        self.lastw = {}
        self.readers = {}
        self.dmacum = {}
        self.barriers = []

    def add(self, eng, fn, r=(), w=(), dma=None):
        op = Op()
        op.eng = eng
        op.fn = fn
        op.dma = dma
        op.signal = False
        op.val = None
        op.deps = []
        seen = set()
        for n in tuple(r) + tuple(w):
            if n not in self.lastw:
                for pf, bop in reversed(self.barriers):
                    if n.startswith(pf):
                        self.lastw[n] = bop
                        self.readers.setdefault(n, [])
                        break
        for n in r:
            d = self.lastw.get(n)
            if d is not None and id(d) not in seen:
                seen.add(id(d))
                op.deps.append((d, "raw"))
        for n in w:
            d = self.lastw.get(n)
            if d is not None and id(d) not in seen:
                seen.add(id(d))
                op.deps.append((d, "waw"))
            for d in self.readers.get(n, ()):
                if id(d) not in seen:
                    seen.add(id(d))
                    op.deps.append((d, "war"))
        for n in r:
            self.readers.setdefault(n, []).append(op)
        for n in w:
            self.lastw[n] = op
            self.readers[n] = []
        if dma is not None:
            self.dmacum[dma[0]] = self.dmacum.get(dma[0], 0) + 16 * dma[1]
            op.val = self.dmacum[dma[0]]
        self.ops.append(op)
        return op

    def barrier(self, prefixes):
        names = [n for n in set(self.lastw) | set(self.readers) if n.startswith(prefixes)]
        bop = self.add("pool", lambda e: e.nop(), r=(), w=names)
        self.barriers.append((prefixes, bop))

    def finalize(self):
        for op in self.ops:
            op.w = []
            for d, kind in op.deps:
                if d.dma is not None:
                    op.w.append(d)
                elif d.eng == op.eng:
                    if op.dma is not None:
                        op.w.append(d)
                        d.signal = True
                    elif op.eng == "pe":
                        continue
                    elif kind == "raw":
                        op.w.append(d)
                        d.signal = True
                else:
                    op.w.append(d)
                    d.signal = True
        cnt = {}
        for op in self.ops:
            if op.dma is None and op.signal:
                cnt[op.eng] = cnt.get(op.eng, 0) + 1
                op.val = cnt[op.eng]
        waited = {e: {} for e in ENGS}
        for op in self.ops:
            ws = {}
            for d in op.w:
                key = ("dma", d.dma[0]) if d.dma is not None else ("eng", d.eng)
                if waited[op.eng].get(key, 0) >= d.val:
                    continue
                ws[key] = max(ws.get(key, 0), d.val)
            for k, v in ws.items():
                waited[op.eng][k] = v
            op.waits = list(ws.items())

    def sem_keys(self):
        return [("eng", e) for e in ENGS] + [("dma", s) for s in self.dmacum]

    def emit(self, eng_name, e, sems):
        for op in self.ops:
            if op.eng != eng_name:
                continue
            for key, v in op.waits:
                e.wait_ge(sems[key], v)
            insts = op.fn(e)
            if not isinstance(insts, (list, tuple)):
                insts = [insts]
            if op.dma is not None:
                assert len(insts) == op.dma[1], (len(insts), op.dma)
                for i in insts:
                    i.then_inc(sems[("dma", op.dma[0])], 16)
            elif op.signal:
                insts[-1].then_inc(sems[("eng", op.eng)], 1)
        if eng_name == "sp":
            for s, v in self.dmacum.items():
                e.wait_ge(sems[("dma", s)], v)


def host_consts():
    bf = ml_dtypes.bfloat16
    c = {}
    c["ident"] = np.eye(128, dtype=np.float32).astype(bf)
    half = 32
    freqs = (10000.0 ** (-np.arange(half, dtype=np.float32) / half)).astype(np.float32)
    pos = np.arange(S, dtype=np.float32)
    ang = (pos[:, None] * freqs[None, :]).astype(np.float32)
    cs = np.stack([np.cos(ang), np.sin(ang)], 0).astype(np.float32)
    c["cossin"] = np.ascontiguousarray(cs.reshape(2, 16, 128, 32).transpose(2, 0, 1, 3))
    n = np.arange(128)[:, None]
    q = np.arange(S)[None, :]
    c["cmpmask"] = np.where((n < 127) & (16 * n + 31 <= q), 0.0, NEGB).astype(bf)
    kk = np.arange(128)[:, None]
    qq = np.arange(128)[None, :]
    tri = np.concatenate([np.where(qq >= kk, 0.0, NEGB), np.where(qq < kk, 0.0, NEGB)], 1)
    c["tri"] = tri.astype(bf)
    j = np.arange(32)[:, None]
    k = np.arange(S)[None, :]
    c["esel"] = (k // 64 == j).astype(np.float32).astype(bf)
    tq = np.arange(S)[:, None]
    jb = np.arange(32)[None, :]
    cur = tq // 64
    forced = (jb == 0) | (jb == cur) | (jb == cur - 1)
    valid = jb * 64 <= tq
    fb = np.where(valid, 1000.0 * forced, -1e30).astype(np.float32)
    c["fb"] = np.ascontiguousarray(fb.reshape(16, 128, 32).transpose(1, 0, 2))
    ci = np.arange(128)[:, None]
    sj = np.arange(32)[None, :]
    ov = ((ci * 16 < (sj + 1) * 64) & (ci * 16 + 32 > sj * 64) & (ci < 127)).astype(np.float32)
    c["ov"] = ov.astype(bf)
    bs = np.zeros((48, 48, 64), np.float32)
    for i in range(48):
        bs[i, i, :] = 1.0
    c["bsel"] = bs.reshape(48, 48 * 64).astype(bf)
    pc = np.ones((128, 4, 16), np.float32)
    for g, w in enumerate((2, 4, 8, 16)):
        t = np.arange(16)
        pc[:, g, :] = (w / np.minimum(t + 1, w))[None, :]
    c["poolcorr"] = pc
    return c


def host_weights(inp):
    f = np.float32
    w = {}
    A = lambda a: np.ascontiguousarray(np.asarray(a, dtype=f))
    w["w_in"] = A(inp["w_in"][0].reshape(8, 128, 4400).transpose(1, 0, 2))
    w["w_pool_br"] = A(inp["w_pool_br"][0].reshape(4, 128, 1024).transpose(1, 0, 2))
    w["w_attn_br"] = A(inp["w_attn_br"][0].reshape(8, 128, 1024).transpose(1, 0, 2))
    w["w_o"] = A(inp["w_o"][0].reshape(8, 128, 1024).transpose(1, 0, 2))
    w["w_up"] = A(inp["w_up"][0].reshape(8, 128, 5504).transpose(1, 0, 2))
    wd = np.zeros((22 * 128, 1024), f)
    wd[:D_FF] = inp["w_down"][0]
    w["w_down"] = A(wd.reshape(22, 128, 1024).transpose(1, 0, 2))
    w["pool_w"] = A(inp["pool_w"][0].transpose(1, 0, 2))
    w1 = np.asarray(inp["cmp_w1"][0]).reshape(2, 32, 64, 128).transpose(2, 0, 1, 3)
    w["cmp_w1"] = A(np.concatenate([w1, w1], 0))
    w["cmp_w2"] = A(np.asarray(inp["cmp_w2"][0]).transpose(1, 0, 2))
    pt = np.asarray(inp["cmp_pos"][0]).transpose(2, 0, 1)
    w["cmp_posT"] = A(np.concatenate([pt, pt], 0))
    w["cmp_b1"] = A(np.asarray(inp["cmp_b1"][0]).T)
    nrm = np.stack([np.asarray(inp["attn_norm_w"][0]).reshape(8, 128).T,
                    np.asarray(inp["ffn_norm_w"][0]).reshape(8, 128).T], 1)
    w["nrm"] = A(nrm)
    w["pool_scale"] = A(np.asarray(inp["pool_scale"][0]).reshape(4, 128).T)
    qk = np.concatenate([np.asarray(inp["q_norm_w"][0])[None], np.asarray(inp["k_norm_w"][0])], 0)
    w["qkw"] = A(np.broadcast_to(qk[None], (128, 4, 64)))
    cw = np.asarray(inp["conv_w"][0])
    cb = np.asarray(inp["conv_b"][0])
    cv = np.zeros((128, 2, 22, 4), f)
    for gv in range(2):
        for fc in range(22):
            rows = 128 if fc < 21 else 64
            sl = slice(gv * D_FF + fc * 128, gv * D_FF + fc * 128 + rows)
            cv[:rows, gv, fc, 0:3] = cw[:, sl].T
            cv[:rows, gv, fc, 3] = cb[sl]
    w["conv"] = cv
    return w


DRAM_SPECS = {
    "w_in": ([128, 8, 4400], F32), "w_pool_br": ([128, 4, 1024], F32), "w_attn_br": ([128, 8, 1024], F32),
    "w_o": ([128, 8, 1024], F32), "w_up": ([128, 8, 5504], F32), "w_down": ([128, 22, 1024], F32),
    "pool_w": ([128, 4, 128], F32), "cmp_w1": ([128, 2, 32, 128], F32), "cmp_w2": ([128, 2, 64], F32),
    "cmp_posT": ([128, 2, 32], F32), "cmp_b1": ([128, 2], F32), "nrm": ([128, 2, 8], F32),
    "pool_scale": ([128, 4], F32), "qkw": ([128, 4, 64], F32), "conv": ([128, 2, 22, 4], F32),
    "ident": ([128, 128], BF16), "cossin": ([128, 2, 16, 32], F32), "cmpmask": ([128, 2048], BF16),
    "tri": ([128, 256], BF16), "esel": ([32, 2048], BF16), "fb": ([128, 16, 32], F32),
    "ov": ([128, 32], BF16), "bsel": ([48, 3072], BF16), "poolcorr": ([128, 4, 16], F32),
}


def build(nseq=4, stop_after=99, dbg=False):
    nc = bass.Bass("TRN2", target_bir_lowering=False)
    sch = Sched()
    dr = {}
    for name, (shape, dt) in DRAM_SPECS.items():
        dr[name] = nc.dram_tensor(name, shape, dt, kind="ExternalInput").ap()
    x_d = nc.dram_tensor("x", [nseq, S, D], F32, kind="ExternalInput").ap()
    out_d = nc.dram_tensor("out", [nseq, S, D], F32, kind="ExternalOutput").ap()
    x1_d = nc.dram_tensor("x1scratch", [nseq + 1, S, D], F32, kind="Internal").ap()[1:nseq + 1]
    dbg_d = {}

    es = contextlib.ExitStack()
    ARENA_B = 206 * 1024
    arena = es.enter_context(nc.sbuf_tensor("arena", [128, ARENA_B // 2], BF16))
    cur = [0]

    def alloc(shape, dt=BF16):
        n = 1
        for s_ in shape[1:]:
            n *= s_
        nb = n * (4 if dt == F32 else 2)
        nb = (nb + 63) // 64 * 64
        off = cur[0]
        cur[0] += nb
        assert cur[0] <= ARENA_B, ("arena overflow", cur[0])
        ap = arena[:, off // 2:(off + nb) // 2]
        if dt == F32:
            ap = ap.bitcast(F32)
        ap = ap[:, 0:n]
        if len(shape) == 3:
            ap = ap.rearrange("p (a b) -> p a b", a=shape[1], b=shape[2])
        elif len(shape) == 4:
            ap = ap.rearrange("p (a b c) -> p a b c", a=shape[1], b=shape[2], c=shape[3])
        elif len(shape) == 5:
            ap = ap.rearrange("p (a b c d) -> p a b c d", a=shape[1], b=shape[2], c=shape[3], d=shape[4])
        return ap

    PB = [es.enter_context(nc.psum_tensor("pb%d" % i, [128, 512], F32)) for i in range(7)]
    BT = es.enter_context(nc.psum_tensor("pbt", [128, 1024], BF16))
    pbrot = [0]

    def nextbank(pool):
        i = pool[pbrot[0] % len(pool)]
        pbrot[0] += 1
        return i

    ident = alloc([128, 128]); cossin = alloc([128, 2, 16, 32], F32); cmpmask = alloc([128, 2048])
    tri = alloc([128, 256]); fb = alloc([128, 16, 32], F32); ov = alloc([128, 32])
    bsel = alloc([128, 3072]); poolcorr = alloc([128, 4, 16], F32); nrm = alloc([128, 2, 8], F32)
    pool_scale = alloc([128, 4], F32); qkw = alloc([128, 4, 64], F32); conv = alloc([128, 2, 22, 4], F32)
    b1 = alloc([128, 2], F32); chid = alloc([128, 2], F32); w2 = alloc([128, 2, 64]); pool_w = alloc([128, 4, 128])
    posT = alloc([128, 2, 32]); ones = alloc([128, 128]); mhalf = alloc([128, 16], F32)
    KCMP = alloc([128, 2, 128]); VCMP = alloc([128, 2, 128])
    ssq = alloc([128, 16], F32); rs = alloc([128, 16], F32); sr = alloc([128, 16], F32); rstd = alloc([128, 16], F32)
    mark0 = cur[0]
    hT = alloc([128, 8, S]); OT = alloc([128, 8, S])
    markX = cur[0]
    KAs = alloc([128, 2, S]); KAw = alloc([128, 2, S])
    VA = alloc([128, 2, 2, 16, 128]); G = alloc([128, S])
    Wsm = alloc([128, 8, 1024])
    w1 = Wsm.rearrange("p a b -> p (a b)").rearrange("p (k l h) -> p k l h", k=2, l=32, h=128)
    ZQ = alloc([128, 1024], F32); SQ = alloc([128, 1024], F32); QW = alloc([128, 1024], F32)
    TA = alloc([128, 512], F32); TB = alloc([128, 512], F32); TC = alloc([128, 512], F32); TD = alloc([128, 512], F32)
    QN = alloc([128, 1024]); KZb = alloc([128, 384], F32)
    markU = cur[0]
    xs = [alloc([128, 1024], F32) for _ in range(2)]
    hn = [alloc([128, 1024]) for _ in range(2)]
    endU1 = cur[0]
    cur[0] = markU
    KC = alloc([128, S]); VC = alloc([128, S])
    HID = alloc([128, 128]); XG = alloc([128, 128], F32); X2 = alloc([128, 128], F32); X3 = alloc([128, 128], F32)
    endU2 = cur[0]
    cur[0] = markU
    QA = alloc([128, 16, 512])
    PT = [alloc([128, 512]) for _ in range(3)]
    RR = alloc([128, 512], F32); R2 = alloc([128, 512], F32)
    U1 = alloc([128, 512], F32); U2 = alloc([128, 512], F32); UT = alloc([128, 512], F32)
    ECN = [alloc([128, 512]) for _ in range(2)]
    PSC = alloc([128, 512], F32); PSCb = alloc([128, 512])
    SC = alloc([128, 4, 32], F32); M8 = alloc([128, 4, 8], F32); SBf = alloc([128, 4, 32], F32); SBb = alloc([128, 4, 32])
    GSB = alloc([128, 512], F32); RCPb = alloc([128, 512], F32)
    endX1 = max(cur[0], endU1, endU2)
    cur[0] = markX
    YP = alloc([128, 4, S])
    markY = cur[0]
    UF = alloc([128, 16 + S], F32); SA = alloc([128, 16 + S], F32); SBp = alloc([128, 16 + S], F32); PLb = alloc([128, S])
    Wp = alloc([128, 8, 512])
    endX2 = cur[0]
    cur[0] = markY
    Wmg = alloc([128, 8, 2048]); Wpb = alloc([128, 4, 1024]); Wab = alloc([128, 8, 1024]); Wo = alloc([128, 8, 1024])
    MT = alloc([128, 8, 512]); SG = [alloc([128, 512], F32) for _ in range(2)]
    T0 = alloc([128, 512], F32); T1 = alloc([128, 512], F32)
    xs6 = [alloc([128, 1024], F32) for _ in range(2)]
    endX3 = cur[0]
    cur[0] = mark0
    Wup = alloc([128, 8, 5504]); Wdn = alloc([128, 22, 1024])
    H2 = [alloc([128, 8, 256]) for _ in range(2)]; X1S = [alloc([128, 1024], F32) for _ in range(4)]
    hn7 = [alloc([128, 1024]) for _ in range(2)]
    UE = [alloc([128, 2, 258], F32) for _ in range(3)]
    CG = [alloc([128, 256], F32) for _ in range(3)]; CV = [alloc([128, 256], F32) for _ in range(3)]
    SGF = [alloc([128, 256], F32) for _ in range(3)]; AF_ = [alloc([128, 256]) for _ in range(3)]
    HALO = alloc([128, 2, 22, 2], F32)
    endF = cur[0]
    print("arena bytes: X1 %d X2 %d X3 %d FFN %d" % (endX1, endX2, endX3, endF))

    def dump(name, ap, shape, dt=F32, reads=()):
        if not dbg:
            return
        d = nc.dram_tensor("dbg_" + name, shape, dt, kind="ExternalOutput").ap()
        dbg_d[name] = d
        sch.add("sp", lambda e, d=d, ap=ap: e.dma_start(out=d, in_=ap), r=reads, w=(), dma=("dbg_" + name, 1))

    def ld(dst, src, slot, wn, eng="sp"):
        sch.add(eng, lambda e: e.dma_start(out=dst, in_=src), r=(), w=(wn,), dma=("L" + wn, 1))

    ld(ident, dr["ident"], "c0", "c:ident"); ld(cossin, dr["cossin"], "c0", "c:cossin")
    ld(cmpmask, dr["cmpmask"], "c0", "c:cmpmask"); ld(tri, dr["tri"], "c0", "c:tri"); ld(fb, dr["fb"], "c0", "c:fb")
    ld(ov, dr["ov"], "c0", "c:ov"); ld(bsel[0:48], dr["bsel"], "c0", "c:bsel"); ld(poolcorr, dr["poolcorr"], "c0", "c:poolcorr")
    ld(nrm, dr["nrm"], "c0", "c:nrm"); ld(pool_scale, dr["pool_scale"], "c0", "c:pool_scale"); ld(qkw, dr["qkw"], "c0", "c:qkw")
    ld(conv, dr["conv"], "c0", "c:conv"); ld(b1, dr["cmp_b1"], "c0", "c:b1")
    ld(w2, dr["cmp_w2"], "c1", "c:w2", "pool"); ld(pool_w, dr["pool_w"], "c1", "c:pool_w", "pool")
    ld(posT, dr["cmp_posT"], "c1", "c:posT", "pool")
    sch.add("dve", lambda e: e.memset(ones, 1.0), w=("c:ones",))
    sch.add("dve", lambda e: e.memset(mhalf, -0.5), w=("c:mhalf",))
    sch.add("dve", lambda e: e.memset(KCMP, 0.0), w=("m:KCMP",))
    sch.add("dve", lambda e: e.memset(VCMP, 0.0), w=("m:VCMP",))
    sch.add("dve", lambda e: e.memset(VCMP[0:127, :, 64:128], 1.0), w=("m:VCMP",))

    scr = {}

    def pieces(a_, b_):
        out = []
        for k in range(a_.shape[1]):
            if len(a_.shape) == 3 and a_.shape[2] > 2816:
                w_ = a_.shape[2] // 4
                for i in range(4):
                    out.append((a_[:, k, i * w_:(i + 1) * w_], b_[:, k, i * w_:(i + 1) * w_]))
            else:
                out.append((a_[:, k], b_[:, k]))
        return out

    def wload(dst, srcap, wn, slot, nk, s=0, key=None, cast_fn=None):
        key = key or slot
        if key not in scr:
            scr[key] = nc.dram_tensor("wscr_" + key, list(dst.shape), BF16, kind="Internal").ap()
        sc = scr[key]
        if s == 0:
            if cast_fn is None:
                sch.add("pool", lambda e: [e.dma_start(out=dst[:, k, :], in_=srcap[:, k, :]) for k in range(nk)], w=(wn,), dma=(slot, nk))
            else:
                cast_fn()
            sch.add("sp", lambda e: [e.dma_start(out=a_, in_=b_) for a_, b_ in pieces(sc, dst)], r=(wn,), w=("scr:" + key,), dma=("st_" + key, len(pieces(sc, dst))))
        else:
            sch.add("sp", lambda e: [e.dma_start(out=a_, in_=b_) for a_, b_ in pieces(dst, sc)], r=("scr:" + key,), w=(wn,), dma=("ld_" + key, len(pieces(dst, sc))))

    def mm(e, out, lhsT, rhs, start, stop):
        return e.matmul(out, lhsT=lhsT, rhs=rhs, start=start, stop=stop)

    def norm_A(src_tile, src_name, hn_t, hn_name, stat_col):
        sc = slice(stat_col, stat_col + 1)
        k_ = str(stat_col)
        sch.add("act", lambda e: e.activation(out=hn_t, in_=src_tile, func=AF.Square, accum_out=ssq[:, sc]),
                r=(src_name,), w=(hn_name, "t:ssq" + k_))
        sch.add("dve", lambda e: e.tensor_scalar(out=rs[:, sc], in0=ssq[:, sc], scalar1=1.0 / D, scalar2=EPS,
                                                 op0=ALU.mult, op1=ALU.add), r=("t:ssq" + k_,), w=("t:rs" + k_,))
        sch.add("pool", lambda e: e.tensor_tensor(out=rstd[:, sc], in0=rs[:, sc], in1=mhalf[:, sc], op=ALU.pow), r=("t:rs" + k_, "c:mhalf"), w=("t:rstd" + k_,))

    def norm_B1(src_tile, src_name, hn_t, hn_name, stat_col):
        sc = slice(stat_col, stat_col + 1)
        sch.add("act", lambda e: e.activation(out=hn_t, in_=src_tile, func=AF.Copy, scale=rstd[:, sc]),
                r=(src_name, "t:rstd" + str(stat_col)), w=(hn_name,))

    def norm_B2(hn_t, hn_name, nrm_idx, dst, dst_name, col0):
        btv = BT[:, 0:1024].rearrange("p (a b) -> p a b", a=8, b=128)
        sch.add("pe", lambda e: [e.transpose(btv[:, kc, :], hn_t[:, kc * 128:(kc + 1) * 128], ident) for kc in range(8)],
                r=(hn_name, "c:ident"), w=("BT",))
        sch.add("dve", lambda e: e.tensor_tensor(out=dst[:, :, col0:col0 + 128], in0=btv,
                                                 in1=nrm[:, nrm_idx, :].unsqueeze(2).to_broadcast([128, 8, 128]),
                                                 op=ALU.mult), r=("BT", "c:nrm"), w=(dst_name,))

    def norm_tile(src_tile, src_name, hn_t, hn_name, nrm_idx, dst, dst_name, col0, stat_col):
        norm_A(src_tile, src_name, hn_t, hn_name, stat_col)
        norm_B1(src_tile, src_name, hn_t, hn_name, stat_col)
        norm_B2(hn_t, hn_name, nrm_idx, dst, dst_name, col0)

    def rope_A(Z, zname, nh, widx_ap, eps_eff):
        n = nh * 64
        sqv = SQ[:, 0:n].rearrange("p (h d) -> p h d", h=nh, d=64)
        qwv = QW[:, 0:n].rearrange("p (h d) -> p h d", h=nh, d=64)
        sch.add("dve", lambda e: e.tensor_tensor(out=sqv, in0=Z, in1=Z, op=ALU.mult), r=(zname,), w=("t:SQ",))
        sch.add("dve", lambda e: e.tensor_reduce(out=ssq[:, 0:nh], in_=sqv, axis=AX.X, op=ALU.add), r=("t:SQ",), w=("t:ssq",))
        sch.add("dve", lambda e: e.tensor_scalar(out=rs[:, 0:nh], in0=ssq[:, 0:nh], scalar1=eps_eff[0], scalar2=eps_eff[1],
                                                 op0=ALU.mult, op1=ALU.add), r=("t:ssq",), w=("t:rs",))
        sch.add("pool", lambda e: e.tensor_tensor(out=rstd[:, 0:nh], in0=rs[:, 0:nh], in1=mhalf[:, 0:nh], op=ALU.pow), r=("t:rs", "c:mhalf"), w=("t:rstd",))
        sch.add("dve", lambda e: e.tensor_tensor(out=qwv, in0=Z, in1=widx_ap, op=ALU.mult), r=(zname, "c:qkw"), w=("t:QW",))

    def rope_B(nh, tt, out_bf, out_name):
        n = nh * 64
        qwv = QW[:, 0:n].rearrange("p (h d) -> p h d", h=nh, d=64)
        h = nh * 32
        cosb = cossin[:, 0, tt, :].unsqueeze(1).to_broadcast([128, nh, 32])
        sinb = cossin[:, 1, tt, :].unsqueeze(1).to_broadcast([128, nh, 32])
        v3 = lambda T: T[:, 0:h].rearrange("p (h d) -> p h d", h=nh, d=32)
        q1 = qwv[:, :, 0:32]
        q2 = qwv[:, :, 32:64]
        sch.add("pool", lambda e: e.tensor_tensor(out=v3(TA), in0=q1, in1=cosb, op=ALU.mult), r=("t:QW", "c:cossin"), w=("t:TA",))
        sch.add("pool", lambda e: e.tensor_tensor(out=v3(TB), in0=q2, in1=sinb, op=ALU.mult), r=("t:QW", "c:cossin"), w=("t:TB",))
        sch.add("pool", lambda e: e.tensor_tensor(out=v3(TC), in0=q2, in1=cosb, op=ALU.mult), r=("t:QW", "c:cossin"), w=("t:TC",))
        sch.add("pool", lambda e: e.tensor_tensor(out=v3(TD), in0=q1, in1=sinb, op=ALU.mult), r=("t:QW", "c:cossin"), w=("t:TD",))
        sch.add("dve", lambda e: e.tensor_tensor(out=v3(TA), in0=v3(TA), in1=v3(TB), op=ALU.subtract), r=("t:TA", "t:TB"), w=("t:TA",))
        sch.add("dve", lambda e: e.tensor_tensor(out=v3(TC), in0=v3(TC), in1=v3(TD), op=ALU.add), r=("t:TC", "t:TD"), w=("t:TC",))
        rb = rstd[:, 0:nh].unsqueeze(2).to_broadcast([128, nh, 32])
        sch.add("dve", lambda e: e.tensor_tensor(out=out_bf[:, :, 0:32], in0=v3(TA), in1=rb, op=ALU.mult), r=("t:TA", "t:rstd"), w=(out_name,))
        sch.add("dve", lambda e: e.tensor_tensor(out=out_bf[:, :, 32:64], in0=v3(TC), in1=rb, op=ALU.mult), r=("t:TC", "t:rstd"), w=(out_name,))


    UPFX = ("u1:", "u2:", "t:", "x1:QA")

    def do_seq(s):
        for tt in range(16):
            xt = xs[tt % 2]
            xn = "u1:xs%d" % (tt % 2)
            sch.add("sp", lambda e, xt=xt, tt=tt: e.dma_start(out=xt, in_=x_d[s, tt * 128:(tt + 1) * 128, :]),
                    w=(xn,), dma=("xs%d" % (tt % 2), 1))
            norm_tile(xt, xn, hn[tt % 2], "u1:hn%d" % (tt % 2), 0, hT, "m:hT%d" % (tt // 4), tt * 128, 0)
        if s == 0:
            dump("hT", hT, [128, 8, S], BF16, reads=["m:hT%d" % i for i in range(4)])
        if stop_after <= 1:
            return
        sch.barrier(UPFX)
        sch.add("sp", lambda e: [e.dma_start(out=KAs[64:96, g, :], in_=dr["esel"]) for g in range(2)], w=("x1:KAs_e",), dma=("esel", 2))
        sch.add("pool", lambda e: e.memset(VA[:, :, :, :, 64:128], 1.0), w=("x1:VA1",))
        WA = Wsm[:, :, 0:816]
        wload(WA, dr["w_in"][:, :, 1536:2352], "x1:Wsm", "wsm", 8, s, "wa")
        ZK = ZQ[:, 0:768]
        zk5 = ZK.rearrange("p (b k g d) -> p b k g d", b=3, k=2, g=2, d=64)
        def p2_tile(tt):
            c = tt // 4
            KZ = KZb.rearrange("p (b g d) -> p b g d", b=3, g=2, d=64)

            def s_mm():
                def kvmm(e, tt=tt):
                    ins = []
                    for kc in range(8):
                        ins.append(mm(e, PB[0][:, 0:512], hT[:, kc, tt * 128:(tt + 1) * 128], WA[:, kc, 0:512], kc == 0, kc == 7))
                    for kc in range(8):
                        ins.append(mm(e, PB[1][:, 0:256], hT[:, kc, tt * 128:(tt + 1) * 128], WA[:, kc, 512:768], kc == 0, kc == 7))
                    return ins
                sch.add("pe", kvmm, r=("m:hT%d" % c, "x1:Wsm"), w=("B0", "B1"))

            def s_copy():
                sch.add("act", lambda e: e.activation(out=ZK[:, 0:512], in_=PB[0][:, 0:512], func=AF.Copy), r=("B0",), w=("t:ZK",))
                sch.add("act", lambda e: e.activation(out=ZK[:, 512:768], in_=PB[1][:, 0:256], func=AF.Copy), r=("B1",), w=("t:ZK",))

            def s_A():
                sch.add("pool", lambda e, tt=tt: e.tensor_copy(out=VA[:, :, :, tt, 0:64], in_=zk5[:, 1:3, 1, :, :]), r=("t:ZK",), w=("x1:VA",))
                sch.add("pool", lambda e: e.tensor_copy(out=KZ, in_=zk5[:, :, 0, :, :]), r=("t:ZK",), w=("t:KZ",))

            def s_B():
                KZ3 = KZb.rearrange("p (h d) -> p h d", h=6, d=64)
                kw_ap = qkw[:, 1:4, :].unsqueeze(2).to_broadcast([128, 3, 2, 64])
                KN = QN[:, 0:384].rearrange("p (h d) -> p h d", h=6, d=64)
                n = 384
                sqv = SQ[:, 0:n].rearrange("p (h d) -> p h d", h=6, d=64)
                qwv = QW[:, 0:n].rearrange("p (h d) -> p h d", h=6, d=64)
                qwv4 = QW[:, 0:n].rearrange("p (b g d) -> p b g d", b=3, g=2, d=64)
                sch.add("dve", lambda e: e.tensor_tensor(out=sqv, in0=KZ3, in1=KZ3, op=ALU.mult), r=("t:KZ",), w=("t:SQ",))
                sch.add("dve", lambda e: e.tensor_reduce(out=ssq[:, 0:6], in_=sqv, axis=AX.X, op=ALU.add), r=("t:SQ",), w=("t:ssq",))
                sch.add("dve", lambda e: e.tensor_scalar(out=rs[:, 0:6], in0=ssq[:, 0:6], scalar1=1.0 / 64, scalar2=EPS,
                                                         op0=ALU.mult, op1=ALU.add), r=("t:ssq",), w=("t:rs",))
                sch.add("pool", lambda e: e.tensor_tensor(out=rstd[:, 0:6], in0=rs[:, 0:6], in1=mhalf[:, 0:6], op=ALU.pow), r=("t:rs", "c:mhalf"), w=("t:rstd",))
                sch.add("dve", lambda e: e.tensor_tensor(out=qwv4, in0=KZ, in1=kw_ap, op=ALU.mult), r=("t:KZ", "c:qkw"), w=("t:QW",))
                nh = 6
                h_ = nh * 32
                cosb = cossin[:, 0, tt, :].unsqueeze(1).to_broadcast([128, nh, 32])
                sinb = cossin[:, 1, tt, :].unsqueeze(1).to_broadcast([128, nh, 32])
                v3 = lambda T: T[:, 0:h_].rearrange("p (h d) -> p h d", h=nh, d=32)
                q1 = qwv[:, :, 0:32]
                q2 = qwv[:, :, 32:64]
                sch.add("pool", lambda e, cosb=cosb, q1=q1: e.tensor_tensor(out=v3(TA), in0=q1, in1=cosb, op=ALU.mult), r=("t:QW", "c:cossin"), w=("t:TA",))
                sch.add("pool", lambda e, sinb=sinb, q2=q2: e.tensor_tensor(out=v3(TB), in0=q2, in1=sinb, op=ALU.mult), r=("t:QW", "c:cossin"), w=("t:TB",))
                sch.add("pool", lambda e, cosb=cosb, q2=q2: e.tensor_tensor(out=v3(TC), in0=q2, in1=cosb, op=ALU.mult), r=("t:QW", "c:cossin"), w=("t:TC",))
                sch.add("pool", lambda e, sinb=sinb, q1=q1: e.tensor_tensor(out=v3(TD), in0=q1, in1=sinb, op=ALU.mult), r=("t:QW", "c:cossin"), w=("t:TD",))
                sch.add("dve", lambda e: e.tensor_tensor(out=v3(TA), in0=v3(TA), in1=v3(TB), op=ALU.subtract), r=("t:TA", "t:TB"), w=("t:TA",))
                sch.add("dve", lambda e: e.tensor_tensor(out=v3(TC), in0=v3(TC), in1=v3(TD), op=ALU.add), r=("t:TC", "t:TD"), w=("t:TC",))
                rb = rstd[:, 0:nh].unsqueeze(2).to_broadcast([128, nh, 32])
                sch.add("dve", lambda e, rb=rb: e.tensor_tensor(out=KN[:, :, 0:32], in0=v3(TA), in1=rb, op=ALU.mult), r=("t:TA", "t:rstd"), w=("t:KN",))
                sch.add("dve", lambda e, rb=rb: e.tensor_tensor(out=KN[:, :, 32:64], in0=v3(TC), in1=rb, op=ALU.mult), r=("t:TC", "t:rstd"), w=("t:KN",))
                btv = BT[:, 0:384].rearrange("p (a b) -> p a b", a=3, b=128)
                sch.add("pe", lambda e: [e.transpose(btv[:, b, :], QN[:, b * 128:(b + 1) * 128], ident) for b in range(3)],
                        r=("t:KN", "c:ident"), w=("BT",))
                cs = slice(tt * 128, (tt + 1) * 128)
                sch.add("dve", lambda e, cs=cs: e.tensor_copy(out=KC[:, cs], in_=btv[:, 0, :]), r=("BT",), w=("u2:KC",))
                sch.add("dve", lambda e, cs=cs: e.tensor_copy(out=KAs[0:64, 0, cs], in_=btv[0:64, 1, :]), r=("BT",), w=("x1:KAs",))
                sch.add("dve", lambda e, cs=cs: e.tensor_copy(out=KAs[0:64, 1, cs], in_=btv[64:128, 1, :]), r=("BT",), w=("x1:KAs",))
                sch.add("dve", lambda e, cs=cs: e.tensor_copy(out=KAw[0:64, 0, cs], in_=btv[0:64, 2, :]), r=("BT",), w=("x1:KAw",))
                sch.add("dve", lambda e, cs=cs: e.tensor_copy(out=KAw[0:64, 1, cs], in_=btv[64:128, 2, :]), r=("BT",), w=("x1:KAw",))
            return s_mm, s_copy, s_A, s_B
        pt_ = [p2_tile(tt) for tt in range(16)]
        pt_[0][0](); pt_[0][1]()
        for tt in range(16):
            if tt + 1 < 16:
                pt_[tt + 1][0]()
            pt_[tt][2]()
            if tt + 1 < 16:
                pt_[tt + 1][1]()
            pt_[tt][3]()
        for c in range(4):
            cc = slice(c * 512, (c + 1) * 512)
            sch.add("pe", lambda e, cc=cc: [mm(e, PB[2][:, :], WA[:, kc, 128:256], hT[:, kc, cc], kc == 0, kc == 7) for kc in range(8)],
                    r=("m:hT%d" % c, "x1:Wsm"), w=("B2",))
            sch.add("act", lambda e, cc=cc: e.activation(out=VC[:, cc], in_=PB[2][:, :], func=AF.Copy), r=("B2",), w=("u2:VC",))
            sch.add("pe", lambda e, cc=cc: [mm(e, PB[3][0:48, :], WA[:, kc, 768:816], hT[:, kc, cc], kc == 0, kc == 7) for kc in range(8)],
                    r=("m:hT%d" % c, "x1:Wsm"), w=("B3",))
            sch.add("act", lambda e, cc=cc: e.activation(out=G[0:48, cc], in_=PB[3][0:48, :], func=AF.Sigmoid), r=("B3",), w=("x1:G",))
        if s == 0:
            dump("KAs", KAs, [128, 2, S], BF16, reads=("x1:KAs", "x1:KAs_e"))
            dump("KAw", KAw, [128, 2, S], BF16, reads=("x1:KAw",))
            dump("KC", KC, [128, S], BF16, reads=("u2:KC",))
            dump("VC", VC, [128, S], BF16, reads=("u2:VC",))
            dump("VA", VA.rearrange("p a b c d -> p (a b c d)"), [128, 2 * 2 * 16 * 128], BF16, reads=("x1:VA", "x1:VA1"))
            dump("G", G[0:48], [48, S], BF16, reads=("x1:G",))
        if stop_after <= 2:
            return
        wload(w1, None, "x1:Wsm", "wsm", 8, s, "w1", cast_fn=lambda: sch.add("pool", lambda e: [e.dma_start(out=w1[:, kv, 8 * i:8 * (i + 1), :], in_=dr["cmp_w1"][:, kv, 8 * i:8 * (i + 1), :]) for kv in range(2) for i in range(4)], w=("x1:Wsm",), dma=("wsm", 8)))
        if s == 0:
            def chm(e):
                ins = []
                for kv in range(2):
                    for l in range(32):
                        ins.append(mm(e, PB[6][:, kv:kv + 1], w1[0:64, kv, l, :], posT[0:64, kv, l:l + 1], l == 0, l == 31))
                return ins
            sch.add("pe", chm, r=("x1:Wsm", "c:posT"), w=("B6",))
            sch.add("dve", lambda e: e.tensor_tensor(out=chid, in0=PB[6][:, 0:2], in1=b1, op=ALU.add), r=("B6", "c:b1"), w=("c:chid",))
        for kv in range(2):
            for g in range(2):
                src = KC if kv == 0 else VC
                srcn = "u2:KC" if kv == 0 else "u2:VC"
                pr = slice(g * 64, (g + 1) * 64)

                def cm(e, src=src, pr=pr, kv=kv):
                    return [mm(e, PB[4][:, 0:127], w1[pr, kv, l, :], src[pr, l:l + 16 * 126 + 1:16], l == 0, l == 31) for l in range(32)]
                sch.add("pe", cm, r=(srcn, "x1:Wsm"), w=("B4",))
                xg = XG[:, 0:127]; x2 = X2[:, 0:127]; x3 = X3[:, 0:127]
                sch.add("act", lambda e, kv=kv: e.activation(out=xg, in_=PB[4][:, 0:127], func=AF.Identity, bias=chid[:, kv:kv + 1]),
                        r=("B4", "c:chid"), w=("u2:XG",))
                sch.add("dve", lambda e: e.tensor_tensor(out=x2, in0=xg, in1=xg, op=ALU.mult), r=("u2:XG",), w=("u2:X2",))
                sch.add("dve", lambda e: e.tensor_tensor(out=x3, in0=x2, in1=xg, op=ALU.mult), r=("u2:X2", "u2:XG"), w=("u2:X3",))
                sch.add("dve", lambda e: e.scalar_tensor_tensor(out=x2, in0=x3, scalar=0.044715, in1=xg, op0=ALU.mult, op1=ALU.add),
                        r=("u2:X3", "u2:XG"), w=("u2:X2",))
                sch.add("act", lambda e: e.activation(out=x3, in_=x2, func=AF.Sigmoid, scale=1.5957691216057308), r=("u2:X2",), w=("u2:X3",))
                sch.add("dve", lambda e: e.tensor_tensor(out=HID[:, 0:127], in0=xg, in1=x3, op=ALU.mult), r=("u2:XG", "u2:X3"), w=("u2:HID",))
                if kv == 0:
                    sch.add("pe", lambda e: mm(e, PB[5][0:64, 0:127], w2[:, 0, :], HID[:, 0:127], True, True), r=("u2:HID", "c:w2"), w=("B5",))
                    sch.add("act", lambda e, g=g: e.activation(out=KCMP[0:64, g, 0:127], in_=PB[5][0:64, 0:127], func=AF.Copy), r=("B5",), w=("m:KCMP",))
                else:
                    sch.add("pe", lambda e: mm(e, PB[5][0:127, 0:64], HID[:, 0:127], w2[:, 1, :], True, True), r=("u2:HID", "c:w2"), w=("B5",))
                    sch.add("act", lambda e, g=g: e.activation(out=VCMP[0:127, g, 0:64], in_=PB[5][0:127, 0:64], func=AF.Copy), r=("B5",), w=("m:VCMP",))
        if s == 0:
            dump("KCMP", KCMP, [128, 2, 128], BF16, reads=("m:KCMP",))
            dump("VCMP", VCMP, [128, 2, 128], BF16, reads=("m:VCMP",))
        if stop_after <= 3:
            return
        sch.barrier(UPFX)
        WQ = Wsm
        wload(WQ, dr["w_in"][:, :, 512:1536], "x1:Wsm", "wsm", 8, s, "wq")
        def attn_chunk(c):
            cc = slice(c * 512, (c + 1) * 512)
            def q_tile(tl):
                tt = 4 * c + tl
                ts_ = slice(tt * 128, (tt + 1) * 128)
                Z3 = ZQ.rearrange("p (h d) -> p h d", h=16, d=64)
                QN3 = QN.rearrange("p (h d) -> p h d", h=16, d=64)
                wq_ap = qkw[:, 0, :].unsqueeze(1).to_broadcast([128, 16, 64])
                btv = BT[:, 0:1024].rearrange("p (a b) -> p a b", a=8, b=128)
                QAe = QA.rearrange("p (i two) q -> p two i q", two=2)
                ls = slice(tl * 128, (tl + 1) * 128)

                def s_mm():
                    def qmm(e):
                        ins = []
                        for hf in range(2):
                            for kc in range(8):
                                ins.append(mm(e, PB[5 + hf][:, :], hT[:, kc, ts_], WQ[:, kc, hf * 512:(hf + 1) * 512], kc == 0, kc == 7))
                        return ins
                    sch.add("pe", qmm, r=("m:hT%d" % c, "x1:Wsm"), w=("B5", "B6"))

                def s_copy():
                    sch.add("act", lambda e: e.activation(out=ZQ[:, 0:512], in_=PB[5][:, :], func=AF.Copy), r=("B5",), w=("t:ZQ",))
                    sch.add("act", lambda e: e.activation(out=ZQ[:, 512:1024], in_=PB[6][:, :], func=AF.Copy), r=("B6",), w=("t:ZQ",))

                def s_A():
                    rope_A(Z3, "t:ZQ", 16, wq_ap, (1.0, 64 * EPS))

                def s_B():
                    rope_B(16, tt, QN3, "t:QN")
                    sch.add("pe", lambda e: [e.transpose(btv[:, i, :], QN[:, i * 128:(i + 1) * 128], ident) for i in range(8)],
                            r=("t:QN", "c:ident"), w=("BT",))
                    sch.add("dve", lambda e: e.tensor_copy(out=QAe[0:64, 0, :, ls], in_=btv[0:64, :, :]), r=("BT",), w=("x1:QA",))
                    sch.add("dve", lambda e: e.tensor_copy(out=QAe[0:64, 1, :, ls], in_=btv[64:128, :, :]), r=("BT",), w=("x1:QA",))
                return s_mm, s_copy, s_A, s_B
            qt = [q_tile(tl) for tl in range(4)]
            qt[0][0](); qt[0][1]()
            for tl in range(4):
                if tl + 1 < 4:
                    qt[tl + 1][0]()
                qt[tl][2]()
                if tl + 1 < 4:
                    qt[tl + 1][1]()
                qt[tl][3]()
            if s == 0 and c == 1:
                dump("QA", QA.rearrange("p a b -> p (a b)"), [128, 16 * 512], BF16, reads=("x1:QA",))
            if stop_after <= 4:
                return
            SB_ = [0, 1, 2]
            OB_ = [3, 4]
            for g in range(2):
                def stA(hl, g=g):
                    h = g * 8 + hl
                    sb = PB[SB_[h % 3]]; sbn = "B%d" % SB_[h % 3]
                    pt = PT[h % 3]; ptn = "t:PT%d" % (h % 3)
                    sch.add("pe", lambda e: [mm(e, sb[:, :], KCMP[0:64, g, :], QA[0:64, h, :], True, False),
                                             mm(e, sb[:, :], ident, cmpmask[:, cc], False, True)],
                            r=("m:KCMP", "x1:QA", "c:ident", "c:cmpmask"), w=(sbn,))
                    sch.add("act", lambda e: e.activation(out=pt, in_=sb[:, :], func=AF.Exp), r=(sbn,), w=(ptn,))

                def stB(hl, g=g):
                    h = g * 8 + hl
                    pt = PT[h % 3]; ptn = "t:PT%d" % (h % 3)
                    ecn = ECN[h % 2]; ecnn = "t:ECN%d" % (h % 2)
                    sch.add("pe", lambda e: mm(e, PB[5][:, :], ones, pt, True, True), r=(ptn, "c:ones"), w=("B5",))
                    sch.add("act", lambda e: e.activation(out=RR, in_=PB[5][:, :], func=AF.Ln, bias=1e-18), r=("B5",), w=("t:RR",))
                    sch.add("act", lambda e: e.activation(out=R2, in_=RR, func=AF.Exp, scale=-1.0), r=("t:RR",), w=("t:R2",))
                    sch.add("dve", lambda e: e.tensor_tensor(out=ecn, in0=pt, in1=R2, op=ALU.mult), r=(ptn, "t:R2"), w=(ecnn,))
                    if hl == 0:
                        sch.add("pool", lambda e: e.tensor_copy(out=PSC, in_=ecn), r=(ecnn,), w=("t:PSC",))
                    else:
                        sch.add("pool", lambda e: e.tensor_tensor(out=PSC, in0=PSC, in1=ecn, op=ALU.add), r=(ecnn, "t:PSC"), w=("t:PSC",))

                def stC(hl, g=g):
                    h = g * 8 + hl
                    ob = PB[OB_[h % 2]]; obn = "B%d" % OB_[h % 2]
                    ecn = ECN[h % 2]; ecnn = "t:ECN%d" % (h % 2)
                    sch.add("pe", lambda e: mm(e, ob[0:64, :], VCMP[:, g, 0:64], ecn, True, True), r=(ecnn, "m:VCMP"), w=(obn,))
                    ci = 3 * h + 0
                    sch.add("pe", lambda e: mm(e, PB[6][0:64, :], bsel[0:48, ci * 64:(ci + 1) * 64], G[0:48, cc], True, True),
                            r=("x1:G", "c:bsel"), w=("B6",))
                    sch.add("act", lambda e: e.activation(out=GSB[0:64, :], in_=PB[6][0:64, :], func=AF.Copy), r=("B6",), w=("t:GSB",))
                    pr = slice((h % 2) * 64, (h % 2) * 64 + 64)
                    sch.add("dve", lambda e: e.tensor_tensor(out=OT[pr, h // 2, cc], in0=ob[0:64, :], in1=GSB[0:64, :], op=ALU.mult),
                            r=(obn, "t:GSB"), w=("m:OT%d_%d" % (c, h),))
                for k in range(8 + 2):
                    if k < 8:
                        stA(k)
                    if 0 <= k - 1 < 8:
                        stB(k - 1)
                    if 0 <= k - 2 < 8:
                        stC(k - 2)
                sch.add("pool", lambda e: e.tensor_copy(out=PSCb, in_=PSC), r=("t:PSC",), w=("t:PSCb",))
                impv = PB[5][:, 0:128].rearrange("p (a b) -> p a b", a=4, b=32)
                sch.add("pe", lambda e: [mm(e, impv[:, qb, :], PSCb[:, qb * 128:(qb + 1) * 128], ov, True, True) for qb in range(4)],
                        r=("t:PSCb", "c:ov"), w=("B5",))
                sch.add("dve", lambda e, c=c: e.tensor_tensor(out=SC, in0=impv, in1=fb[:, 4 * c:4 * c + 4, :], op=ALU.add), r=("B5", "c:fb"), w=("t:SC",))
                for qb in range(4):
                    sch.add("dve", lambda e, qb=qb: e.max(out=M8[:, qb, :], in_=SC[:, qb, :]), r=("t:SC",), w=("t:M8",))
                    sch.add("dve", lambda e, qb=qb: e.tensor_scalar(out=SBf[:, qb, :], in0=SC[:, qb, :], scalar1=M8[:, qb, 7:8], scalar2=1.0,
                                                                    op0=ALU.is_ge, op1=ALU.subtract), r=("t:SC", "t:M8"), w=("t:SBf",))
                sch.add("dve", lambda e: e.tensor_scalar(out=SBb, in0=SBf, scalar1=-NEGB, scalar2=None, op0=ALU.mult), r=("t:SBf",), w=("t:SBb",))
                sch.add("pe", lambda e: [e.transpose(BT[0:32, qb * 128:(qb + 1) * 128], SBb[:, qb, :], ident) for qb in range(4)],
                        r=("t:SBb", "c:ident"), w=("BT",))
                sch.add("dve", lambda e, g=g: e.tensor_copy(out=QA[64:96, g * 8:(g + 1) * 8, :],
                                                            in_=BT[0:32, 0:512].unsqueeze(1).to_broadcast([32, 8, 512])),
                        r=("BT",), w=("x1:QAs",))
                if s == 0 and c == 1 and g == 0:
                    dump("SC", SC.rearrange("p a b -> p (a b)"), [128, 128], F32, reads=("t:SC",))
                    dump("SBf", SBf.rearrange("p a b -> p (a b)"), [128, 128], F32, reads=("t:SBf",))
            if stop_after <= 5:
                return
            jobs = []
            for h in range(16):
                g = h // 8
                tl_ = []
                for kt in range(0, 4 * c + 4):
                    j = kt - 4 * c
                    lo = 0 if j < 0 else 128 * j
                    tl_.append(dict(kt=kt, lo=lo, hi=512, mask=(None if j < 0 else (0, lo))))
                jobs.append(dict(h=h, g=g, br=1, tiles=tl_))
                tl_ = []
                order = [4] + ([0, 1, 2, 3] if c >= 1 else []) + [5, 6, 7]
                for i in order:
                    kt = 4 * c - 4 + i
                    if i <= 3:
                        tl_.append(dict(kt=kt, lo=0, hi=128 * (i + 1), mask=(1, 128 * i)))
                    else:
                        tl_.append(dict(kt=kt, lo=128 * (i - 4), hi=512, mask=(0, 128 * (i - 4))))
                jobs.append(dict(h=h, g=g, br=2, tiles=tl_))
            flat = []
            for jb_i, jb in enumerate(jobs):
                for ti, t in enumerate(jb["tiles"]):
                    flat.append((jb_i, ti))
            srot = [0]

            def rec_S(k):
                jb_i, ti = flat[k]
                jb = jobs[jb_i]; t = jb["tiles"][ti]
                si = k % 3
                sb = PB[SB_[si]]; t["si"] = si
                lo, hi, kt, g, h = t["lo"], t["hi"], t["kt"], jb["g"], jb["h"]
                ks = slice(kt * 128, (kt + 1) * 128)
                if jb["br"] == 1:
                    lhs = KAs[0:96, g, ks]; rhs = QA[0:96, h, lo:hi]; rn = ("x1:KAs", "x1:KAs_e", "x1:QA", "x1:QAs")
                else:
                    lhs = KAw[0:64, g, ks]; rhs = QA[0:64, h, lo:hi]; rn = ("x1:KAw", "x1:QA")
                mk = t["mask"]

                def f(e):
                    ins = [mm(e, sb[:, lo:hi], lhs, rhs, True, mk is None)]
                    if mk is not None:
                        ins.append(mm(e, sb[:, mk[1]:mk[1] + 128], ident, tri[:, mk[0] * 128:(mk[0] + 1) * 128], False, True))
                    return ins
                sch.add("pe", f, r=rn + ("c:ident", "c:tri"), w=("B%d" % SB_[si],))
                pt = PT[si]
                sch.add("act", lambda e: e.activation(out=pt[:, lo:hi], in_=sb[:, lo:hi], func=AF.Exp), r=("B%d" % SB_[si],), w=("t:PT%d" % si,))

            def rec_PV(k):
                jb_i, ti = flat[k]
                jb = jobs[jb_i]; t = jb["tiles"][ti]
                si = t["si"]
                oi = jb_i % 2
                ob = PB[OB_[oi]]; obn = "B%d" % OB_[oi]
                lo, hi, kt, g, h, br = t["lo"], t["hi"], t["kt"], jb["g"], jb["h"], jb["br"]
                pt = PT[si]
                first = ti == 0
                last = ti == len(jb["tiles"]) - 1
                sch.add("pe", lambda e: mm(e, ob[:, lo:hi], VA[:, br - 1, g, kt, :], pt[:, lo:hi], first, last),
                        r=("t:PT%d" % si, "x1:VA", "x1:VA1"), w=(obn,))
                if last:
                    ci = 3 * h + br
                    sch.add("pe", lambda e: mm(e, PB[6][0:64, :], bsel[0:48, ci * 64:(ci + 1) * 64], G[0:48, cc], True, True),
                            r=("x1:G", "c:bsel"), w=("B6",))
                    sch.add("dve", lambda e: e.reciprocal(out=RCPb[0:64, :], in_=ob[64:128, :]), r=(obn,), w=("t:RCP",))
                    sch.add("dve", lambda e: e.tensor_tensor(out=R2[0:64, :], in0=RCPb[0:64, :], in1=PB[6][0:64, :], op=ALU.mult), r=("t:RCP", "B6"), w=("t:R2",))
                    pr = slice((h % 2) * 64, (h % 2) * 64 + 64)
                    if br == 1:
                        sch.add("dve", lambda e: e.tensor_tensor(out=U1[pr, :], in0=ob[0:64, :], in1=R2[0:64, :], op=ALU.mult), r=(obn, "t:R2"), w=("t:U1",))
                    else:
                        sch.add("dve", lambda e: e.tensor_tensor(out=U2[pr, :], in0=ob[0:64, :], in1=R2[0:64, :], op=ALU.mult), r=(obn, "t:R2"), w=("t:U2",))
                        otn = "m:OT%d_%d" % (c, h)
                        sch.add("pool", lambda e: e.tensor_tensor(out=UT[pr, :], in0=U1[pr, :], in1=U2[pr, :], op=ALU.add), r=("t:U1", "t:U2"), w=("t:UT",))
                        sch.add("pool", lambda e: e.tensor_tensor(out=OT[pr, h // 2, cc], in0=OT[pr, h // 2, cc], in1=UT[pr, :], op=ALU.add),
                                r=("t:UT", otn), w=(otn,))
            DEPTH = 2
            for k in range(len(flat) + DEPTH):
                if k < len(flat):
                    rec_S(k)
                if k - DEPTH >= 0:
                    rec_PV(k - DEPTH)
        for c_i in range(4):
            attn_chunk(c_i)
        if s == 0:
            dump("OT", OT, [128, 8, S], BF16, reads=["m:OT%d_%d" % (c_, h_) for c_ in range(4) for h_ in range(16)])
        if stop_after <= 6:
            return
        sch.barrier(("x1:", "x2:", "t:", "u1:", "u2:"))
        wload(Wp, dr["w_in"][:, :, 0:512], "x2:Wp", "wsm", 8, s, "wp")
        sch.add("dve", lambda e: e.memset(UF[:, 0:16], 0.0), w=("x2:UF",))
        sch.add("dve", lambda e: e.memset(SA[:, 0:16], 0.0), w=("x2:SA",))
        sch.add("dve", lambda e: e.memset(SBp[:, 0:16], 0.0), w=("x2:SB",))
        for g in range(4):
            for c in range(4):
                cc = slice(c * 512, (c + 1) * 512)
                bi = nextbank([0, 1, 2, 3])
                sch.add("pe", lambda e, bi=bi, g=g, cc=cc: [mm(e, PB[bi][:, :], Wp[:, kc, g * 128:(g + 1) * 128], hT[:, kc, cc], kc == 0, kc == 7) for kc in range(8)],
                        r=("m:hT%d" % c, "x2:Wp"), w=("B%d" % bi,))
                sch.add("act", lambda e, bi=bi, c=c: e.activation(out=UF[:, 16 + c * 512:16 + (c + 1) * 512], in_=PB[bi][:, :], func=AF.Copy),
                        r=("B%d" % bi,), w=("x2:UF",))
            bufs = [(UF, "x2:UF"), (SA, "x2:SA"), (SBp, "x2:SB")]
            srcb = bufs[0]
            for lvl in range(g + 1):
                sh = 1 << lvl
                dstb = bufs[1 + (lvl % 2)]
                eng = "dve" if lvl % 2 == 0 else "pool"
                sch.add(eng, lambda e, srcb=srcb, dstb=dstb, sh=sh: e.tensor_tensor(out=dstb[0][:, 16:16 + S], in0=srcb[0][:, 16:16 + S],
                                                                                  in1=srcb[0][:, 16 - sh:16 + S - sh], op=ALU.add),
                        r=(srcb[1],), w=(dstb[1],))
                srcb = dstb
            wdw = 2 << g
            sch.add("dve", lambda e, srcb=srcb, g=g: e.tensor_tensor(out=srcb[0][:, 16:32], in0=srcb[0][:, 16:32], in1=poolcorr[:, g, :], op=ALU.mult),
                    r=(srcb[1], "c:poolcorr"), w=(srcb[1],))
            sch.add("dve", lambda e, srcb=srcb, wdw=wdw: e.scalar_tensor_tensor(out=PLb, in0=srcb[0][:, 16:16 + S], scalar=1.0 / wdw, in1=UF[:, 16:16 + S],
                                                                             op0=ALU.mult, op1=ALU.subtract), r=(srcb[1], "x2:UF"), w=("x2:PLb",))
            for c in range(4):
                cc = slice(c * 512, (c + 1) * 512)
                bi = nextbank([4, 5, 6])
                sch.add("pe", lambda e, bi=bi, g=g, cc=cc: mm(e, PB[bi][:, :], pool_w[:, g, :], PLb[:, cc], True, True), r=("x2:PLb", "c:pool_w"), w=("B%d" % bi,))
                sch.add("act", lambda e, bi=bi, g=g, cc=cc: e.activation(out=YP[:, g, cc], in_=PB[bi][:, :], func=AF.Copy, scale=pool_scale[:, g:g + 1]),
                        r=("B%d" % bi, "c:pool_scale"), w=("m:YP",))
        if s == 0:
            dump("YP", YP, [128, 4, S], BF16, reads=("m:YP",))
        if stop_after <= 7:
            return
        sch.barrier(("x2:", "x3:", "t:"))
        wload(Wmg, dr["w_in"][:, :, 2352:4400], "x3:Wmg", "wmg", 8, s)
        wload(Wpb, dr["w_pool_br"], "x3:Wpb", "wpb", 4, s)
        wload(Wab, dr["w_attn_br"], "x3:Wab", "wab", 8, s)
        wload(Wo, dr["w_o"], "x3:Wo", "wo", 8, s)
        for c in range(4):
            cc = slice(c * 512, (c + 1) * 512)
            otn = ["m:OT%d_%d" % (c, h_) for h_ in range(16)]
            for dc in range(8):
                dsl = slice(dc * 128, (dc + 1) * 128)
                b_yp, b_ya, b_m0, b_m1 = [nextbank([0, 1, 2, 3, 4, 5, 6]) for _ in range(4)]
                sch.add("pe", lambda e, b=b_m0, dsl=dsl, cc=cc: [mm(e, PB[b][:, :], Wmg[:, kc, dsl], hT[:, kc, cc], kc == 0, kc == 7) for kc in range(8)],
                        r=("m:hT%d" % c, "x3:Wmg"), w=("B%d" % b_m0,))
                sch.add("pe", lambda e, b=b_m1, dc=dc, cc=cc: [mm(e, PB[b][:, :], Wmg[:, kc, 1024 + dc * 128:1024 + (dc + 1) * 128], hT[:, kc, cc], kc == 0, kc == 7) for kc in range(8)],
                        r=("m:hT%d" % c, "x3:Wmg"), w=("B%d" % b_m1,))
                sch.add("pe", lambda e, b=b_yp, dsl=dsl, cc=cc: [mm(e, PB[b][:, :], Wpb[:, g, dsl], YP[:, g, cc], g == 0, g == 3) for g in range(4)],
                        r=("m:YP", "x3:Wpb"), w=("B%d" % b_yp,))
                sch.add("pe", lambda e, b=b_ya, dsl=dsl, cc=cc: [mm(e, PB[b][:, :], Wab[:, i, dsl], OT[:, i, cc], i == 0, i == 7) for i in range(8)],
                        r=tuple(otn) + ("x3:Wab",), w=("B%d" % b_ya,))
                sch.add("act", lambda e, b=b_m0: e.activation(out=SG[0], in_=PB[b][:, :], func=AF.Sigmoid), r=("B%d" % b_m0,), w=("x3:SG0",))
                sch.add("act", lambda e, b=b_m1: e.activation(out=SG[1], in_=PB[b][:, :], func=AF.Sigmoid), r=("B%d" % b_m1,), w=("x3:SG1",))
                sch.add("dve", lambda e, b=b_yp: e.tensor_tensor(out=T0, in0=SG[0], in1=PB[b][:, :], op=ALU.mult), r=("x3:SG0", "B%d" % b_yp), w=("x3:T0",))
                sch.add("dve", lambda e, b=b_ya: e.tensor_tensor(out=T1, in0=SG[1], in1=PB[b][:, :], op=ALU.mult), r=("x3:SG1", "B%d" % b_ya), w=("x3:T1",))
                sch.add("pool", lambda e, dc=dc: e.tensor_tensor(out=MT[:, dc, :], in0=T0, in1=T1, op=ALU.add), r=("x3:T0", "x3:T1"), w=("x3:MT",))
            def ld6(tt):
                sch.add("sp", lambda e: e.dma_start(out=xs6[tt % 2], in_=x_d[s, tt * 128:(tt + 1) * 128, :]), w=("x3:xs%d" % (tt % 2),), dma=("xs6%d" % (tt % 2), 1))
            for tl in range(4):
                tt = 4 * c + tl
                ls = slice(tl * 128, (tl + 1) * 128)
                xt = xs6[tt % 2]; xn = "x3:xs%d" % (tt % 2)
                if tl == 0:
                    ld6(tt)
                if tl + 1 < 4:
                    ld6(tt + 1)
                b0, b1_ = nextbank([0, 1, 2, 3, 4, 5, 6]), nextbank([0, 1, 2, 3, 4, 5, 6])
                for hf, b in ((0, b0), (1, b1_)):
                    sch.add("pe", lambda e, b=b, hf=hf, ls=ls: [mm(e, PB[b][:, :], MT[:, dc, ls], Wo[:, dc, hf * 512:(hf + 1) * 512], dc == 0, dc == 7) for dc in range(8)],
                            r=("x3:MT", "x3:Wo"), w=("B%d" % b,))
                    sch.add("dve", lambda e, b=b, hf=hf, xt=xt: e.tensor_tensor(out=xt[:, hf * 512:(hf + 1) * 512], in0=xt[:, hf * 512:(hf + 1) * 512], in1=PB[b][:, :], op=ALU.add),
                            r=(xn, "B%d" % b), w=(xn,))
                sch.add("sp", lambda e, xt=xt, tt=tt: e.dma_start(out=(x1_d if stop_after > 8 else out_d)[s, tt * 128:(tt + 1) * 128, :], in_=xt), r=(xn,), w=("o:%d_%d" % (s, tt),), dma=("xo6%d" % (tt % 2), 1))
        if stop_after <= 8:
            return
        sch.barrier(("m:", "x1:", "x2:", "x3:", "f:", "t:", "u1:", "u2:"))
        sch.add("dve", lambda e: e.memset(HALO, 0.0), w=tuple("f:HALO%d" % i for i in range(22)))
        DB = [0, 1, 2, 3]
        UB = [4, 5, 6]
        def pf(c8n, stage):
            for tl in range(2):
                tt = 2 * c8n + tl
                xi = (c8n % 2) * 2 + tl
                xt = X1S[xi]; xn = "f:x1s%d" % xi
                if stage == 0:
                    sch.add("sp", lambda e, xt=xt, tt=tt: e.dma_start(out=xt, in_=x1_d[s, tt * 128:(tt + 1) * 128, :]), r=("o:%d_%d" % (s, tt),), w=(xn,), dma=("x1s%d" % xi, 1))
                elif stage == 1:
                    norm_A(xt, xn, hn7[tl], "f:hn%d" % tl, 1 + tl)
                elif stage == 2:
                    norm_B1(xt, xn, hn7[tl], "f:hn%d" % tl, 1 + tl)
                else:
                    norm_B2(hn7[tl], "f:hn%d" % tl, 1, H2[c8n % 2], "f:H2_%d" % (c8n % 2), tl * 128)
        pf(0, 0)
        wload(Wup, None, "f:Wup", "wup", 32, s, "wup", cast_fn=lambda: sch.add("pool", lambda e: [e.dma_start(out=Wup[:, k, 1376 * i:1376 * (i + 1)], in_=dr["w_up"][:, k, 1376 * i:1376 * (i + 1)]) for k in range(8) for i in range(4)], w=("f:Wup",), dma=("wup", 32)))
        wload(Wdn, dr["w_down"], "f:Wdn", "wdn", 22, s)
        for st_ in range(1, 4):
            pf(0, st_)
        for c8 in range(8):
            H2c = H2[c8 % 2]; h2n = "f:H2_%d" % (c8 % 2)

            def rec_up(fc, H2c=H2c, h2n=h2n):
                rows = 128 if fc < 21 else 64
                par = fc % 3
                b = UB[par]; bn = "B%d" % b
                ue = UE[par]; uen = "f:UE%d" % par

                def upmm(e):
                    ins = []
                    for gv in range(2):
                        col0 = gv * D_FF + fc * 128
                        for kc in range(8):
                            ins.append(mm(e, PB[b][0:rows, gv * 256:(gv + 1) * 256], Wup[:, kc, col0:col0 + rows], H2c[:, kc, :], kc == 0, kc == 7))
                    return ins
                sch.add("pe", upmm, r=(h2n, "f:Wup"), w=(bn,))
                hn_ = "f:HALO%d" % fc
                sch.add("pool", lambda e: e.tensor_copy(out=ue[0:rows, :, 0:2], in_=HALO[0:rows, :, fc, :]), r=(hn_,), w=(uen,))
                sch.add("act", lambda e: e.activation(out=ue[0:rows, :, 2:258], in_=PB[b][0:rows, :].rearrange("p (g t) -> p g t", g=2, t=256), func=AF.Copy),
                        r=(bn,), w=(uen,))
                sch.add("pool", lambda e: e.tensor_copy(out=HALO[0:rows, :, fc, :], in_=ue[0:rows, :, 256:258]), r=(uen,), w=(hn_,))

            def rec_conv(fc):
                rows = 128 if fc < 21 else 64
                par = fc % 3
                ue = UE[par]; uen = "f:UE%d" % par
                cxs = ((0, CG[par], "f:CG%d" % par), (1, CV[par], "f:CV%d" % par))
                cw = lambda tap, gv: conv[0:rows, gv, fc, tap:tap + 1]
                for gv, cx, cxn in cxs:
                    sch.add("dve", lambda e, cx=cx, gv=gv: e.tensor_scalar(out=cx[0:rows, :], in0=ue[0:rows, gv, 2:258], scalar1=cw(2, gv), scalar2=cw(3, gv), op0=ALU.mult, op1=ALU.add),
                            r=(uen, "c:conv"), w=(cxn,))
                for gv, cx, cxn in cxs:
                    sch.add("dve", lambda e, cx=cx, gv=gv: e.scalar_tensor_tensor(out=cx[0:rows, :], in0=ue[0:rows, gv, 1:257], scalar=cw(1, gv), in1=cx[0:rows, :], op0=ALU.mult, op1=ALU.add),
                            r=(uen, cxn, "c:conv"), w=(cxn,))
                for gv, cx, cxn in cxs:
                    sch.add("dve", lambda e, cx=cx, gv=gv: e.scalar_tensor_tensor(out=cx[0:rows, :], in0=ue[0:rows, gv, 0:256], scalar=cw(0, gv), in1=cx[0:rows, :], op0=ALU.mult, op1=ALU.add),
                            r=(uen, cxn, "c:conv"), w=(cxn,))

            def rec_act(fc):
                rows = 128 if fc < 21 else 64
                par = fc % 3
                sg = SGF[par]; af = AF_[par]
                sch.add("act", lambda e: e.activation(out=sg[0:rows, :], in_=CG[par][0:rows, :], func=AF.Silu), r=("f:CG%d" % par,), w=("f:SG%d" % par,))
                sch.add("pool", lambda e: e.tensor_tensor(out=af[0:rows, :], in0=sg[0:rows, :], in1=CV[par][0:rows, :], op=ALU.mult),
                        r=("f:SG%d" % par, "f:CV%d" % par), w=("f:A%d" % par,))

            def rec_down(fc):
                rows = 128 if fc < 21 else 64
                par = fc % 3
                af = AF_[par]

                def dmm(e):
                    ins = []
                    for tl in range(2):
                        for hf in range(2):
                            ins.append(mm(e, PB[DB[tl * 2 + hf]][:, :], af[0:rows, tl * 128:(tl + 1) * 128], Wdn[0:rows, fc, hf * 512:(hf + 1) * 512], fc == 0, fc == 21))
                    return ins
                sch.add("pe", dmm, r=("f:A%d" % par, "f:Wdn"), w=("B0", "B1", "B2", "B3"))
            for k in range(22 + 3):
                if k < 22:
                    rec_up(k)
                    rec_conv(k)
                if 0 <= k - 1 < 22:
                    rec_act(k - 1)
                if 0 <= k - 3 < 22:
                    rec_down(k - 3)
                if c8 + 1 < 8 and k in (2, 6, 9, 12):
                    pf(c8 + 1, (2, 6, 9, 12).index(k))
            for tl in range(2):
                tt = 2 * c8 + tl
                xi = (c8 % 2) * 2 + tl
                xt = X1S[xi]; xn = "f:x1s%d" % xi
                for hf in range(2):
                    b = DB[tl * 2 + hf]
                    sch.add("dve", lambda e, xt=xt, b=b, hf=hf: e.tensor_tensor(out=xt[:, hf * 512:(hf + 1) * 512], in0=xt[:, hf * 512:(hf + 1) * 512], in1=PB[b][:, :], op=ALU.add),
                            r=(xn, "B%d" % b), w=(xn,))
                sch.add("sp", lambda e, xt=xt, tt=tt: e.dma_start(out=out_d[s, tt * 128:(tt + 1) * 128, :], in_=xt), r=(xn,), w=("of:%d_%d" % (s, tt),), dma=("x1o%d" % xi, 1))
        sch.barrier(("m:", "x1:", "x2:", "x3:", "f:", "t:", "u1:", "u2:"))

    for s_i in range(nseq):
        do_seq(s_i)

    sch.finalize()
    print("ops:", len(sch.ops))
    sems = {}
    for k in sch.sem_keys():
        sems[k] = es.enter_context(nc.semaphore("s_%s_%s" % k))
    with nc.Block() as block:
        @block.sync
        def _(e):
            sch.emit("sp", e, sems)

        @block.tensor
        def _(e):
            sch.emit("pe", e, sems)

        @block.scalar
        def _(e):
            sch.emit("act", e, sems)

        @block.vector
        def _(e):
            sch.emit("dve", e, sems)

        @block.gpsimd
        def _(e):
            sch.emit("pool", e, sems)
    es.close()
    return nc, dbg_d


_CACHE = {}


def kernel(**inputs):
    x = np.asarray(inputs["x"], dtype=np.float32)
    B = x.shape[0]
    nseq = B // NCORES
    consts = host_consts()
    wts = host_weights(inputs)
    if "nc" not in _CACHE:
        _CACHE["nc"] = build(nseq=nseq)[0]
    nc = _CACHE["nc"]
    in_maps = []
    for c in range(NCORES):
        m = {"x": np.ascontiguousarray(x[c * nseq:(c + 1) * nseq])}
        m.update(consts)
        m.update(wts)
        in_maps.append(m)
    res = run_bass_kernel_spmd(nc, in_maps, core_ids=list(range(NCORES)))
    out = np.concatenate([np.asarray(r["out"], dtype=np.float32) for r in res.results], axis=0)
    return out
```

```python
import contextlib
import os
import numpy as np
import ml_dtypes
import concourse.bass as bass
import concourse.mybir as mybir
from concourse.bass_utils import run_bass_kernel_spmd

F32 = mybir.dt.float32
BF16 = mybir.dt.bfloat16
ALU = mybir.AluOpType
AF = mybir.ActivationFunctionType
AX = mybir.AxisListType

S = 2048
D = 1024
NCORES = 8
EPS = 1e-6
NEGB = -30000.0
D_FF = 2752
ENGS = ("pe", "act", "dve", "pool", "sp")


class Op:
    __slots__ = ("eng", "fn", "dma", "deps", "signal", "val", "w", "waits")


class Sched:
    def __init__(self):
        self.ops = []
        self.lastw = {}
        self.readers = {}
        self.dmacum = {}
        self.barriers = []

    def add(self, eng, fn, r=(), w=(), dma=None):
        op = Op()
        op.eng = eng
        op.fn = fn
        op.dma = dma
        op.signal = False
        op.val = None
        op.deps = []
        seen = set()
        for n in tuple(r) + tuple(w):
            if n not in self.lastw:
                for pf, bop in reversed(self.barriers):
                    if n.startswith(pf):
                        self.lastw[n] = bop
                        self.readers.setdefault(n, [])
                        break
        for n in r:
            d = self.lastw.get(n)
            if d is not None and id(d) not in seen:
                seen.add(id(d))
                op.deps.append((d, "raw"))
        for n in w:
            d = self.lastw.get(n)
            if d is not None and id(d) not in seen:
                seen.add(id(d))
                op.deps.append((d, "waw"))
            for d in self.readers.get(n, ()):
                if id(d) not in seen:
                    seen.add(id(d))
                    op.deps.append((d, "war"))
        for n in r:
            self.readers.setdefault(n, []).append(op)
        for n in w:
            self.lastw[n] = op
            self.readers[n] = []
        if dma is not None:
            self.dmacum[dma[0]] = self.dmacum.get(dma[0], 0) + 16 * dma[1]
            op.val = self.dmacum[dma[0]]
        self.ops.append(op)
        return op

    def barrier(self, prefixes):
        names = [n for n in set(self.lastw) | set(self.readers) if n.startswith(prefixes)]
        bop = self.add("pool", lambda e: e.nop(), r=(), w=names)
        self.barriers.append((prefixes, bop))

    def finalize(self):
        for op in self.ops:
            op.w = []
            for d, kind in op.deps:
                if d.dma is not None:
                    op.w.append(d)
                elif d.eng == op.eng:
                    if op.dma is not None:
                        op.w.append(d)
                        d.signal = True
                    elif op.eng == "pe":
                        continue
                    elif kind == "raw":
                        op.w.append(d)
                        d.signal = True
                else:
                    op.w.append(d)
                    d.signal = True
        cnt = {}
        for op in self.ops:
            if op.dma is None and op.signal:
                cnt[op.eng] = cnt.get(op.eng, 0) + 1
                op.val = cnt[op.eng]
        waited = {e: {} for e in ENGS}
        for op in self.ops:
            ws = {}
            for d in op.w:
                key = ("dma", d.dma[0]) if d.dma is not None else ("eng", d.eng)
                if waited[op.eng].get(key, 0) >= d.val:
                    continue
                ws[key] = max(ws.get(key, 0), d.val)
            for k, v in ws.items():
                waited[op.eng][k] = v
            op.waits = list(ws.items())

    def sem_keys(self):
        return [("eng", e) for e in ENGS] + [("dma", s) for s in self.dmacum]

    def emit(self, eng_name, e, sems):
        for op in self.ops:
            if op.eng != eng_name:
                continue
            for key, v in op.waits:
                e.wait_ge(sems[key], v)
            insts = op.fn(e)
            if not isinstance(insts, (list, tuple)):
                insts = [insts]
            if op.dma is not None:
                assert len(insts) == op.dma[1], (len(insts), op.dma)
                for i in insts:
                    i.then_inc(sems[("dma", op.dma[0])], 16)
            elif op.signal:
                insts[-1].then_inc(sems[("eng", op.eng)], 1)
        if eng_name == "sp":
            for s, v in self.dmacum.items():
                e.wait_ge(sems[("dma", s)], v)


def host_consts():
    bf = ml_dtypes.bfloat16
    c = {}
    c["ident"] = np.eye(128, dtype=np.float32).astype(bf)
    half = 32
    freqs = (10000.0 ** (-np.arange(half, dtype=np.float32) / half)).astype(np.float32)
    pos = np.arange(S, dtype=np.float32)
    ang = (pos[:, None] * freqs[None, :]).astype(np.float32)
    cs = np.stack([np.cos(ang), np.sin(ang)], 0).astype(np.float32)
    c["cossin"] = np.ascontiguousarray(cs.reshape(2, 16, 128, 32).transpose(2, 0, 1, 3))
    n = np.arange(128)[:, None]
    q = np.arange(S)[None, :]
    c["cmpmask"] = np.where((n < 127) & (16 * n + 31 <= q), 0.0, NEGB).astype(bf)
    kk = np.arange(128)[:, None]
    qq = np.arange(128)[None, :]
    tri = np.concatenate([np.where(qq >= kk, 0.0, NEGB), np.where(qq < kk, 0.0, NEGB)], 1)
    c["tri"] = tri.astype(bf)
    j = np.arange(32)[:, None]
    k = np.arange(S)[None, :]
    c["esel"] = (k // 64 == j).astype(np.float32).astype(bf)
    tq = np.arange(S)[:, None]
    jb = np.arange(32)[None, :]
    cur = tq // 64
    forced = (jb == 0) | (jb == cur) | (jb == cur - 1)
    valid = jb * 64 <= tq
    fb = np.where(valid, 1000.0 * forced, -1e30).astype(np.float32)
    c["fb"] = np.ascontiguousarray(fb.reshape(16, 128, 32).transpose(1, 0, 2))
    ci = np.arange(128)[:, None]
    sj = np.arange(32)[None, :]
    ov = ((ci * 16 < (sj + 1) * 64) & (ci * 16 + 32 > sj * 64) & (ci < 127)).astype(np.float32)
    c["ov"] = ov.astype(bf)
    bs = np.zeros((48, 48, 64), np.float32)
    for i in range(48):
        bs[i, i, :] = 1.0
    c["bsel"] = bs.reshape(48, 48 * 64).astype(bf)
    pc = np.ones((128, 4, 16), np.float32)
    for g, w in enumerate((2, 4, 8, 16)):
        t = np.arange(16)
        pc[:, g, :] = (w / np.minimum(t + 1, w))[None, :]
    c["poolcorr"] = pc
    return c


def host_weights(inp):
    f = np.float32
    w = {}
    A = lambda a: np.ascontiguousarray(np.asarray(a, dtype=f))
    w["w_in"] = A(inp["w_in"][0].reshape(8, 128, 4400).transpose(1, 0, 2))
    w["w_pool_br"] = A(inp["w_pool_br"][0].reshape(4, 128, 1024).transpose(1, 0, 2))
    w["w_attn_br"] = A(inp["w_attn_br"][0].reshape(8, 128, 1024).transpose(1, 0, 2))
    w["w_o"] = A(inp["w_o"][0].reshape(8, 128, 1024).transpose(1, 0, 2))
    w["w_up"] = A(inp["w_up"][0].reshape(8, 128, 5504).transpose(1, 0, 2))
    wd = np.zeros((22 * 128, 1024), f)
    wd[:D_FF] = inp["w_down"][0]
    w["w_down"] = A(wd.reshape(22, 128, 1024).transpose(1, 0, 2))
    w["pool_w"] = A(inp["pool_w"][0].transpose(1, 0, 2))
    w1 = np.asarray(inp["cmp_w1"][0]).reshape(2, 32, 64, 128).transpose(2, 0, 1, 3)
    w["cmp_w1"] = A(np.concatenate([w1, w1], 0))
    w["cmp_w2"] = A(np.asarray(inp["cmp_w2"][0]).transpose(1, 0, 2))
    pt = np.asarray(inp["cmp_pos"][0]).transpose(2, 0, 1)
    w["cmp_posT"] = A(np.concatenate([pt, pt], 0))
    w["cmp_b1"] = A(np.asarray(inp["cmp_b1"][0]).T)
    nrm = np.stack([np.asarray(inp["attn_norm_w"][0]).reshape(8, 128).T,
                    np.asarray(inp["ffn_norm_w"][0]).reshape(8, 128).T], 1)
    w["nrm"] = A(nrm)
    w["pool_scale"] = A(np.asarray(inp["pool_scale"][0]).reshape(4, 128).T)
    qk = np.concatenate([np.asarray(inp["q_norm_w"][0])[None], np.asarray(inp["k_norm_w"][0])], 0)
    w["qkw"] = A(np.broadcast_to(qk[None], (128, 4, 64)))
    cw = np.asarray(inp["conv_w"][0])
    cb = np.asarray(inp["conv_b"][0])
    cv = np.zeros((128, 2, 22, 4), f)
    for gv in range(2):
        for fc in range(22):
            rows = 128 if fc < 21 else 64
            sl = slice(gv * D_FF + fc * 128, gv * D_FF + fc * 128 + rows)
            cv[:rows, gv, fc, 0:3] = cw[:, sl].T
            cv[:rows, gv, fc, 3] = cb[sl]
    w["conv"] = cv
    return w


DRAM_SPECS = {
    "w_in": ([128, 8, 4400], F32), "w_pool_br": ([128, 4, 1024], F32), "w_attn_br": ([128, 8, 1024], F32),
    "w_o": ([128, 8, 1024], F32), "w_up": ([128, 8, 5504], F32), "w_down": ([128, 22, 1024], F32),
    "pool_w": ([128, 4, 128], F32), "cmp_w1": ([128, 2, 32, 128], F32), "cmp_w2": ([128, 2, 64], F32),
    "cmp_posT": ([128, 2, 32], F32), "cmp_b1": ([128, 2], F32), "nrm": ([128, 2, 8], F32),
    "pool_scale": ([128, 4], F32), "qkw": ([128, 4, 64], F32), "conv": ([128, 2, 22, 4], F32),
    "ident": ([128, 128], BF16), "cossin": ([128, 2, 16, 32], F32), "cmpmask": ([128, 2048], BF16),
    "tri": ([128, 256], BF16), "esel": ([32, 2048], BF16), "fb": ([128, 16, 32], F32),
    "ov": ([128, 32], BF16), "bsel": ([48, 3072], BF16), "poolcorr": ([128, 4, 16], F32),
}


def build(nseq=4, stop_after=99, dbg=False):
    nc = bass.Bass("TRN2", target_bir_lowering=False)
    sch = Sched()
    dr = {}
    for name, (shape, dt) in DRAM_SPECS.items():
        dr[name] = nc.dram_tensor(name, shape, dt, kind="ExternalInput").ap()
    x_d = nc.dram_tensor("x", [nseq, S, D], F32, kind="ExternalInput").ap()
    out_d = nc.dram_tensor("out", [nseq, S, D], F32, kind="ExternalOutput").ap()
    x1_d = nc.dram_tensor("x1scratch", [nseq + 1, S, D], F32, kind="Internal").ap()[1:nseq + 1]
    dbg_d = {}

    es = contextlib.ExitStack()
    ARENA_B = 206 * 1024
    arena = es.enter_context(nc.sbuf_tensor("arena", [128, ARENA_B // 2], BF16))
    cur = [0]

    def alloc(shape, dt=BF16):
        n = 1
        for s_ in shape[1:]:
            n *= s_
        nb = n * (4 if dt == F32 else 2)
        nb = (nb + 63) // 64 * 64
        off = cur[0]
        cur[0] += nb
        assert cur[0] <= ARENA_B, ("arena overflow", cur[0])
        ap = arena[:, off // 2:(off + nb) // 2]
        if dt == F32:
            ap = ap.bitcast(F32)
        ap = ap[:, 0:n]
        if len(shape) == 3:
            ap = ap.rearrange("p (a b) -> p a b", a=shape[1], b=shape[2])
        elif len(shape) == 4:
            ap = ap.rearrange("p (a b c) -> p a b c", a=shape[1], b=shape[2], c=shape[3])
        elif len(shape) == 5:
            ap = ap.rearrange("p (a b c d) -> p a b c d", a=shape[1], b=shape[2], c=shape[3], d=shape[4])
        return ap

    PB = [es.enter_context(nc.psum_tensor("pb%d" % i, [128, 512], F32)) for i in range(7)]
    BT = es.enter_context(nc.psum_tensor("pbt", [128, 1024], BF16))
    pbrot = [0]

    def nextbank(pool):
        i = pool[pbrot[0] % len(pool)]
        pbrot[0] += 1
        return i

    ident = alloc([128, 128]); cossin = alloc([128, 2, 16, 32], F32); cmpmask = alloc([128, 2048])
    tri = alloc([128, 256]); fb = alloc([128, 16, 32], F32); ov = alloc([128, 32])
    bsel = alloc([128, 3072]); poolcorr = alloc([128, 4, 16], F32); nrm = alloc([128, 2, 8], F32)
    pool_scale = alloc([128, 4], F32); qkw = alloc([128, 4, 64], F32); conv = alloc([128, 2, 22, 4], F32)
    b1 = alloc([128, 2], F32); chid = alloc([128, 2], F32); w2 = alloc([128, 2, 64]); pool_w = alloc([128, 4, 128])
    posT = alloc([128, 2, 32]); ones = alloc([128, 128]); mhalf = alloc([128, 16], F32)
    KCMP = alloc([128, 2, 128]); VCMP = alloc([128, 2, 128])
    ssq = alloc([128, 16], F32); rs = alloc([128, 16], F32); sr = alloc([128, 16], F32); rstd = alloc([128, 16], F32)
    mark0 = cur[0]
    hT = alloc([128, 8, S]); OT = alloc([128, 8, S])
    markX = cur[0]
    KAs = alloc([128, 2, S]); KAw = alloc([128, 2, S])
    VA = alloc([128, 2, 2, 16, 128]); G = alloc([128, S])
    Wsm = alloc([128, 8, 1024])
    w1 = Wsm.rearrange("p a b -> p (a b)").rearrange("p (k l h) -> p k l h", k=2, l=32, h=128)
    ZQ = alloc([128, 1024], F32); SQ = alloc([128, 1024], F32); QW = alloc([128, 1024], F32)
    TA = alloc([128, 512], F32); TB = alloc([128, 512], F32); TC = alloc([128, 512], F32); TD = alloc([128, 512], F32)
    QN = alloc([128, 1024]); KZb = alloc([128, 384], F32)
    markU = cur[0]
    xs = [alloc([128, 1024], F32) for _ in range(2)]
    hn = [alloc([128, 1024]) for _ in range(2)]
    endU1 = cur[0]
    cur[0] = markU
    KC = alloc([128, S]); VC = alloc([128, S])
    HID = alloc([128, 128]); XG = alloc([128, 128], F32); X2 = alloc([128, 128], F32); X3 = alloc([128, 128], F32)
    endU2 = cur[0]
    cur[0] = markU
    QA = alloc([128, 16, 512])
    PT = [alloc([128, 512]) for _ in range(3)]
    RR = alloc([128, 512], F32); R2 = alloc([128, 512], F32)
    U1 = alloc([128, 512], F32); U2 = alloc([128, 512], F32); UT = alloc([128, 512], F32)
    ECN = [alloc([128, 512]) for _ in range(2)]
    PSC = alloc([128, 512], F32); PSCb = alloc([128, 512])
    SC = alloc([128, 4, 32], F32); M8 = alloc([128, 4, 8], F32); SBf = alloc([128, 4, 32], F32); SBb = alloc([128, 4, 32])
    GSB = alloc([128, 512], F32); RCPb = alloc([128, 512], F32)
    endX1 = max(cur[0], endU1, endU2)
    cur[0] = markX
    YP = alloc([128, 4, S])
    markY = cur[0]
    UF = alloc([128, 16 + S], F32); SA = alloc([128, 16 + S], F32); SBp = alloc([128, 16 + S], F32); PLb = alloc([128, S])
    Wp = alloc([128, 8, 512])
    endX2 = cur[0]
    cur[0] = markY
    Wmg = alloc([128, 8, 2048]); Wpb = alloc([128, 4, 1024]); Wab = alloc([128, 8, 1024]); Wo = alloc([128, 8, 1024])
    MT = alloc([128, 8, 512]); SG = [alloc([128, 512], F32) for _ in range(2)]
    T0 = alloc([128, 512], F32); T1 = alloc([128, 512], F32)
    xs6 = [alloc([128, 1024], F32) for _ in range(2)]
    endX3 = cur[0]
    cur[0] = mark0
    Wup = alloc([128, 8, 5504]); Wdn = alloc([128, 22, 1024])
    H2 = [alloc([128, 8, 256]) for _ in range(2)]; X1S = [alloc([128, 1024], F32) for _ in range(4)]
    hn7 = [alloc([128, 1024]) for _ in range(2)]
    UE = [alloc([128, 2, 258], F32) for _ in range(3)]
    CG = [alloc([128, 256], F32) for _ in range(3)]; CV = [alloc([128, 256], F32) for _ in range(3)]
    SGF = [alloc([128, 256], F32) for _ in range(3)]; AF_ = [alloc([128, 256]) for _ in range(3)]
    HALO = alloc([128, 2, 22, 2], F32)
    endF = cur[0]
    print("arena bytes: X1 %d X2 %d X3 %d FFN %d" % (endX1, endX2, endX3, endF))

    def dump(name, ap, shape, dt=F32, reads=()):
        if not dbg:
            return
        d = nc.dram_tensor("dbg_" + name, shape, dt, kind="ExternalOutput").ap()
        dbg_d[name] = d
        sch.add("sp", lambda e, d=d, ap=ap: e.dma_start(out=d, in_=ap), r=reads, w=(), dma=("dbg_" + name, 1))

    def ld(dst, src, slot, wn, eng="sp"):
        sch.add(eng, lambda e: e.dma_start(out=dst, in_=src), r=(), w=(wn,), dma=("L" + wn, 1))

    ld(ident, dr["ident"], "c0", "c:ident"); ld(cossin, dr["cossin"], "c0", "c:cossin")
    ld(cmpmask, dr["cmpmask"], "c0", "c:cmpmask"); ld(tri, dr["tri"], "c0", "c:tri"); ld(fb, dr["fb"], "c0", "c:fb")
    ld(ov, dr["ov"], "c0", "c:ov"); ld(bsel[0:48], dr["bsel"], "c0", "c:bsel"); ld(poolcorr, dr["poolcorr"], "c0", "c:poolcorr")
    ld(nrm, dr["nrm"], "c0", "c:nrm"); ld(pool_scale, dr["pool_scale"], "c0", "c:pool_scale"); ld(qkw, dr["qkw"], "c0", "c:qkw")
    ld(conv, dr["conv"], "c0", "c:conv"); ld(b1, dr["cmp_b1"], "c0", "c:b1")
    ld(w2, dr["cmp_w2"], "c1", "c:w2", "pool"); ld(pool_w, dr["pool_w"], "c1", "c:pool_w", "pool")
    ld(posT, dr["cmp_posT"], "c1", "c:posT", "pool")
    sch.add("dve", lambda e: e.memset(ones, 1.0), w=("c:ones",))
    sch.add("dve", lambda e: e.memset(mhalf, -0.5), w=("c:mhalf",))
    sch.add("dve", lambda e: e.memset(KCMP, 0.0), w=("m:KCMP",))
    sch.add("dve", lambda e: e.memset(VCMP, 0.0), w=("m:VCMP",))
    sch.add("dve", lambda e: e.memset(VCMP[0:127, :, 64:128], 1.0), w=("m:VCMP",))

    scr = {}

    def pieces(a_, b_):
        out = []
        for k in range(a_.shape[1]):
            if len(a_.shape) == 3 and a_.shape[2] > 2816:
                w_ = a_.shape[2] // 4
                for i in range(4):
                    out.append((a_[:, k, i * w_:(i + 1) * w_], b_[:, k, i * w_:(i + 1) * w_]))
            else:
                out.append((a_[:, k], b_[:, k]))
        return out

    def wload(dst, srcap, wn, slot, nk, s=0, key=None, cast_fn=None):
        key = key or slot
        if key not in scr:
            scr[key] = nc.dram_tensor("wscr_" + key, list(dst.shape), BF16, kind="Internal").ap()
        sc = scr[key]
        if s == 0:
            if cast_fn is None:
                sch.add("pool", lambda e: [e.dma_start(out=dst[:, k, :], in_=srcap[:, k, :]) for k in range(nk)], w=(wn,), dma=(slot, nk))
            else:
                cast_fn()
            sch.add("sp", lambda e: [e.dma_start(out=a_, in_=b_) for a_, b_ in pieces(sc, dst)], r=(wn,), w=("scr:" + key,), dma=("st_" + key, len(pieces(sc, dst))))
        else:
            sch.add("sp", lambda e: [e.dma_start(out=a_, in_=b_) for a_, b_ in pieces(dst, sc)], r=("scr:" + key,), w=(wn,), dma=("ld_" + key, len(pieces(dst, sc))))

    def mm(e, out, lhsT, rhs, start, stop):
        return e.matmul(out, lhsT=lhsT, rhs=rhs, start=start, stop=stop)

    def norm_A(src_tile, src_name, hn_t, hn_name, stat_col):
        sc = slice(stat_col, stat_col + 1)
        k_ = str(stat_col)
        sch.add("act", lambda e: e.activation(out=hn_t, in_=src_tile, func=AF.Square, accum_out=ssq[:, sc]),
                r=(src_name,), w=(hn_name, "t:ssq" + k_))
        sch.add("dve", lambda e: e.tensor_scalar(out=rs[:, sc], in0=ssq[:, sc], scalar1=1.0 / D, scalar2=EPS,
                                                 op0=ALU.mult, op1=ALU.add), r=("t:ssq" + k_,), w=("t:rs" + k_,))
        sch.add("pool", lambda e: e.tensor_tensor(out=rstd[:, sc], in0=rs[:, sc], in1=mhalf[:, sc], op=ALU.pow), r=("t:rs" + k_, "c:mhalf"), w=("t:rstd" + k_,))

    def norm_B1(src_tile, src_name, hn_t, hn_name, stat_col):
        sc = slice(stat_col, stat_col + 1)
        sch.add("act", lambda e: e.activation(out=hn_t, in_=src_tile, func=AF.Copy, scale=rstd[:, sc]),
                r=(src_name, "t:rstd" + str(stat_col)), w=(hn_name,))

    def norm_B2(hn_t, hn_name, nrm_idx, dst, dst_name, col0):
        btv = BT[:, 0:1024].rearrange("p (a b) -> p a b", a=8, b=128)
        sch.add("pe", lambda e: [e.transpose(btv[:, kc, :], hn_t[:, kc * 128:(kc + 1) * 128], ident) for kc in range(8)],
                r=(hn_name, "c:ident"), w=("BT",))
        sch.add("dve", lambda e: e.tensor_tensor(out=dst[:, :, col0:col0 + 128], in0=btv,
                                                 in1=nrm[:, nrm_idx, :].unsqueeze(2).to_broadcast([128, 8, 128]),
                                                 op=ALU.mult), r=("BT", "c:nrm"), w=(dst_name,))

    def norm_tile(src_tile, src_name, hn_t, hn_name, nrm_idx, dst, dst_name, col0, stat_col):
        norm_A(src_tile, src_name, hn_t, hn_name, stat_col)
        norm_B1(src_tile, src_name, hn_t, hn_name, stat_col)
        norm_B2(hn_t, hn_name, nrm_idx, dst, dst_name, col0)

    def rope_A(Z, zname, nh, widx_ap, eps_eff):
        n = nh * 64
        sqv = SQ[:, 0:n].rearrange("p (h d) -> p h d", h=nh, d=64)
        qwv = QW[:, 0:n].rearrange("p (h d) -> p h d", h=nh, d=64)
        sch.add("dve", lambda e: e.tensor_tensor(out=sqv, in0=Z, in1=Z, op=ALU.mult), r=(zname,), w=("t:SQ",))
        sch.add("dve", lambda e: e.tensor_reduce(out=ssq[:, 0:nh], in_=sqv, axis=AX.X, op=ALU.add), r=("t:SQ",), w=("t:ssq",))
        sch.add("dve", lambda e: e.tensor_scalar(out=rs[:, 0:nh], in0=ssq[:, 0:nh], scalar1=eps_eff[0], scalar2=eps_eff[1],
                                                 op0=ALU.mult, op1=ALU.add), r=("t:ssq",), w=("t:rs",))
        sch.add("pool", lambda e: e.tensor_tensor(out=rstd[:, 0:nh], in0=rs[:, 0:nh], in1=mhalf[:, 0:nh], op=ALU.pow), r=("t:rs", "c:mhalf"), w=("t:rstd",))
        sch.add("dve", lambda e: e.tensor_tensor(out=qwv, in0=Z, in1=widx_ap, op=ALU.mult), r=(zname, "c:qkw"), w=("t:QW",))

    def rope_B(nh, tt, out_bf, out_name):
        n = nh * 64
        qwv = QW[:, 0:n].rearrange("p (h d) -> p h d", h=nh, d=64)
        h = nh * 32
        cosb = cossin[:, 0, tt, :].unsqueeze(1).to_broadcast([128, nh, 32])
        sinb = cossin[:, 1, tt, :].unsqueeze(1).to_broadcast([128, nh, 32])
        v3 = lambda T: T[:, 0:h].rearrange("p (h d) -> p h d", h=nh, d=32)
        q1 = qwv[:, :, 0:32]
        q2 = qwv[:, :, 32:64]
        sch.add("pool", lambda e: e.tensor_tensor(out=v3(TA), in0=q1, in1=cosb, op=ALU.mult), r=("t:QW", "c:cossin"), w=("t:TA",))
        sch.add("pool", lambda e: e.tensor_tensor(out=v3(TB), in0=q2, in1=sinb, op=ALU.mult), r=("t:QW", "c:cossin"), w=("t:TB",))
        sch.add("pool", lambda e: e.tensor_tensor(out=v3(TC), in0=q2, in1=cosb, op=ALU.mult), r=("t:QW", "c:cossin"), w=("t:TC",))
        sch.add("pool", lambda e: e.tensor_tensor(out=v3(TD), in0=q1, in1=sinb, op=ALU.mult), r=("t:QW", "c:cossin"), w=("t:TD",))
        sch.add("dve", lambda e: e.tensor_tensor(out=v3(TA), in0=v3(TA), in1=v3(TB), op=ALU.subtract), r=("t:TA", "t:TB"), w=("t:TA",))
        sch.add("dve", lambda e: e.tensor_tensor(out=v3(TC), in0=v3(TC), in1=v3(TD), op=ALU.add), r=("t:TC", "t:TD"), w=("t:TC",))
        rb = rstd[:, 0:nh].unsqueeze(2).to_broadcast([128, nh, 32])
        sch.add("dve", lambda e: e.tensor_tensor(out=out_bf[:, :, 0:32], in0=v3(TA), in1=rb, op=ALU.mult), r=("t:TA", "t:rstd"), w=(out_name,))
        sch.add("dve", lambda e: e.tensor_tensor(out=out_bf[:, :, 32:64], in0=v3(TC), in1=rb, op=ALU.mult), r=("t:TC", "t:rstd"), w=(out_name,))


    UPFX = ("u1:", "u2:", "t:", "x1:QA")

    def do_seq(s):
        for tt in range(16):
            xt = xs[tt % 2]
            xn = "u1:xs%d" % (tt % 2)
            sch.add("sp", lambda e, xt=xt, tt=tt: e.dma_start(out=xt, in_=x_d[s, tt * 128:(tt + 1) * 128, :]),
                    w=(xn,), dma=("xs%d" % (tt % 2), 1))
            norm_tile(xt, xn, hn[tt % 2], "u1:hn%d" % (tt % 2), 0, hT, "m:hT%d" % (tt // 4), tt * 128, 0)
        if s == 0:
            dump("hT", hT, [128, 8, S], BF16, reads=["m:hT%d" % i for i in range(4)])
        if stop_after <= 1:
            return
        sch.barrier(UPFX)
        sch.add("sp", lambda e: [e.dma_start(out=KAs[64:96, g, :], in_=dr["esel"]) for g in range(2)], w=("x1:KAs_e",), dma=("esel", 2))
        sch.add("pool", lambda e: e.memset(VA[:, :, :, :, 64:128], 1.0), w=("x1:VA1",))
        WA = Wsm[:, :, 0:816]
        wload(WA, dr["w_in"][:, :, 1536:2352], "x1:Wsm", "wsm", 8, s, "wa")
        ZK = ZQ[:, 0:768]
        zk5 = ZK.rearrange("p (b k g d) -> p b k g d", b=3, k=2, g=2, d=64)
        def p2_tile(tt):
            c = tt // 4
            KZ = KZb.rearrange("p (b g d) -> p b g d", b=3, g=2, d=64)

            def s_mm():
                def kvmm(e, tt=tt):
                    ins = []
                    for kc in range(8):
                        ins.append(mm(e, PB[0][:, 0:512], hT[:, kc, tt * 128:(tt + 1) * 128], WA[:, kc, 0:512], kc == 0, kc == 7))
                    for kc in range(8):
                        ins.append(mm(e, PB[1][:, 0:256], hT[:, kc, tt * 128:(tt + 1) * 128], WA[:, kc, 512:768], kc == 0, kc == 7))
                    return ins
                sch.add("pe", kvmm, r=("m:hT%d" % c, "x1:Wsm"), w=("B0", "B1"))

            def s_copy():
                sch.add("act", lambda e: e.activation(out=ZK[:, 0:512], in_=PB[0][:, 0:512], func=AF.Copy), r=("B0",), w=("t:ZK",))
                sch.add("act", lambda e: e.activation(out=ZK[:, 512:768], in_=PB[1][:, 0:256], func=AF.Copy), r=("B1",), w=("t:ZK",))

            def s_A():
                sch.add("pool", lambda e, tt=tt: e.tensor_copy(out=VA[:, :, :, tt, 0:64], in_=zk5[:, 1:3, 1, :, :]), r=("t:ZK",), w=("x1:VA",))
                sch.add("pool", lambda e: e.tensor_copy(out=KZ, in_=zk5[:, :, 0, :, :]), r=("t:ZK",), w=("t:KZ",))

            def s_B():
                KZ3 = KZb.rearrange("p (h d) -> p h d", h=6, d=64)
                kw_ap = qkw[:, 1:4, :].unsqueeze(2).to_broadcast([128, 3, 2, 64])
                KN = QN[:, 0:384].rearrange("p (h d) -> p h d", h=6, d=64)
                n = 384
                sqv = SQ[:, 0:n].rearrange("p (h d) -> p h d", h=6, d=64)
                qwv = QW[:, 0:n].rearrange("p (h d) -> p h d", h=6, d=64)
                qwv4 = QW[:, 0:n].rearrange("p (b g d) -> p b g d", b=3, g=2, d=64)
                sch.add("dve", lambda e: e.tensor_tensor(out=sqv, in0=KZ3, in1=KZ3, op=ALU.mult), r=("t:KZ",), w=("t:SQ",))
                sch.add("dve", lambda e: e.tensor_reduce(out=ssq[:, 0:6], in_=sqv, axis=AX.X, op=ALU.add), r=("t:SQ",), w=("t:ssq",))
                sch.add("dve", lambda e: e.tensor_scalar(out=rs[:, 0:6], in0=ssq[:, 0:6], scalar1=1.0 / 64, scalar2=EPS,
                                                         op0=ALU.mult, op1=ALU.add), r=("t:ssq",), w=("t:rs",))
                sch.add("pool", lambda e: e.tensor_tensor(out=rstd[:, 0:6], in0=rs[:, 0:6], in1=mhalf[:, 0:6], op=ALU.pow), r=("t:rs", "c:mhalf"), w=("t:rstd",))
                sch.add("dve", lambda e: e.tensor_tensor(out=qwv4, in0=KZ, in1=kw_ap, op=ALU.mult), r=("t:KZ", "c:qkw"), w=("t:QW",))
                nh = 6
                h_ = nh * 32
                cosb = cossin[:, 0, tt, :].unsqueeze(1).to_broadcast([128, nh, 32])
                sinb = cossin[:, 1, tt, :].unsqueeze(1).to_broadcast([128, nh, 32])
                v3 = lambda T: T[:, 0:h_].rearrange("p (h d) -> p h d", h=nh, d=32)
                q1 = qwv[:, :, 0:32]
                q2 = qwv[:, :, 32:64]
                sch.add("pool", lambda e, cosb=cosb, q1=q1: e.tensor_tensor(out=v3(TA), in0=q1, in1=cosb, op=ALU.mult), r=("t:QW", "c:cossin"), w=("t:TA",))
                sch.add("pool", lambda e, sinb=sinb, q2=q2: e.tensor_tensor(out=v3(TB), in0=q2, in1=sinb, op=ALU.mult), r=("t:QW", "c:cossin"), w=("t:TB",))
                sch.add("pool", lambda e, cosb=cosb, q2=q2: e.tensor_tensor(out=v3(TC), in0=q2, in1=cosb, op=ALU.mult), r=("t:QW", "c:cossin"), w=("t:TC",))
                sch.add("pool", lambda e, sinb=sinb, q1=q1: e.tensor_tensor(out=v3(TD), in0=q1, in1=sinb, op=ALU.mult), r=("t:QW", "c:cossin"), w=("t:TD",))
                sch.add("dve", lambda e: e.tensor_tensor(out=v3(TA), in0=v3(TA), in1=v3(TB), op=ALU.subtract), r=("t:TA", "t:TB"), w=("t:TA",))
                sch.add("dve", lambda e: e.tensor_tensor(out=v3(TC), in0=v3(TC), in1=v3(TD), op=ALU.add), r=("t:TC", "t:TD"), w=("t:TC",))
                rb = rstd[:, 0:nh].unsqueeze(2).to_broadcast([128, nh, 32])
                sch.add("dve", lambda e, rb=rb: e.tensor_tensor(out=KN[:, :, 0:32], in0=v3(TA), in1=rb, op=ALU.mult), r=("t:TA", "t:rstd"), w=("t:KN",))
                sch.add("dve", lambda e, rb=rb: e.tensor_tensor(out=KN[:, :, 32:64], in0=v3(TC), in1=rb, op=ALU.mult), r=("t:TC", "t:rstd"), w=("t:KN",))
                btv = BT[:, 0:384].rearrange("p (a b) -> p a b", a=3, b=128)
                sch.add("pe", lambda e: [e.transpose(btv[:, b, :], QN[:, b * 128:(b + 1) * 128], ident) for b in range(3)],
                        r=("t:KN", "c:ident"), w=("BT",))
                cs = slice(tt * 128, (tt + 1) * 128)
                sch.add("dve", lambda e, cs=cs: e.tensor_copy(out=KC[:, cs], in_=btv[:, 0, :]), r=("BT",), w=("u2:KC",))
                sch.add("dve", lambda e, cs=cs: e.tensor_copy(out=KAs[0:64, 0, cs], in_=btv[0:64, 1, :]), r=("BT",), w=("x1:KAs",))
                sch.add("dve", lambda e, cs=cs: e.tensor_copy(out=KAs[0:64, 1, cs], in_=btv[64:128, 1, :]), r=("BT",), w=("x1:KAs",))
                sch.add("dve", lambda e, cs=cs: e.tensor_copy(out=KAw[0:64, 0, cs], in_=btv[0:64, 2, :]), r=("BT",), w=("x1:KAw",))
                sch.add("dve", lambda e, cs=cs: e.tensor_copy(out=KAw[0:64, 1, cs], in_=btv[64:128, 2, :]), r=("BT",), w=("x1:KAw",))
            return s_mm, s_copy, s_A, s_B
        pt_ = [p2_tile(tt) for tt in range(16)]
        pt_[0][0](); pt_[0][1]()
        for tt in range(16):
            if tt + 1 < 16:
                pt_[tt + 1][0]()
            pt_[tt][2]()
            if tt + 1 < 16:
                pt_[tt + 1][1]()
            pt_[tt][3]()
        for c in range(4):
            cc = slice(c * 512, (c + 1) * 512)
            sch.add("pe", lambda e, cc=cc: [mm(e, PB[2][:, :], WA[:, kc, 128:256], hT[:, kc, cc], kc == 0, kc == 7) for kc in range(8)],
                    r=("m:hT%d" % c, "x1:Wsm"), w=("B2",))
            sch.add("act", lambda e, cc=cc: e.activation(out=VC[:, cc], in_=PB[2][:, :], func=AF.Copy), r=("B2",), w=("u2:VC",))
            sch.add("pe", lambda e, cc=cc: [mm(e, PB[3][0:48, :], WA[:, kc, 768:816], hT[:, kc, cc], kc == 0, kc == 7) for kc in range(8)],
                    r=("m:hT%d" % c, "x1:Wsm"), w=("B3",))
            sch.add("act", lambda e, cc=cc: e.activation(out=G[0:48, cc], in_=PB[3][0:48, :], func=AF.Sigmoid), r=("B3",), w=("x1:G",))
        if s == 0:
            dump("KAs", KAs, [128, 2, S], BF16, reads=("x1:KAs", "x1:KAs_e"))
            dump("KAw", KAw, [128, 2, S], BF16, reads=("x1:KAw",))
            dump("KC", KC, [128, S], BF16, reads=("u2:KC",))
            dump("VC", VC, [128, S], BF16, reads=("u2:VC",))
            dump("VA", VA.rearrange("p a b c d -> p (a b c d)"), [128, 2 * 2 * 16 * 128], BF16, reads=("x1:VA", "x1:VA1"))
            dump("G", G[0:48], [48, S], BF16, reads=("x1:G",))
        if stop_after <= 2:
            return
        wload(w1, None, "x1:Wsm", "wsm", 8, s, "w1", cast_fn=lambda: sch.add("pool", lambda e: [e.dma_start(out=w1[:, kv, 8 * i:8 * (i + 1), :], in_=dr["cmp_w1"][:, kv, 8 * i:8 * (i + 1), :]) for kv in range(2) for i in range(4)], w=("x1:Wsm",), dma=("wsm", 8)))
        if s == 0:
            def chm(e):
                ins = []
                for kv in range(2):
                    for l in range(32):
                        ins.append(mm(e, PB[6][:, kv:kv + 1], w1[0:64, kv, l, :], posT[0:64, kv, l:l + 1], l == 0, l == 31))
                return ins
            sch.add("pe", chm, r=("x1:Wsm", "c:posT"), w=("B6",))
            sch.add("dve", lambda e: e.tensor_tensor(out=chid, in0=PB[6][:, 0:2], in1=b1, op=ALU.add), r=("B6", "c:b1"), w=("c:chid",))
        for kv in range(2):
            for g in range(2):
                src = KC if kv == 0 else VC
                srcn = "u2:KC" if kv == 0 else "u2:VC"
                pr = slice(g * 64, (g + 1) * 64)

                def cm(e, src=src, pr=pr, kv=kv):
                    return [mm(e, PB[4][:, 0:127], w1[pr, kv, l, :], src[pr, l:l + 16 * 126 + 1:16], l == 0, l == 31) for l in range(32)]
                sch.add("pe", cm, r=(srcn, "x1:Wsm"), w=("B4",))
                xg = XG[:, 0:127]; x2 = X2[:, 0:127]; x3 = X3[:, 0:127]
                sch.add("act", lambda e, kv=kv: e.activation(out=xg, in_=PB[4][:, 0:127], func=AF.Identity, bias=chid[:, kv:kv + 1]),
                        r=("B4", "c:chid"), w=("u2:XG",))
                sch.add("dve", lambda e: e.tensor_tensor(out=x2, in0=xg, in1=xg, op=ALU.mult), r=("u2:XG",), w=("u2:X2",))
                sch.add("dve", lambda e: e.tensor_tensor(out=x3, in0=x2, in1=xg, op=ALU.mult), r=("u2:X2", "u2:XG"), w=("u2:X3",))
                sch.add("dve", lambda e: e.scalar_tensor_tensor(out=x2, in0=x3, scalar=0.044715, in1=xg, op0=ALU.mult, op1=ALU.add),
                        r=("u2:X3", "u2:XG"), w=("u2:X2",))
                sch.add("act", lambda e: e.activation(out=x3, in_=x2, func=AF.Sigmoid, scale=1.5957691216057308), r=("u2:X2",), w=("u2:X3",))
                sch.add("dve", lambda e: e.tensor_tensor(out=HID[:, 0:127], in0=xg, in1=x3, op=ALU.mult), r=("u2:XG", "u2:X3"), w=("u2:HID",))
                if kv == 0:
                    sch.add("pe", lambda e: mm(e, PB[5][0:64, 0:127], w2[:, 0, :], HID[:, 0:127], True, True), r=("u2:HID", "c:w2"), w=("B5",))
                    sch.add("act", lambda e, g=g: e.activation(out=KCMP[0:64, g, 0:127], in_=PB[5][0:64, 0:127], func=AF.Copy), r=("B5",), w=("m:KCMP",))
                else:
                    sch.add("pe", lambda e: mm(e, PB[5][0:127, 0:64], HID[:, 0:127], w2[:, 1, :], True, True), r=("u2:HID", "c:w2"), w=("B5",))
                    sch.add("act", lambda e, g=g: e.activation(out=VCMP[0:127, g, 0:64], in_=PB[5][0:127, 0:64], func=AF.Copy), r=("B5",), w=("m:VCMP",))
        if s == 0:
            dump("KCMP", KCMP, [128, 2, 128], BF16, reads=("m:KCMP",))
            dump("VCMP", VCMP, [128, 2, 128], BF16, reads=("m:VCMP",))
        if stop_after <= 3:
            return
        sch.barrier(UPFX)
        WQ = Wsm
        wload(WQ, dr["w_in"][:, :, 512:1536], "x1:Wsm", "wsm", 8, s, "wq")
        def attn_chunk(c):
            cc = slice(c * 512, (c + 1) * 512)
            def q_tile(tl):
                tt = 4 * c + tl
                ts_ = slice(tt * 128, (tt + 1) * 128)
                Z3 = ZQ.rearrange("p (h d) -> p h d", h=16, d=64)
                QN3 = QN.rearrange("p (h d) -> p h d", h=16, d=64)
                wq_ap = qkw[:, 0, :].unsqueeze(1).to_broadcast([128, 16, 64])
                btv = BT[:, 0:1024].rearrange("p (a b) -> p a b", a=8, b=128)
                QAe = QA.rearrange("p (i two) q -> p two i q", two=2)
                ls = slice(tl * 128, (tl + 1) * 128)

                def s_mm():
                    def qmm(e):
                        ins = []
                        for hf in range(2):
                            for kc in range(8):
                                ins.append(mm(e, PB[5 + hf][:, :], hT[:, kc, ts_], WQ[:, kc, hf * 512:(hf + 1) * 512], kc == 0, kc == 7))
                        return ins
                    sch.add("pe", qmm, r=("m:hT%d" % c, "x1:Wsm"), w=("B5", "B6"))

                def s_copy():
                    sch.add("act", lambda e: e.activation(out=ZQ[:, 0:512], in_=PB[5][:, :], func=AF.Copy), r=("B5",), w=("t:ZQ",))
                    sch.add("act", lambda e: e.activation(out=ZQ[:, 512:1024], in_=PB[6][:, :], func=AF.Copy), r=("B6",), w=("t:ZQ",))

                def s_A():
                    rope_A(Z3, "t:ZQ", 16, wq_ap, (1.0, 64 * EPS))

                def s_B():
                    rope_B(16, tt, QN3, "t:QN")
                    sch.add("pe", lambda e: [e.transpose(btv[:, i, :], QN[:, i * 128:(i + 1) * 128], ident) for i in range(8)],
                            r=("t:QN", "c:ident"), w=("BT",))
                    sch.add("dve", lambda e: e.tensor_copy(out=QAe[0:64, 0, :, ls], in_=btv[0:64, :, :]), r=("BT",), w=("x1:QA",))
                    sch.add("dve", lambda e: e.tensor_copy(out=QAe[0:64, 1, :, ls], in_=btv[64:128, :, :]), r=("BT",), w=("x1:QA",))
                return s_mm, s_copy, s_A, s_B
            qt = [q_tile(tl) for tl in range(4)]
            qt[0][0](); qt[0][1]()
            for tl in range(4):
                if tl + 1 < 4:
                    qt[tl + 1][0]()
                qt[tl][2]()
                if tl + 1 < 4:
                    qt[tl + 1][1]()
                qt[tl][3]()
            if s == 0 and c == 1:
                dump("QA", QA.rearrange("p a b -> p (a b)"), [128, 16 * 512], BF16, reads=("x1:QA",))
            if stop_after <= 4:
                return
            SB_ = [0, 1, 2]
            OB_ = [3, 4]
            for g in range(2):
                def stA(hl, g=g):
                    h = g * 8 + hl
                    sb = PB[SB_[h % 3]]; sbn = "B%d" % SB_[h % 3]
                    pt = PT[h % 3]; ptn = "t:PT%d" % (h % 3)
                    sch.add("pe", lambda e: [mm(e, sb[:, :], KCMP[0:64, g, :], QA[0:64, h, :], True, False),
                                             mm(e, sb[:, :], ident, cmpmask[:, cc], False, True)],
                            r=("m:KCMP", "x1:QA", "c:ident", "c:cmpmask"), w=(sbn,))
                    sch.add("act", lambda e: e.activation(out=pt, in_=sb[:, :], func=AF.Exp), r=(sbn,), w=(ptn,))

                def stB(hl, g=g):
                    h = g * 8 + hl
                    pt = PT[h % 3]; ptn = "t:PT%d" % (h % 3)
                    ecn = ECN[h % 2]; ecnn = "t:ECN%d" % (h % 2)
                    sch.add("pe", lambda e: mm(e, PB[5][:, :], ones, pt, True, True), r=(ptn, "c:ones"), w=("B5",))
                    sch.add("act", lambda e: e.activation(out=RR, in_=PB[5][:, :], func=AF.Ln, bias=1e-18), r=("B5",), w=("t:RR",))
                    sch.add("act", lambda e: e.activation(out=R2, in_=RR, func=AF.Exp, scale=-1.0), r=("t:RR",), w=("t:R2",))
                    sch.add("dve", lambda e: e.tensor_tensor(out=ecn, in0=pt, in1=R2, op=ALU.mult), r=(ptn, "t:R2"), w=(ecnn,))
                    if hl == 0:
                        sch.add("pool", lambda e: e.tensor_copy(out=PSC, in_=ecn), r=(ecnn,), w=("t:PSC",))
                    else:
                        sch.add("pool", lambda e: e.tensor_tensor(out=PSC, in0=PSC, in1=ecn, op=ALU.add), r=(ecnn, "t:PSC"), w=("t:PSC",))

                def stC(hl, g=g):
                    h = g * 8 + hl
                    ob = PB[OB_[h % 2]]; obn = "B%d" % OB_[h % 2]
                    ecn = ECN[h % 2]; ecnn = "t:ECN%d" % (h % 2)
                    sch.add("pe", lambda e: mm(e, ob[0:64, :], VCMP[:, g, 0:64], ecn, True, True), r=(ecnn, "m:VCMP"), w=(obn,))
                    ci = 3 * h + 0
                    sch.add("pe", lambda e: mm(e, PB[6][0:64, :], bsel[0:48, ci * 64:(ci + 1) * 64], G[0:48, cc], True, True),
                            r=("x1:G", "c:bsel"), w=("B6",))
                    sch.add("act", lambda e: e.activation(out=GSB[0:64, :], in_=PB[6][0:64, :], func=AF.Copy), r=("B6",), w=("t:GSB",))
                    pr = slice((h % 2) * 64, (h % 2) * 64 + 64)
                    sch.add("dve", lambda e: e.tensor_tensor(out=OT[pr, h // 2, cc], in0=ob[0:64, :], in1=GSB[0:64, :], op=ALU.mult),
                            r=(obn, "t:GSB"), w=("m:OT%d_%d" % (c, h),))
                for k in range(8 + 2):
                    if k < 8:
                        stA(k)
                    if 0 <= k - 1 < 8:
                        stB(k - 1)
                    if 0 <= k - 2 < 8:
                        stC(k - 2)
                sch.add("pool", lambda e: e.tensor_copy(out=PSCb, in_=PSC), r=("t:PSC",), w=("t:PSCb",))
                impv = PB[5][:, 0:128].rearrange("p (a b) -> p a b", a=4, b=32)
                sch.add("pe", lambda e: [mm(e, impv[:, qb, :], PSCb[:, qb * 128:(qb + 1) * 128], ov, True, True) for qb in range(4)],
                        r=("t:PSCb", "c:ov"), w=("B5",))
                sch.add("dve", lambda e, c=c: e.tensor_tensor(out=SC, in0=impv, in1=fb[:, 4 * c:4 * c + 4, :], op=ALU.add), r=("B5", "c:fb"), w=("t:SC",))
                for qb in range(4):
                    sch.add("dve", lambda e, qb=qb: e.max(out=M8[:, qb, :], in_=SC[:, qb, :]), r=("t:SC",), w=("t:M8",))
                    sch.add("dve", lambda e, qb=qb: e.tensor_scalar(out=SBf[:, qb, :], in0=SC[:, qb, :], scalar1=M8[:, qb, 7:8], scalar2=1.0,
                                                                    op0=ALU.is_ge, op1=ALU.subtract), r=("t:SC", "t:M8"), w=("t:SBf",))
                sch.add("dve", lambda e: e.tensor_scalar(out=SBb, in0=SBf, scalar1=-NEGB, scalar2=None, op0=ALU.mult), r=("t:SBf",), w=("t:SBb",))
                sch.add("pe", lambda e: [e.transpose(BT[0:32, qb * 128:(qb + 1) * 128], SBb[:, qb, :], ident) for qb in range(4)],
                        r=("t:SBb", "c:ident"), w=("BT",))
                sch.add("dve", lambda e, g=g: e.tensor_copy(out=QA[64:96, g * 8:(g + 1) * 8, :],
                                                            in_=BT[0:32, 0:512].unsqueeze(1).to_broadcast([32, 8, 512])),
                        r=("BT",), w=("x1:QAs",))
                if s == 0 and c == 1 and g == 0:
                    dump("SC", SC.rearrange("p a b -> p (a b)"), [128, 128], F32, reads=("t:SC",))
                    dump("SBf", SBf.rearrange("p a b -> p (a b)"), [128, 128], F32, reads=("t:SBf",))
            if stop_after <= 5:
                return
            jobs = []
            for h in range(16):
                g = h // 8
                tl_ = []
                for kt in range(0, 4 * c + 4):
                    j = kt - 4 * c
                    lo = 0 if j < 0 else 128 * j
                    tl_.append(dict(kt=kt, lo=lo, hi=512, mask=(None if j < 0 else (0, lo))))
                jobs.append(dict(h=h, g=g, br=1, tiles=tl_))
                tl_ = []
                order = [4] + ([0, 1, 2, 3] if c >= 1 else []) + [5, 6, 7]
                for i in order:
                    kt = 4 * c - 4 + i
                    if i <= 3:
                        tl_.append(dict(kt=kt, lo=0, hi=128 * (i + 1), mask=(1, 128 * i)))
                    else:
                        tl_.append(dict(kt=kt, lo=128 * (i - 4), hi=512, mask=(0, 128 * (i - 4))))
                jobs.append(dict(h=h, g=g, br=2, tiles=tl_))
            flat = []
            for jb_i, jb in enumerate(jobs):
                for ti, t in enumerate(jb["tiles"]):
                    flat.append((jb_i, ti))
            srot = [0]

            def rec_S(k):
                jb_i, ti = flat[k]
                jb = jobs[jb_i]; t = jb["tiles"][ti]
                si = k % 3
                sb = PB[SB_[si]]; t["si"] = si
                lo, hi, kt, g, h = t["lo"], t["hi"], t["kt"], jb["g"], jb["h"]
                ks = slice(kt * 128, (kt + 1) * 128)
                if jb["br"] == 1:
                    lhs = KAs[0:96, g, ks]; rhs = QA[0:96, h, lo:hi]; rn = ("x1:KAs", "x1:KAs_e", "x1:QA", "x1:QAs")
                else:
                    lhs = KAw[0:64, g, ks]; rhs = QA[0:64, h, lo:hi]; rn = ("x1:KAw", "x1:QA")
                mk = t["mask"]

                def f(e):
                    ins = [mm(e, sb[:, lo:hi], lhs, rhs, True, mk is None)]
                    if mk is not None:
                        ins.append(mm(e, sb[:, mk[1]:mk[1] + 128], ident, tri[:, mk[0] * 128:(mk[0] + 1) * 128], False, True))
                    return ins
                sch.add("pe", f, r=rn + ("c:ident", "c:tri"), w=("B%d" % SB_[si],))
                pt = PT[si]
                sch.add("act", lambda e: e.activation(out=pt[:, lo:hi], in_=sb[:, lo:hi], func=AF.Exp), r=("B%d" % SB_[si],), w=("t:PT%d" % si,))

            def rec_PV(k):
                jb_i, ti = flat[k]
                jb = jobs[jb_i]; t = jb["tiles"][ti]
                si = t["si"]
                oi = jb_i % 2
                ob = PB[OB_[oi]]; obn = "B%d" % OB_[oi]
                lo, hi, kt, g, h, br = t["lo"], t["hi"], t["kt"], jb["g"], jb["h"], jb["br"]
                pt = PT[si]
                first = ti == 0
                last = ti == len(jb["tiles"]) - 1
                sch.add("pe", lambda e: mm(e, ob[:, lo:hi], VA[:, br - 1, g, kt, :], pt[:, lo:hi], first, last),
                        r=("t:PT%d" % si, "x1:VA", "x1:VA1"), w=(obn,))
                if last:
                    ci = 3 * h + br
                    sch.add("pe", lambda e: mm(e, PB[6][0:64, :], bsel[0:48, ci * 64:(ci + 1) * 64], G[0:48, cc], True, True),
                            r=("x1:G", "c:bsel"), w=("B6",))
                    sch.add("dve", lambda e: e.reciprocal(out=RCPb[0:64, :], in_=ob[64:128, :]), r=(obn,), w=("t:RCP",))
                    sch.add("dve", lambda e: e.tensor_tensor(out=R2[0:64, :], in0=RCPb[0:64, :], in1=PB[6][0:64, :], op=ALU.mult), r=("t:RCP", "B6"), w=("t:R2",))
                    pr = slice((h % 2) * 64, (h % 2) * 64 + 64)
                    if br == 1:
                        sch.add("dve", lambda e: e.tensor_tensor(out=U1[pr, :], in0=ob[0:64, :], in1=R2[0:64, :], op=ALU.mult), r=(obn, "t:R2"), w=("t:U1",))
                    else:
                        sch.add("dve", lambda e: e.tensor_tensor(out=U2[pr, :], in0=ob[0:64, :], in1=R2[0:64, :], op=ALU.mult), r=(obn, "t:R2"), w=("t:U2",))
                        otn = "m:OT%d_%d" % (c, h)
                        sch.add("pool", lambda e: e.tensor_tensor(out=UT[pr, :], in0=U1[pr, :], in1=U2[pr, :], op=ALU.add), r=("t:U1", "t:U2"), w=("t:UT",))
                        sch.add("pool", lambda e: e.tensor_tensor(out=OT[pr, h // 2, cc], in0=OT[pr, h // 2, cc], in1=UT[pr, :], op=ALU.add),
                                r=("t:UT", otn), w=(otn,))
            DEPTH = 2
            for k in range(len(flat) + DEPTH):
                if k < len(flat):
                    rec_S(k)
                if k - DEPTH >= 0:
                    rec_PV(k - DEPTH)
        for c_i in range(4):
            attn_chunk(c_i)
        if s == 0:
            dump("OT", OT, [128, 8, S], BF16, reads=["m:OT%d_%d" % (c_, h_) for c_ in range(4) for h_ in range(16)])
        if stop_after <= 6:
            return
        sch.barrier(("x1:", "x2:", "t:", "u1:", "u2:"))
        wload(Wp, dr["w_in"][:, :, 0:512], "x2:Wp", "wsm", 8, s, "wp")
        sch.add("dve", lambda e: e.memset(UF[:, 0:16], 0.0), w=("x2:UF",))
        sch.add("dve", lambda e: e.memset(SA[:, 0:16], 0.0), w=("x2:SA",))
        sch.add("dve", lambda e: e.memset(SBp[:, 0:16], 0.0), w=("x2:SB",))
        for g in range(4):
            for c in range(4):
                cc = slice(c * 512, (c + 1) * 512)
                bi = nextbank([0, 1, 2, 3])
                sch.add("pe", lambda e, bi=bi, g=g, cc=cc: [mm(e, PB[bi][:, :], Wp[:, kc, g * 128:(g + 1) * 128], hT[:, kc, cc], kc == 0, kc == 7) for kc in range(8)],
                        r=("m:hT%d" % c, "x2:Wp"), w=("B%d" % bi,))
                sch.add("act", lambda e, bi=bi, c=c: e.activation(out=UF[:, 16 + c * 512:16 + (c + 1) * 512], in_=PB[bi][:, :], func=AF.Copy),
                        r=("B%d" % bi,), w=("x2:UF",))
            bufs = [(UF, "x2:UF"), (SA, "x2:SA"), (SBp, "x2:SB")]
            srcb = bufs[0]
            for lvl in range(g + 1):
                sh = 1 << lvl
                dstb = bufs[1 + (lvl % 2)]
                eng = "dve" if lvl % 2 == 0 else "pool"
                sch.add(eng, lambda e, srcb=srcb, dstb=dstb, sh=sh: e.tensor_tensor(out=dstb[0][:, 16:16 + S], in0=srcb[0][:, 16:16 + S],
                                                                                  in1=srcb[0][:, 16 - sh:16 + S - sh], op=ALU.add),
                        r=(srcb[1],), w=(dstb[1],))
                srcb = dstb
            wdw = 2 << g
            sch.add("dve", lambda e, srcb=srcb, g=g: e.tensor_tensor(out=srcb[0][:, 16:32], in0=srcb[0][:, 16:32], in1=poolcorr[:, g, :], op=ALU.mult),
                    r=(srcb[1], "c:poolcorr"), w=(srcb[1],))
            sch.add("dve", lambda e, srcb=srcb, wdw=wdw: e.scalar_tensor_tensor(out=PLb, in0=srcb[0][:, 16:16 + S], scalar=1.0 / wdw, in1=UF[:, 16:16 + S],
                                                                             op0=ALU.mult, op1=ALU.subtract), r=(srcb[1], "x2:UF"), w=("x2:PLb",))
            for c in range(4):
                cc = slice(c * 512, (c + 1) * 512)
                bi = nextbank([4, 5, 6])
                sch.add("pe", lambda e, bi=bi, g=g, cc=cc: mm(e, PB[bi][:, :], pool_w[:, g, :], PLb[:, cc], True, True), r=("x2:PLb", "c:pool_w"), w=("B%d" % bi,))
                sch.add("act", lambda e, bi=bi, g=g, cc=cc: e.activation(out=YP[:, g, cc], in_=PB[bi][:, :], func=AF.Copy, scale=pool_scale[:, g:g + 1]),
                        r=("B%d" % bi, "c:pool_scale"), w=("m:YP",))
        if s == 0:
            dump("YP", YP, [128, 4, S], BF16, reads=("m:YP",))
        if stop_after <= 7:
            return
        sch.barrier(("x2:", "x3:", "t:"))
        wload(Wmg, dr["w_in"][:, :, 2352:4400], "x3:Wmg", "wmg", 8, s)
        wload(Wpb, dr["w_pool_br"], "x3:Wpb", "wpb", 4, s)
        wload(Wab, dr["w_attn_br"], "x3:Wab", "wab", 8, s)
        wload(Wo, dr["w_o"], "x3:Wo", "wo", 8, s)
        for c in range(4):
            cc = slice(c * 512, (c + 1) * 512)
            otn = ["m:OT%d_%d" % (c, h_) for h_ in range(16)]
            for dc in range(8):
                dsl = slice(dc * 128, (dc + 1) * 128)
                b_yp, b_ya, b_m0, b_m1 = [nextbank([0, 1, 2, 3, 4, 5, 6]) for _ in range(4)]
                sch.add("pe", lambda e, b=b_m0, dsl=dsl, cc=cc: [mm(e, PB[b][:, :], Wmg[:, kc, dsl], hT[:, kc, cc], kc == 0, kc == 7) for kc in range(8)],
                        r=("m:hT%d" % c, "x3:Wmg"), w=("B%d" % b_m0,))
                sch.add("pe", lambda e, b=b_m1, dc=dc, cc=cc: [mm(e, PB[b][:, :], Wmg[:, kc, 1024 + dc * 128:1024 + (dc + 1) * 128], hT[:, kc, cc], kc == 0, kc == 7) for kc in range(8)],
                        r=("m:hT%d" % c, "x3:Wmg"), w=("B%d" % b_m1,))
                sch.add("pe", lambda e, b=b_yp, dsl=dsl, cc=cc: [mm(e, PB[b][:, :], Wpb[:, g, dsl], YP[:, g, cc], g == 0, g == 3) for g in range(4)],
                        r=("m:YP", "x3:Wpb"), w=("B%d" % b_yp,))
                sch.add("pe", lambda e, b=b_ya, dsl=dsl, cc=cc: [mm(e, PB[b][:, :], Wab[:, i, dsl], OT[:, i, cc], i == 0, i == 7) for i in range(8)],
                        r=tuple(otn) + ("x3:Wab",), w=("B%d" % b_ya,))
                sch.add("act", lambda e, b=b_m0: e.activation(out=SG[0], in_=PB[b][:, :], func=AF.Sigmoid), r=("B%d" % b_m0,), w=("x3:SG0",))
                sch.add("act", lambda e, b=b_m1: e.activation(out=SG[1], in_=PB[b][:, :], func=AF.Sigmoid), r=("B%d" % b_m1,), w=("x3:SG1",))
                sch.add("dve", lambda e, b=b_yp: e.tensor_tensor(out=T0, in0=SG[0], in1=PB[b][:, :], op=ALU.mult), r=("x3:SG0", "B%d" % b_yp), w=("x3:T0",))
                sch.add("dve", lambda e, b=b_ya: e.tensor_tensor(out=T1, in0=SG[1], in1=PB[b][:, :], op=ALU.mult), r=("x3:SG1", "B%d" % b_ya), w=("x3:T1",))
                sch.add("pool", lambda e, dc=dc: e.tensor_tensor(out=MT[:, dc, :], in0=T0, in1=T1, op=ALU.add), r=("x3:T0", "x3:T1"), w=("x3:MT",))
            def ld6(tt):
                sch.add("sp", lambda e: e.dma_start(out=xs6[tt % 2], in_=x_d[s, tt * 128:(tt + 1) * 128, :]), w=("x3:xs%d" % (tt % 2),), dma=("xs6%d" % (tt % 2), 1))
            for tl in range(4):
                tt = 4 * c + tl
                ls = slice(tl * 128, (tl + 1) * 128)
                xt = xs6[tt % 2]; xn = "x3:xs%d" % (tt % 2)
                if tl == 0:
                    ld6(tt)
                if tl + 1 < 4:
                    ld6(tt + 1)
                b0, b1_ = nextbank([0, 1, 2, 3, 4, 5, 6]), nextbank([0, 1, 2, 3, 4, 5, 6])
                for hf, b in ((0, b0), (1, b1_)):
                    sch.add("pe", lambda e, b=b, hf=hf, ls=ls: [mm(e, PB[b][:, :], MT[:, dc, ls], Wo[:, dc, hf * 512:(hf + 1) * 512], dc == 0, dc == 7) for dc in range(8)],
                            r=("x3:MT", "x3:Wo"), w=("B%d" % b,))
                    sch.add("dve", lambda e, b=b, hf=hf, xt=xt: e.tensor_tensor(out=xt[:, hf * 512:(hf + 1) * 512], in0=xt[:, hf * 512:(hf + 1) * 512], in1=PB[b][:, :], op=ALU.add),
                            r=(xn, "B%d" % b), w=(xn,))
                sch.add("sp", lambda e, xt=xt, tt=tt: e.dma_start(out=(x1_d if stop_after > 8 else out_d)[s, tt * 128:(tt + 1) * 128, :], in_=xt), r=(xn,), w=("o:%d_%d" % (s, tt),), dma=("xo6%d" % (tt % 2), 1))
        if stop_after <= 8:
            return
        sch.barrier(("m:", "x1:", "x2:", "x3:", "f:", "t:", "u1:", "u2:"))
        sch.add("dve", lambda e: e.memset(HALO, 0.0), w=tuple("f:HALO%d" % i for i in range(22)))
        DB = [0, 1, 2, 3]
        UB = [4, 5, 6]
        def pf(c8n, stage):
            for tl in range(2):
                tt = 2 * c8n + tl
                xi = (c8n % 2) * 2 + tl
                xt = X1S[xi]; xn = "f:x1s%d" % xi
                if stage == 0:
                    sch.add("sp", lambda e, xt=xt, tt=tt: e.dma_start(out=xt, in_=x1_d[s, tt * 128:(tt + 1) * 128, :]), r=("o:%d_%d" % (s, tt),), w=(xn,), dma=("x1s%d" % xi, 1))
                elif stage == 1:
                    norm_A(xt, xn, hn7[tl], "f:hn%d" % tl, 1 + tl)
                elif stage == 2:
                    norm_B1(xt, xn, hn7[tl], "f:hn%d" % tl, 1 + tl)
                else:
                    norm_B2(hn7[tl], "f:hn%d" % tl, 1, H2[c8n % 2], "f:H2_%d" % (c8n % 2), tl * 128)
        pf(0, 0)
        wload(Wup, None, "f:Wup", "wup", 32, s, "wup", cast_fn=lambda: sch.add("pool", lambda e: [e.dma_start(out=Wup[:, k, 1376 * i:1376 * (i + 1)], in_=dr["w_up"][:, k, 1376 * i:1376 * (i + 1)]) for k in range(8) for i in range(4)], w=("f:Wup",), dma=("wup", 32)))
        wload(Wdn, dr["w_down"], "f:Wdn", "wdn", 22, s)
        for st_ in range(1, 4):
            pf(0, st_)
        for c8 in range(8):
            H2c = H2[c8 % 2]; h2n = "f:H2_%d" % (c8 % 2)

            def rec_up(fc, H2c=H2c, h2n=h2n):
                rows = 128 if fc < 21 else 64
                par = fc % 3
                b = UB[par]; bn = "B%d" % b
                ue = UE[par]; uen = "f:UE%d" % par

                def upmm(e):
                    ins = []
                    for gv in range(2):
                        col0 = gv * D_FF + fc * 128
                        for kc in range(8):
                            ins.append(mm(e, PB[b][0:rows, gv * 256:(gv + 1) * 256], Wup[:, kc, col0:col0 + rows], H2c[:, kc, :], kc == 0, kc == 7))
                    return ins
                sch.add("pe", upmm, r=(h2n, "f:Wup"), w=(bn,))
                hn_ = "f:HALO%d" % fc
                sch.add("pool", lambda e: e.tensor_copy(out=ue[0:rows, :, 0:2], in_=HALO[0:rows, :, fc, :]), r=(hn_,), w=(uen,))
                sch.add("act", lambda e: e.activation(out=ue[0:rows, :, 2:258], in_=PB[b][0:rows, :].rearrange("p (g t) -> p g t", g=2, t=256), func=AF.Copy),
                        r=(bn,), w=(uen,))
                sch.add("pool", lambda e: e.tensor_copy(out=HALO[0:rows, :, fc, :], in_=ue[0:rows, :, 256:258]), r=(uen,), w=(hn_,))

            def rec_conv(fc):
                rows = 128 if fc < 21 else 64
                par = fc % 3
                ue = UE[par]; uen = "f:UE%d" % par
                cxs = ((0, CG[par], "f:CG%d" % par), (1, CV[par], "f:CV%d" % par))
                cw = lambda tap, gv: conv[0:rows, gv, fc, tap:tap + 1]
                for gv, cx, cxn in cxs:
                    sch.add("dve", lambda e, cx=cx, gv=gv: e.tensor_scalar(out=cx[0:rows, :], in0=ue[0:rows, gv, 2:258], scalar1=cw(2, gv), scalar2=cw(3, gv), op0=ALU.mult, op1=ALU.add),
                            r=(uen, "c:conv"), w=(cxn,))
                for gv, cx, cxn in cxs:
                    sch.add("dve", lambda e, cx=cx, gv=gv: e.scalar_tensor_tensor(out=cx[0:rows, :], in0=ue[0:rows, gv, 1:257], scalar=cw(1, gv), in1=cx[0:rows, :], op0=ALU.mult, op1=ALU.add),
                            r=(uen, cxn, "c:conv"), w=(cxn,))
                for gv, cx, cxn in cxs:
                    sch.add("dve", lambda e, cx=cx, gv=gv: e.scalar_tensor_tensor(out=cx[0:rows, :], in0=ue[0:rows, gv, 0:256], scalar=cw(0, gv), in1=cx[0:rows, :], op0=ALU.mult, op1=ALU.add),
                            r=(uen, cxn, "c:conv"), w=(cxn,))

            def rec_act(fc):
                rows = 128 if fc < 21 else 64
                par = fc % 3
                sg = SGF[par]; af = AF_[par]
                sch.add("act", lambda e: e.activation(out=sg[0:rows, :], in_=CG[par][0:rows, :], func=AF.Silu), r=("f:CG%d" % par,), w=("f:SG%d" % par,))
                sch.add("pool", lambda e: e.tensor_tensor(out=af[0:rows, :], in0=sg[0:rows, :], in1=CV[par][0:rows, :], op=ALU.mult),
                        r=("f:SG%d" % par, "f:CV%d" % par), w=("f:A%d" % par,))

            def rec_down(fc):
                rows = 128 if fc < 21 else 64
                par = fc % 3
                af = AF_[par]

                def dmm(e):
                    ins = []
                    for tl in range(2):
                        for hf in range(2):
                            ins.append(mm(e, PB[DB[tl * 2 + hf]][:, :], af[0:rows, tl * 128:(tl + 1) * 128], Wdn[0:rows, fc, hf * 512:(hf + 1) * 512], fc == 0, fc == 21))
                    return ins
                sch.add("pe", dmm, r=("f:A%d" % par, "f:Wdn"), w=("B0", "B1", "B2", "B3"))
            for k in range(22 + 4):
                if k < 22:
                    rec_up(k)
                    rec_conv(k)
                if 0 <= k - 2 < 22:
                    rec_act(k - 2)
                if 0 <= k - 4 < 22:
                    rec_down(k - 4)
                if c8 + 1 < 8 and k in (2, 6, 9, 12):
                    pf(c8 + 1, (2, 6, 9, 12).index(k))
            for tl in range(2):
                tt = 2 * c8 + tl
                xi = (c8 % 2) * 2 + tl
                xt = X1S[xi]; xn = "f:x1s%d" % xi
                for hf in range(2):
                    b = DB[tl * 2 + hf]
                    sch.add("dve", lambda e, xt=xt, b=b, hf=hf: e.tensor_tensor(out=xt[:, hf * 512:(hf + 1) * 512], in0=xt[:, hf * 512:(hf + 1) * 512], in1=PB[b][:, :], op=ALU.add),
                            r=(xn, "B%d" % b), w=(xn,))
                sch.add("sp", lambda e, xt=xt, tt=tt: e.dma_start(out=out_d[s, tt * 128:(tt + 1) * 128, :], in_=xt), r=(xn,), w=("of:%d_%d" % (s, tt),), dma=("x1o%d" % xi, 1))
        sch.barrier(("m:", "x1:", "x2:", "x3:", "f:", "t:", "u1:", "u2:"))

    for s_i in range(nseq):
        do_seq(s_i)

    sch.finalize()
    print("ops:", len(sch.ops))
    sems = {}
    for k in sch.sem_keys():
        sems[k] = es.enter_context(nc.semaphore("s_%s_%s" % k))
    with nc.Block() as block:
        @block.sync
        def _(e):
            sch.emit("sp", e, sems)

        @block.tensor
        def _(e):
            sch.emit("pe", e, sems)

        @block.scalar
        def _(e):
            sch.emit("act", e, sems)

        @block.vector
        def _(e):
            sch.emit("dve", e, sems)

        @block.gpsimd
        def _(e):
            sch.emit("pool", e, sems)
    es.close()
    return nc, dbg_d


_CACHE = {}


def kernel(**inputs):
    x = np.asarray(inputs["x"], dtype=np.float32)
    B = x.shape[0]
    nseq = B // NCORES
    consts = host_consts()
    wts = host_weights(inputs)
    if "nc" not in _CACHE:
        _CACHE["nc"] = build(nseq=nseq)[0]
    nc = _CACHE["nc"]
    in_maps = []
    for c in range(NCORES):
        m = {"x": np.ascontiguousarray(x[c * nseq:(c + 1) * nseq])}
        m.update(consts)
        m.update(wts)
        in_maps.append(m)
    res = run_bass_kernel_spmd(nc, in_maps, core_ids=list(range(NCORES)))
    out = np.concatenate([np.asarray(r["out"], dtype=np.float32) for r in res.results], axis=0)
    return out
```
